# Optimizing a Trainium2 kernel written in Bass

```python
import math
import jax, jax.numpy as jnp
from jax import lax
import numpy as np

D_MODEL = 1024
BATCH = 8
SEQ = 4096
DEPTH = 2

N_A_LAYERS = DEPTH // 2
N_B_LAYERS = DEPTH - N_A_LAYERS

A_HEADS = 8
A_DK = 128
A_DV = 128
A_CONV = 4
A_CHUNK = 64
A_QK = A_HEADS * A_DK
A_V = A_HEADS * A_DV
A_IN = 2 * A_QK + 2 * A_V + 2 * A_HEADS

B_HEADS = 16
B_KV_HEADS = 4
B_GROUP = B_HEADS // B_KV_HEADS
B_DH = 64
B_WINDOW = 128
B_BLOCK = 128

FFN_DIM = 2816
FFN_CONV = 3

EPS = 1e-6

kernel_name = 'yoco_deltanet_swa_sink_convffn'


def rmsnorm(x, g):
    xf = x.astype(jnp.float32)
    y = xf * lax.rsqrt(jnp.mean(xf * xf, axis=-1, keepdims=True) + EPS)
    return (y * g.astype(jnp.float32)).astype(x.dtype)


def l2norm(x):
    return x * lax.rsqrt(jnp.sum(x * x, axis=-1, keepdims=True) + EPS)


def causal_dwconv(x, w):
    K = w.shape[0]
    S = x.shape[1]
    xp = jnp.pad(x, ((0, 0), (K - 1, 0), (0, 0)))
    y = xp[:, K - 1:K - 1 + S] * w[K - 1]
    for j in range(K - 1):
        y = y + xp[:, j:j + S] * w[j]
    return y


def gated_delta_rule_chunked(q, k, v, beta, g):
    Bsz, S, H, DK = q.shape
    DV = v.shape[-1]
    C = A_CHUNK
    N = S // C

    def chunk(t):
        t = t.reshape((Bsz, N, C, H) + t.shape[3:])
        return jnp.moveaxis(t, 3, 1)

    q, k, v, beta, g = chunk(q), chunk(k), chunk(v), chunk(beta), chunk(g)
    gc = jnp.cumsum(g, axis=-1)
    idx = jnp.arange(C)
    strict = idx[:, None] > idx[None, :]
    incl = idx[:, None] >= idx[None, :]
    diff = gc[..., :, None] - gc[..., None, :]
    decay = jnp.exp(jnp.where(incl, diff, -jnp.inf))
    kb = k * beta[..., None]
    Lmat = jnp.where(strict, jnp.einsum('bhnid,bhnjd->bhnij', kb, k) * decay, 0.0)
    eye = jnp.eye(C, dtype=Lmat.dtype)
    rhs = jnp.concatenate([v * beta[..., None], kb * jnp.exp(gc)[..., None]], axis=-1)
    sol = lax.linalg.triangular_solve(eye + Lmat, rhs, left_side=True, lower=True, unit_diagonal=True)
    value, k_cumdecay = sol[..., :DV], sol[..., DV:]
    attn_intra = jnp.einsum('bhnid,bhnjd->bhnij', q, k) * decay
    q_dec = q * jnp.exp(gc)[..., None]
    k_dec = k * jnp.exp(gc[..., -1:] - gc)[..., None]
    g_last = jnp.exp(gc[..., -1])

    def step(state, xs):
        a_c, qd, kd, val, kcd, gl = xs
        v_new = val - jnp.einsum('bhcd,bhde->bhce', kcd, state)
        o = jnp.einsum('bhcd,bhde->bhce', qd, state) + jnp.einsum('bhij,bhje->bhie', a_c, v_new)
        state = state * gl[..., None, None] + jnp.einsum('bhcd,bhce->bhde', kd, v_new)
        return state, o

    xs = tuple(jnp.moveaxis(t, 2, 0) for t in (attn_intra, q_dec, k_dec, value, k_cumdecay, g_last))
    state0 = jnp.zeros((Bsz, H, DK, DV), jnp.float32)
    _, o = lax.scan(step, state0, xs)
    o = jnp.moveaxis(jnp.moveaxis(o, 0, 2), 1, 3)
    return o.reshape(Bsz, S, H, DV)


def deltanet_mixer(h, w_in, conv_w, A_log, dt_bias, onorm_g, w_out):
    Bsz, S, _ = h.shape
    f32 = jnp.float32
    p = h @ w_in
    qkv, gate, b_raw, a_raw = jnp.split(p, [2 * A_QK + A_V, 2 * A_QK + 2 * A_V, 2 * A_QK + 2 * A_V + A_HEADS], axis=-1)
    qkv = jax.nn.silu(causal_dwconv(qkv, conv_w))
    q, k, v = jnp.split(qkv, [A_QK, 2 * A_QK], axis=-1)
    q = l2norm(q.reshape(Bsz, S, A_HEADS, A_DK).astype(f32)) * (A_DK ** -0.5)
    k = l2norm(k.reshape(Bsz, S, A_HEADS, A_DK).astype(f32))
    v = v.reshape(Bsz, S, A_HEADS, A_DV).astype(f32)
    beta = jax.nn.sigmoid(b_raw.astype(f32))
    g = -jnp.exp(A_log.astype(f32)) * jax.nn.softplus(a_raw.astype(f32) + dt_bias.astype(f32))
    o = gated_delta_rule_chunked(q, k, v, beta, g)
    o = rmsnorm(o, onorm_g) * jax.nn.silu(gate.reshape(Bsz, S, A_HEADS, A_DV).astype(f32))
    return o.reshape(Bsz, S, A_V).astype(h.dtype) @ w_out


def swa_sink_attention(h, k_sh, v_sh, w_q, b_q, sinks, w_o, b_o):
    Bsz, S, _ = h.shape
    f32 = jnp.float32
    NB = S // B_BLOCK
    q = (h @ w_q + b_q).astype(f32) * (B_DH ** -0.5)
    q = q.reshape(Bsz, NB, B_BLOCK, B_KV_HEADS, B_GROUP, B_DH)

    def with_prev(t):
        t = t.astype(f32).reshape(Bsz, NB, B_BLOCK, B_KV_HEADS, B_DH)
        prev = jnp.pad(t, ((0, 0), (1, 0), (0, 0), (0, 0), (0, 0)))[:, :-1]
        return jnp.concatenate([prev, t], axis=2)

    kb, vb = with_prev(k_sh), with_prev(v_sh)
    slopes = (2.0 ** (-8.0 * jnp.arange(1, B_HEADS + 1, dtype=f32) / B_HEADS)).reshape(B_KV_HEADS, B_GROUP)
    qi = jnp.arange(B_BLOCK)[:, None]
    kj = jnp.arange(2 * B_BLOCK)[None, :]
    dist = qi + B_BLOCK - kj
    band = (dist >= 0) & (dist < B_WINDOW)
    bias = -slopes[:, :, None, None] * dist.astype(f32)
    sink = sinks.astype(f32).reshape(B_KV_HEADS, B_GROUP)[None, :, :, None]

    def block_attn(args):
        qb, kk, vv, blk = args
        s = jnp.einsum('bqkgd,bskd->bkgqs', qb, kk) + bias
        valid = band & ((blk > 0) | (kj >= B_BLOCK))
        s = jnp.where(valid, s, -jnp.inf)
        m = jnp.maximum(jnp.max(s, axis=-1), sink)
        p = jnp.exp(s - m[..., None])
        denom = jnp.sum(p, axis=-1) + jnp.exp(sink - m)
        o = jnp.einsum('bkgqs,bskd->bqkgd', p, vv)
        return o / jnp.transpose(denom, (0, 3, 1, 2))[..., None]

    xs = (jnp.moveaxis(q, 1, 0), jnp.moveaxis(kb, 1, 0), jnp.moveaxis(vb, 1, 0), jnp.arange(NB, dtype=jnp.int32))
    o = lax.map(block_attn, xs)
    o = jnp.moveaxis(o, 0, 1).reshape(Bsz, S, B_HEADS * B_DH)
    return o.astype(h.dtype) @ w_o + b_o


def conv_ffn(h, w_up, conv_w, conv_b, w_down):
    u = causal_dwconv(h @ w_up, conv_w) + conv_b
    gate, up = jnp.split(u, 2, axis=-1)
    return (jax.nn.silu(gate) * up) @ w_down


def setup_inputs(seed: int = 0) -> dict:
    key = jax.random.key(seed)
    ks = jax.random.split(key, 24)
    f32 = jnp.float32

    def nrm(k, shape, fan_in):
        return jax.random.normal(k, shape, f32) * (fan_in ** -0.5)

    def gain(k, shape):
        return 1.0 + 0.05 * jax.random.normal(k, shape, f32)

    def small(k, shape):
        return 0.02 * jax.random.normal(k, shape, f32)

    x = jax.random.normal(ks[0], (BATCH, SEQ, D_MODEL), f32)
    a_norm = gain(ks[1], (N_A_LAYERS, D_MODEL))
    a_w_in = nrm(ks[2], (N_A_LAYERS, D_MODEL, A_IN), D_MODEL)
    a_conv_w = nrm(ks[3], (N_A_LAYERS, A_CONV, 2 * A_QK + A_V), A_CONV)
    a_A_log = jnp.log(jax.random.uniform(ks[4], (N_A_LAYERS, A_HEADS), f32, 1.0, 16.0))
    dt = jnp.exp(jax.random.uniform(ks[5], (N_A_LAYERS, A_HEADS), f32, math.log(1e-3), math.log(1e-1)))
    a_dt_bias = dt + jnp.log(-jnp.expm1(-dt))
    a_onorm = gain(ks[6], (N_A_LAYERS, A_DV))
    a_w_out = nrm(ks[7], (N_A_LAYERS, A_V, D_MODEL), A_V)
    kv_norm = gain(ks[8], (D_MODEL,))
    kv_w = nrm(ks[9], (D_MODEL, 2 * B_KV_HEADS * B_DH), D_MODEL)
    kv_b = small(ks[10], (2 * B_KV_HEADS * B_DH,))
    b_norm = gain(ks[11], (N_B_LAYERS, D_MODEL))
    b_w_q = nrm(ks[12], (N_B_LAYERS, D_MODEL, B_HEADS * B_DH), D_MODEL)
    b_b_q = small(ks[13], (N_B_LAYERS, B_HEADS * B_DH))
    b_sinks = jax.random.normal(ks[14], (N_B_LAYERS, B_HEADS), f32)
    b_w_o = nrm(ks[15], (N_B_LAYERS, B_HEADS * B_DH, D_MODEL), B_HEADS * B_DH)
    b_b_o = small(ks[16], (N_B_LAYERS, D_MODEL))
    f_norm = gain(ks[17], (DEPTH, D_MODEL))
    f_w_up = nrm(ks[18], (DEPTH, D_MODEL, 2 * FFN_DIM), D_MODEL)
    f_conv_w = nrm(ks[19], (DEPTH, FFN_CONV, 2 * FFN_DIM), FFN_CONV)
    f_conv_b = small(ks[20], (DEPTH, 2 * FFN_DIM))
    f_w_down = nrm(ks[21], (DEPTH, FFN_DIM, D_MODEL), FFN_DIM)
    final_norm = gain(ks[22], (D_MODEL,))
    return {'x': x, 'a_norm': a_norm, 'a_w_in': a_w_in, 'a_conv_w': a_conv_w, 'a_A_log': a_A_log,
            'a_dt_bias': a_dt_bias, 'a_onorm': a_onorm, 'a_w_out': a_w_out,
            'kv_norm': kv_norm, 'kv_w': kv_w, 'kv_b': kv_b,
            'b_norm': b_norm, 'b_w_q': b_w_q, 'b_b_q': b_b_q, 'b_sinks': b_sinks, 'b_w_o': b_w_o, 'b_b_o': b_b_o,
            'f_norm': f_norm, 'f_w_up': f_w_up, 'f_conv_w': f_conv_w, 'f_conv_b': f_conv_b, 'f_w_down': f_w_down,
            'final_norm': final_norm}


def reference(x, a_norm, a_w_in, a_conv_w, a_A_log, a_dt_bias, a_onorm, a_w_out,
              kv_norm, kv_w, kv_b,
              b_norm, b_w_q, b_b_q, b_sinks, b_w_o, b_b_o,
              f_norm, f_w_up, f_conv_w, f_conv_b, f_w_down, final_norm):
    Bsz, S, _ = x.shape
    h = x
    k_sh = None
    v_sh = None
    for layer in range(DEPTH):
        if layer < N_A_LAYERS:
            i = layer
            h = h + deltanet_mixer(rmsnorm(h, a_norm[i]), a_w_in[i], a_conv_w[i], a_A_log[i],
                                   a_dt_bias[i], a_onorm[i], a_w_out[i])
        else:
            i = layer - N_A_LAYERS
            if i == 0:
                kv = rmsnorm(h, kv_norm) @ kv_w + kv_b
                k_sh, v_sh = jnp.split(kv, 2, axis=-1)
                k_sh = k_sh.reshape(Bsz, S, B_KV_HEADS, B_DH)
                v_sh = v_sh.reshape(Bsz, S, B_KV_HEADS, B_DH)
            h = h + swa_sink_attention(rmsnorm(h, b_norm[i]), k_sh, v_sh, b_w_q[i], b_b_q[i],
                                       b_sinks[i], b_w_o[i], b_b_o[i])
        h = h + conv_ffn(rmsnorm(h, f_norm[layer]), f_w_up[layer], f_conv_w[layer],
                         f_conv_b[layer], f_w_down[layer])
    return rmsnorm(h, final_norm)
```

```python
import numpy as np
from contextlib import ExitStack
import concourse.bass as bass
import concourse.mybir as mybir
from concourse.bass_utils import run_bass_kernel_spmd

F32 = mybir.dt.float32
BF16 = mybir.dt.bfloat16
AF = mybir.ActivationFunctionType
ALU = mybir.AluOpType
AX = mybir.AxisListType

D = 1024
S = 4096
NB = 8
FF = 2816
EPS = 1e-6
ENGS = ["sync", "scalar", "vector", "gpsimd", "tensor"]


class Op:
    __slots__ = ("eng", "fn", "deps", "needed", "idx", "chan", "chan_val")

    def __init__(self, eng, fn):
        self.eng = eng
        self.fn = fn
        self.deps = []
        self.needed = False
        self.idx = None
        self.chan = None
        self.chan_val = None


class Sched:
    def __init__(self, nc, block, es):
        self.nc = nc
        self.block = block
        self.es = es
        self.sem = {e: es.enter_context(nc.semaphore("s_" + e)) for e in ENGS}
        self.count = {e: 0 for e in ENGS}
        self.chans = {}
        self.waited = {e: {} for e in ENGS}
        self._reset()

    def _reset(self):
        self.ops = {e: [] for e in ENGS}
        self.last_w = {}
        self.readers = {}

    def chan(self, name):
        if name not in self.chans:
            self.chans[name] = [self.es.enter_context(self.nc.semaphore("c_" + name)), 0, None]
        return self.chans[name]

    def add(self, eng, fn, reads=(), writes=(), chan=None):
        op = Op(eng, fn)
        deps = []
        for r in reads:
            w = self.last_w.get(r)
            if w is not None:
                deps.append(w)
        for w_ in writes:
            w = self.last_w.get(w_)
            if w is not None:
                deps.append(w)
            deps.extend(self.readers.get(w_, ()))
        if chan is not None:
            ch = self.chan(chan)
            if ch[2] is not None:
                deps.append(ch[2])
            ch[1] += 16
            ch[2] = op
            op.chan = ch
            op.chan_val = ch[1]
        seen = set()
        for d in deps:
            if id(d) in seen or d is op:
                continue
            seen.add(id(d))
            if d.eng == "tensor" and eng == "tensor" and d.chan is None:
                continue
            d.needed = True
            op.deps.append(d)
        for r in reads:
            self.readers.setdefault(r, []).append(op)
        for w_ in writes:
            self.last_w[w_] = op
            self.readers[w_] = []
        self.ops[eng].append(op)
        return op

    def dma(self, eng, out, in_, reads=(), writes=(), chan=None, **kw):
        assert chan is not None
        return self.add(eng, lambda e: e.dma_start(out=out, in_=in_, **kw), reads, writes, chan=chan)

    def flush(self):
        for e in ENGS:
            comp = [op for op in self.ops[e] if op.chan is None]
            if comp:
                comp[-1].needed = True
            for op in self.ops[e]:
                if op.chan is None and op.needed:
                    self.count[e] += 1
                    op.idx = self.count[e]
        finals = [(self.sem[e], self.count[e]) for e in ENGS if self.count[e] > 0]
        finals += [(ch[0], ch[1]) for ch in self.chans.values() if ch[1] > 0]

        def emit(ename):
            ops = self.ops[ename]
            waited = self.waited[ename]

            def body(eng):
                for op in ops:
                    for d in op.deps:
                        if d.chan is not None:
                            sem, val = d.chan[0], d.chan_val
                        else:
                            sem, val = self.sem[d.eng], d.idx
                        if waited.get(sem.num if hasattr(sem, "num") else id(sem), 0) < val:
                            eng.wait_ge(sem, val)
                            waited[sem.num if hasattr(sem, "num") else id(sem)] = val
                    ins = op.fn(eng)
                    if op.chan is not None:
                        ins.then_inc(op.chan[0], 16)
                    elif op.needed:
                        ins.then_inc(self.sem[ename], 1)
                for sem, val in finals:
                    key = sem.num if hasattr(sem, "num") else id(sem)
                    if waited.get(key, 0) < val:
                        eng.wait_ge(sem, val)
                        waited[key] = val

            getattr(self.block, ename)(body)

        for e in ENGS:
            emit(e)
        self._reset()


_UNIQ = [0]
_DBG = [0, 0]


def _uname(name):
    _UNIQ[0] += 1
    return "t%d_%s" % (_UNIQ[0], name)


def _bc(ap, n):
    return ap.partition_broadcast(n)


class Ctx:
    pass


def load_consts(sc, es, nc, cx, ident_ap):
    cx.ident = es.enter_context(nc.sbuf_tensor("sb_ident", [128, 128], F32))
    cx.identb = es.enter_context(nc.sbuf_tensor("sb_identb", [128, 128], BF16))
    cx.epsb = es.enter_context(nc.sbuf_tensor("sb_epsb", [128, 1], F32))
    sc.dma("sync", cx.ident[:], ident_ap, writes=["ident"], chan="const")
    sc.add("vector", lambda e: e.tensor_copy(cx.identb[:], cx.ident[:]), reads=["ident"], writes=["identb"])
    sc.add("vector", lambda e: e.memset(cx.epsb[:], EPS), writes=["epsb"])


def norm_transpose(sc, cx, key, x_ap, xkey, gbc, gkey, ssq, rstd, junk, xn, xT_dst, xTkey, ps_bf, pskey,
                   evac_eng="scalar"):
    sc.add("scalar", lambda e: e.activation(out=junk, in_=x_ap, func=AF.Square, accum_out=ssq),
           reads=[xkey], writes=["junk", key + "ssq"])
    sc.add("scalar", lambda e: e.activation(out=rstd, in_=ssq, func=AF.Sqrt, bias=cx.epsb[:], scale=1.0 / D),
           reads=[key + "ssq", "epsb"], writes=[key + "rstd"])
    sc.add("vector", lambda e: e.reciprocal(rstd, rstd), reads=[key + "rstd"], writes=[key + "rstd"])
    sc.add("vector", lambda e: e.scalar_tensor_tensor(out=xn, in0=x_ap, scalar=rstd, in1=gbc,
                                                      op0=ALU.mult, op1=ALU.mult),
           reads=[xkey, key + "rstd", gkey], writes=[key + "xn"])

    def tr(e):
        ins = None
        for c in range(8):
            ins = e.transpose(ps_bf[:, c * 128:(c + 1) * 128], xn[:, c * 128:(c + 1) * 128], cx.identb[:])
        return ins

    sc.add("tensor", tr, reads=[key + "xn", "identb"], writes=[pskey])
    src = ps_bf.rearrange("p (c t) -> p c t", c=8)
    if evac_eng == "scalar":
        sc.add("scalar", lambda e: e.copy(xT_dst, src), reads=[pskey], writes=[xTkey])
    else:
        sc.add("vector", lambda e: e.tensor_copy(xT_dst, src), reads=[pskey], writes=[xTkey])


def load_rows_transposed(sc, es, nc, cx, name, src_ap, nrow, nchunk, ps, pskey, dst, dstkey):
    tmp = es.enter_context(nc.sbuf_tensor(_uname(name + "_tmp"), [nchunk, nrow, 128], F32))
    for j in range(nrow):
        sc.dma("sync", tmp[:, j, :], src_ap[j].rearrange("(c p) -> c p", p=128), writes=[name + "_tmp%d" % j],
               chan="const")

    def tr(e):
        ins = None
        for j in range(nrow):
            ins = e.transpose(ps[:, j * nchunk:(j + 1) * nchunk], tmp[:, j, :], cx.ident[0:nchunk, 0:nchunk])
        return ins

    sc.add("tensor", tr, reads=[name + "_tmp%d" % j for j in range(nrow)] + ["ident"], writes=[pskey])
    sc.add("vector", lambda e: e.tensor_copy(dst.rearrange("p c j -> p j c"),
                                             ps[:, 0:nrow * nchunk].rearrange("p (j c) -> p j c", j=nrow)),
           reads=[pskey], writes=[dstkey])


def ffn_phase(sc, nc, cx, h_in, h_out, g_ap, wup_ap, cw_ap, cb_ap, wdn_ap, fin_ap=None, ntiles=None):
    T = 256
    NT = (S // T) if ntiles is None else ntiles
    NCH = 2 * FF // 128
    NG = FF // 128
    with ExitStack() as es:
        def sb(name, shape, dt):
            return es.enter_context(nc.sbuf_tensor(_uname(name), shape, dt))

        def psum(name, shape, dt):
            return es.enter_context(nc.psum_tensor(_uname(name), shape, dt))

        wup = sb("wup", [128, 8, 2 * FF], BF16)
        wdn = sb("wdn", [128, NG, D], BF16)
        hb = [sb("hb%d" % i, [128, 2, D], F32) for i in range(2)]
        gbc = sb("gbc", [128, D], F32)
        fgbc = sb("fgbc", [128, D], F32) if fin_ap is not None else None
        xn = [sb("xn%d" % i, [128, D], BF16) for i in range(2)]
        junk = sb("junk", [128, D], F32)
        stat = sb("stat", [128, 16], F32)
        xT = [sb("xT%d" % i, [128, 8, T + 2], BF16) for i in range(2)]
        yb = [sb("yb%d" % i, [128, T], F32) for i in range(4)]
        sg = [sb("sg%d" % i, [128, T], F32) for i in range(4)]
        act = [sb("act%d" % i, [128, NG, T], BF16) for i in range(2)]
        cw = sb("cw", [128, NCH, 3], F32)
        cb = sb("cb", [128, NCH, 1], F32)
        ps_tr = [psum("ps_tr0", [128, D], BF16)] * 2
        ps_up = [psum("ps_up%d" % i, [128, 512], F32) for i in range(4)]
        ps_dn = [psum("ps_dn%d" % i, [128, 512], F32) for i in range(3)]

        sc.dma("sync", gbc[:], _bc(g_ap, 128), writes=["gbc"], chan="const")
        if fin_ap is not None:
            sc.dma("sync", fgbc[:], _bc(fin_ap, 128), writes=["fgbc"], chan="const")
        load_rows_transposed(sc, es, nc, cx, "cw", cw_ap, 3, NCH, ps_up[0], "ps_up0", cw[:], "cw")
        load_rows_transposed(sc, es, nc, cx, "cb", cb_ap.rearrange("(o n) -> o n", o=1), 1, NCH, ps_up[1],
                             "ps_up1", cb[:], "cb")
        for kc in range(8):
            for half in range(2):
                lo = half * FF
                sc.dma("gpsimd", wup[:, kc, lo:lo + FF], wup_ap[kc * 128:(kc + 1) * 128, lo:lo + FF],
                       writes=["wup"], chan="w%d" % ((kc * 2 + half) % 4))
        for c in range(NG):
            sc.dma("gpsimd", wdn[:, c, :], wdn_ap[c * 128:(c + 1) * 128, :], writes=["wdn"], chan="w%d" % (c % 4))
        sc.add("vector", lambda e: e.memset(xT[0][:, :, 0:2], 0.0), writes=["xT0_lead"])

        def load(t):
            sl = t % 2
            sc.dma("sync", hb[sl][:], h_in[t * T:(t + 1) * T, :].rearrange("(s p) d -> p s d", p=128),
                   writes=["hb%d_0" % sl, "hb%d_1" % sl], chan="hb%d" % sl)

        def down_group(t, s):
            sl = t % 2
            act_ = act[sl]
            hk = "hb%d_%d" % (sl, s)
            for n in range(2):
                pi = (s * 2 + n) % 3
                psd = ps_dn[pi]

                def mmd(e, s=s, n=n, psd=psd, act_=act_):
                    ins = None
                    for c in range(NG):
                        ins = e.matmul(psd[:], act_[:, c, s * 128:(s + 1) * 128], wdn[:, c, n * 512:(n + 1) * 512],
                                       start=(c == 0), stop=(c == NG - 1))
                    return ins

                sc.add("tensor", mmd, reads=["wdn"] + ["act%d_%d" % (sl, c) for c in range(NG)], writes=["ps_dn%d" % pi])
                sc.add("vector", lambda e, s=s, n=n, psd=psd, sl=sl: e.tensor_tensor(
                    out=hb[sl][:, s, n * 512:(n + 1) * 512], in0=psd[:], in1=hb[sl][:, s, n * 512:(n + 1) * 512],
                    op=ALU.add),
                    reads=["ps_dn%d" % pi, hk], writes=[hk])
            row0 = t * T + s * 128
            if fin_ap is not None:
                k = "f%d" % s
                sc.add("scalar", lambda e, sl=sl, s=s: e.activation(out=junk[:], in_=hb[sl][:, s, :], func=AF.Square,
                                                                    accum_out=stat[:, 8 + s:9 + s]),
                       reads=[hk], writes=["junk", k + "ssq"])
                sc.add("scalar", lambda e, s=s: e.activation(out=stat[:, 12 + s:13 + s], in_=stat[:, 8 + s:9 + s],
                                                             func=AF.Sqrt, bias=cx.epsb[:], scale=1.0 / D),
                       reads=[k + "ssq", "epsb"], writes=[k + "rstd"])
                sc.add("vector", lambda e, s=s: e.reciprocal(stat[:, 12 + s:13 + s], stat[:, 12 + s:13 + s]),
                       reads=[k + "rstd"], writes=[k + "rstd"])
                sc.add("vector", lambda e, sl=sl, s=s: e.scalar_tensor_tensor(
                    out=hb[sl][:, s, :], in0=hb[sl][:, s, :], scalar=stat[:, 12 + s:13 + s], in1=fgbc[:],
                    op0=ALU.mult, op1=ALU.mult),
                    reads=[hk, k + "rstd", "fgbc"], writes=[hk])
            sc.dma("sync", h_out[row0:row0 + 128, :], hb[sl][:, s, :], reads=[hk], chan="ho%d_%d" % (sl, s))

        def do_norm(t):
            sl = t % 2
            for s in range(2):
                k = "n%d" % s
                norm_transpose(sc, cx, k, hb[sl][:, s, :], "hb%d_%d" % (sl, s), gbc[:], "gbc",
                               stat[:, s:s + 1], stat[:, 4 + s:5 + s], junk[:], xn[s][:],
                               xT[sl][:, :, 2 + s * 128:2 + (s + 1) * 128], "xT%d_%d" % (sl, s),
                               ps_tr[s][:], "ps_tr0")
            if t > 0:
                sc.add("vector", lambda e, sl=sl: e.tensor_copy(xT[sl][:, :, 0:2], xT[1 - sl][:, :, T:T + 2]),
                       reads=["xT%d_1" % (1 - sl)], writes=["xT%d_lead" % sl])

        load(0)
        if NT > 1:
            load(1)
        nst = 0
        for t in range(NT):
            sl = t % 2
            if t == 0:
                do_norm(0)
            for j_ in range(NG):
                late = []
                for which in range(2):
                    c = j_ + which * NG
                    pi = nst % 4
                    si = nst % 4
                    nst += 1
                    pst = ps_up[pi]

                    def mm(e, c=c, pst=pst, sl=sl):
                        ins = None
                        for kc in range(8):
                            ins = e.matmul(pst[:, 0:T + 2], wup[:, kc, c * 128:(c + 1) * 128], xT[sl][:, kc, :],
                                           start=(kc == 0), stop=(kc == 7))
                        return ins

                    sc.add("tensor", mm, reads=["wup", "xT%d_0" % sl, "xT%d_1" % sl, "xT%d_lead" % sl],
                           writes=["ps_up%d" % pi])
                    y_ = yb[si]
                    sc.add("scalar", lambda e, pst=pst, y_=y_, c=c: e.activation(
                        out=y_[:], in_=pst[:, 2:T + 2], func=AF.Identity, bias=cb[:, c, 0:1], scale=cw[:, c, 2:3]),
                        reads=["ps_up%d" % pi, "cw", "cb"], writes=["yb%d" % si])
                    for j in (1, 0):
                        sc.add("vector", lambda e, pst=pst, y_=y_, c=c, j=j: e.scalar_tensor_tensor(
                            out=y_[:], in0=pst[:, j:T + j], scalar=cw[:, c, j:j + 1], in1=y_[:],
                            op0=ALU.mult, op1=ALU.add),
                            reads=["ps_up%d" % pi, "cw", "yb%d" % si], writes=["yb%d" % si])
                    if which == 0:
                        gi = j_ % 4
                        sg_ = sg[gi]
                        late.append(lambda sg_=sg_, y_=y_, si=si, gi=gi: sc.add(
                            "scalar", lambda e: e.activation(out=sg_[:], in_=y_[:], func=AF.Silu),
                            reads=["yb%d" % si], writes=["sg%d" % gi]))
                    else:
                        gi = j_ % 4
                        sg_ = sg[gi]
                        late.append(lambda sg_=sg_, y_=y_, j_=j_, sl=sl, si=si, gi=gi: sc.add(
                            "gpsimd", lambda e: e.tensor_tensor(out=act[sl][:, j_, :], in0=sg_[:], in1=y_[:], op=ALU.mult),
                            reads=["sg%d" % gi, "yb%d" % si], writes=["act%d_%d" % (sl, j_)]))
                for fn_ in late:
                    fn_()
                if t >= 1 and j_ in (5, 15):
                    down_group(t - 1, 0 if j_ == 5 else 1)
                    if j_ == 15 and t + 1 < NT:
                        load(t + 1)
                if j_ == 18 and t + 1 < NT:
                    do_norm(t + 1)
        for s in range(2):
            down_group(NT - 1, s)
        sc.flush()


def attn_phase(sc, nc, cx, h_in, h_out, gkv_ap, wkv_ap, bkv_ap, gq_ap, wq_ap, bq_ap, sinks_ap, wo_ap, bo_ap,
               bm_ap, nblk=None, debug_stage=0):
    NBLK = (S // 128) if nblk is None else nblk
    SL = NBLK * 128
    T = 256
    NT = SL // T
    with ExitStack() as es0:
        def sb0(name, shape, dt):
            return es0.enter_context(nc.sbuf_tensor(_uname(name), shape, dt))

        qT = sb0("qT", [128, 8, SL], BF16)
        kT = sb0("kT", [128, 4, SL], BF16)
        vv = sb0("vv", [128, NBLK, 256], BF16)
        with ExitStack() as es:
            def sb(name, shape, dt):
                return es.enter_context(nc.sbuf_tensor(_uname(name), shape, dt))

            def psum(name, shape, dt):
                return es.enter_context(nc.psum_tensor(_uname(name), shape, dt))

            wq = sb("wq", [128, 8, D], BF16)
            wkv = sb("wkv", [128, 8, 512], BF16)
            hb = [sb("hb%d" % i, [128, 2, D], F32) for i in range(2)]
            gq = sb("gq", [128, D], F32)
            gkv = sb("gkv", [128, D], F32)
            junk = sb("junk", [128, D], F32)
            xnq = [sb("xnq%d" % i, [128, D], BF16) for i in range(2)]
            xnk = [sb("xnk%d" % i, [128, D], BF16) for i in range(2)]
            xTq = sb("xTq", [128, 8, T], BF16)
            xTk = sb("xTk", [128, 8, T], BF16)
            stat = sb("stat", [128, 8], F32)
            bq8 = sb("bq8", [128, 8], F32)
            kbd = sb("kbd", [128, 4], F32)
            vbb = sb("vbb", [128, 256], F32)
            ps_tr = [psum("ps_tr%d" % i, [128, D], BF16) for i in range(2)]
            ps_q = [psum("ps_q%d" % i, [128, 512], F32) for i in range(2)]
            ps_k = [psum("ps_k%d" % i, [128, 512], F32) for i in range(2)]
            ps_v = [psum("ps_v%d" % i, [128, 512], F32) for i in range(2)]

            sc.dma("sync", gq[:], _bc(gq_ap, 128), writes=["gq"], chan="const")
            sc.dma("sync", gkv[:], _bc(gkv_ap, 128), writes=["gkv"], chan="const")
            sc.dma("sync", vbb[:], _bc(bkv_ap[256:512], 128), writes=["vbb"], chan="const")
            sc.dma("sync", bq8[:], bq_ap.rearrange("(c p) -> p c", p=128), writes=["bq8"], chan="const",
                   allow_slow_non_contiguous=True)
            for dup in range(2):
                sc.dma("sync", kbd[dup * 64:(dup + 1) * 64, :], bkv_ap[0:256].rearrange("(k p) -> p k", p=64),
                       writes=["kbd%d" % dup], chan="const", allow_slow_non_contiguous=True)
            sc.add("vector", lambda e: e.tensor_scalar(bq8[:], bq8[:], 0.125, None, op0=ALU.mult),
                   reads=["bq8"], writes=["bq8"])
            for kc in range(8):
                sc.dma("gpsimd", wq[:, kc, :], wq_ap[kc * 128:(kc + 1) * 128, :], writes=["wq"], chan="w%d" % (kc % 4))
            for kc in range(8):
                sc.dma("gpsimd", wkv[:, kc, :], wkv_ap[kc * 128:(kc + 1) * 128, :], writes=["wkv"],
                       chan="w%d" % (kc % 4))

            def load(t):
                sl = t % 2
                sc.dma("sync", hb[sl][:], h_in[t * T:(t + 1) * T, :].rearrange("(s p) d -> p s d", p=128),
                       writes=["hb%d" % sl], chan="hb%d" % sl)

            load(0)
            for t in range(NT):
                sl = t % 2
                if t + 1 < NT:
                    load(t + 1)
                for s in range(2):
                    x_ap = hb[sl][:, s, :]
                    ssq = stat[:, s:s + 1]
                    rstd = stat[:, 4 + s:5 + s]
                    k = "n%d" % s
                    sc.add("scalar", lambda e, x_ap=x_ap, ssq=ssq: e.activation(out=junk[:], in_=x_ap, func=AF.Square,
                                                                               accum_out=ssq),
                           reads=["hb%d" % sl], writes=["junk", k + "ssq"])
                    sc.add("scalar", lambda e, ssq=ssq, rstd=rstd: e.activation(out=rstd, in_=ssq, func=AF.Sqrt,
                                                                               bias=cx.epsb[:], scale=1.0 / D),
                           reads=[k + "ssq", "epsb"], writes=[k + "rstd"])
                    sc.add("vector", lambda e, rstd=rstd: e.reciprocal(rstd, rstd), reads=[k + "rstd"],
                           writes=[k + "rstd"])
                    sc.add("vector", lambda e, s=s, x_ap=x_ap, rstd=rstd: e.scalar_tensor_tensor(
                        out=xnq[s][:], in0=x_ap, scalar=rstd, in1=gq[:], op0=ALU.mult, op1=ALU.mult),
                        reads=["hb%d" % sl, k + "rstd", "gq"], writes=["xnq%d" % s])
                    sc.add("vector", lambda e, s=s, x_ap=x_ap, rstd=rstd: e.scalar_tensor_tensor(
                        out=xnk[s][:], in0=x_ap, scalar=rstd, in1=gkv[:], op0=ALU.mult, op1=ALU.mult),
                        reads=["hb%d" % sl, k + "rstd", "gkv"], writes=["xnk%d" % s])
                    for (xn_, xT_, nm, pi) in ((xnq[s], xTq, "q", 0), (xnk[s], xTk, "k", 1)):
                        def tr(e, xn_=xn_, pi=pi):
                            ins = None
                            for c in range(8):
                                ins = e.transpose(ps_tr[pi][:, c * 128:(c + 1) * 128], xn_[:, c * 128:(c + 1) * 128],
                                                  cx.identb[:])
                            return ins

                        sc.add("tensor", tr, reads=["xn%s%d" % (nm, s), "identb"], writes=["ps_tr%d" % pi])
                        sc.add("scalar", lambda e, xT_=xT_, pi=pi, s=s: e.copy(
                            xT_[:, :, s * 128:(s + 1) * 128], ps_tr[pi][:].rearrange("p (c t) -> p c t", c=8)),
                            reads=["ps_tr%d" % pi], writes=["xT%s_%d" % (nm, s)])
                for c in range(8):
                    pq = ps_q[c % 2]

                    def mmq(e, c=c, pq=pq):
                        ins = None
                        for kc in range(8):
                            ins = e.matmul(pq[:, 0:T], wq[:, kc, c * 128:(c + 1) * 128], xTq[:, kc, :],
                                           start=(kc == 0), stop=(kc == 7))
                        return ins

                    sc.add("tensor", mmq, reads=["wq", "xTq_0", "xTq_1"], writes=["ps_q%d" % (c % 2)])
                    sc.add("scalar", lambda e, c=c, pq=pq, t=t: e.activation(
                        out=qT[:, c, t * T:(t + 1) * T], in_=pq[:, 0:T], func=AF.Identity, bias=bq8[:, c:c + 1],
                        scale=0.125),
                        reads=["ps_q%d" % (c % 2), "bq8"], writes=["qT"])
                for kvh in range(4):
                    pk = ps_k[kvh % 2]

                    def mmk(e, kvh=kvh, pk=pk):
                        ins = None
                        for dup in range(2):
                            for kc in range(8):
                                ins = e.matmul(pk[dup * 64:(dup + 1) * 64, 0:T], wkv[:, kc, kvh * 64:(kvh + 1) * 64],
                                               xTk[:, kc, :], start=(kc == 0), stop=(kc == 7))
                        return ins

                    sc.add("tensor", mmk, reads=["wkv", "xTk_0", "xTk_1"], writes=["ps_k%d" % (kvh % 2)])
                    sc.add("vector", lambda e, kvh=kvh, pk=pk, t=t: e.tensor_scalar(
                        kT[:, kvh, t * T:(t + 1) * T], pk[:, 0:T], kbd[:, kvh:kvh + 1], None, op0=ALU.add),
                        reads=["ps_k%d" % (kvh % 2), "kbd0", "kbd1"], writes=["kT"])
                for s in range(2):
                    pv = ps_v[s]

                    def mmv(e, s=s, pv=pv):
                        ins = None
                        for kc in range(8):
                            ins = e.matmul(pv[:, 0:256], xTk[:, kc, s * 128:(s + 1) * 128], wkv[:, kc, 256:512],
                                           start=(kc == 0), stop=(kc == 7))
                        return ins

                    sc.add("tensor", mmv, reads=["wkv", "xTk_%d" % s], writes=["ps_v%d" % s])
                    sc.add("vector", lambda e, s=s, pv=pv, t=t: e.tensor_tensor(
                        out=vv[:, t * 2 + s, :], in0=pv[:, 0:256], in1=vbb[:], op=ALU.add),
                        reads=["ps_v%d" % s, "vbb"], writes=["vv"])
            sc.flush()
        if debug_stage == 1:
            return
        with ExitStack() as es:
            def sb(name, shape, dt):
                return es.enter_context(nc.sbuf_tensor(_uname(name), shape, dt))

            def psum(name, shape, dt):
                return es.enter_context(nc.psum_tensor(_uname(name), shape, dt))

            wo = sb("wo", [128, 8, D], BF16)
            bm = sb("bm", [128, 16, 256], F32)
            bob = sb("bob", [128, D], F32)
            sink = sb("sink", [128, 16], F32)
            hb = [sb("hb%d" % i, [128, D], F32) for i in range(2)]
            ss = [sb("ss%d" % i, [128, 4, 260], F32) for i in range(4)]
            pp = [sb("pp%d" % i, [128, 4, 260], BF16) for i in range(2)]
            pn = [sb("pn%d" % i, [128, 4, 256], BF16) for i in range(2)]
            pT = [sb("pT%d" % i, [128, 4, 2, 128], BF16) for i in range(2)]
            oT = sb("oT", [128, 8, 128], BF16)
            ho = [sb("ho%d" % i, [128, D], F32) for i in range(2)]
            st = [sb("st%d" % i, [128, 16], F32) for i in range(2)]
            kpl = [sb("kpl%d" % i, [128, 4, 256], BF16) for i in range(2)]
            kph = [sb("kph%d" % i, [128, 4, 256], BF16) for i in range(2)]
            ps_s = [psum("ps_s%d" % i, [128, 4, 256], F32) for i in range(2)]
            ps_pT = [psum("ps_pT%d" % i, [128, 4, 2, 128], BF16) for i in range(2)]
            ps_oT = psum("ps_oT", [128, 8, 128], F32)
            ps_of = ps_oT[:].rearrange("p c t -> p (c t)")

            for i in range(2):
                sc.add("gpsimd", lambda e, i=i: e.memset(kpl[i][:], 0.0), writes=["kpl%d" % i])
                sc.add("gpsimd", lambda e, i=i: e.memset(kph[i][:], 0.0), writes=["kph%d" % i])
            for kc in range(8):
                sc.dma("gpsimd", wo[:, kc, :], wo_ap[kc * 128:(kc + 1) * 128, :], writes=["wo"], chan="w%d" % (kc % 4))
            sc.dma("sync", bm[:], bm_ap, writes=["bm"], chan="const")
            sc.dma("sync", bob[:], _bc(bo_ap, 128), writes=["bob"], chan="const")
            sc.dma("sync", sink[:], _bc(sinks_ap, 128), writes=["sink"], chan="const")
            for hg in range(4):
                sc.add("vector", lambda e, hg=hg: e.tensor_copy(ss[hg][:, :, 0:1], sink[:, 4 * hg:4 * hg + 4].unsqueeze(2)),
                       reads=["sink"], writes=["ss%d" % hg])

            def load2(n):
                sl = n % 2
                sc.dma("sync", hb[sl][:], h_in[n * 128:(n + 1) * 128, :], writes=["hb%d" % sl], chan="hb%d" % sl)
                sc.add("gpsimd", lambda e, sl=sl: e.tensor_tensor(out=hb[sl][:], in0=hb[sl][:], in1=bob[:], op=ALU.add),
                       reads=["hb%d" % sl, "bob"], writes=["hb%d" % sl])

            def kpcopy(n):
                nk = 128 if n == 0 else 256
                k0 = 0 if n == 0 else (n - 1) * 128
                kp_ = (kpl[n % 2], kph[n % 2])
                sc.add("gpsimd", lambda e, kp_=kp_, nk=nk, k0=k0: e.tensor_copy(kp_[0][0:64, :, 0:nk],
                                                                            kT[0:64, :, k0:k0 + nk]),
                       reads=["kT"], writes=["kpl%d" % (n % 2)])
                sc.add("gpsimd", lambda e, kp_=kp_, nk=nk, k0=k0: e.tensor_copy(kp_[1][64:128, :, 0:nk],
                                                                            kT[64:128, :, k0:k0 + nk]),
                       reads=["kT"], writes=["kph%d" % (n % 2)])

            load2(0)
            kpcopy(0)
            for n in range(NBLK):
                if n + 1 < NBLK:
                    load2(n + 1)
                    kpcopy(n + 1)
                nk = 128 if n == 0 else 256
                nkb = nk // 128
                k0 = 0 if n == 0 else (n - 1) * 128
                blk0 = 0 if n == 0 else n - 1
                kp_ = (kpl[n % 2], kph[n % 2])
                for pair in range(2):
                    grp = [(2 * pair + j, j) for j in range(2)]

                    for hg, i2 in grp:
                        def mmqk(e, hg=hg, pss=ps_s[i2], n=n, nk=nk, kp_=kp_):
                            ins = None
                            for g in range(4):
                                h = 4 * hg + g
                                c, hf = h // 2, h % 2
                                ins = e.matmul(pss[:, g, 0:nk], qT[:, c, n * 128:(n + 1) * 128],
                                               kp_[hf][:, hg, 0:nk], start=True, stop=True)
                            return ins

                        sc.add("tensor", mmqk, reads=["qT", "kpl%d" % (n % 2), "kph%d" % (n % 2)],
                               writes=["ps_s%d" % i2])
                    for hg, i2 in grp:
                        sc.add("vector", lambda e, hg=hg, i2=i2, nk=nk: e.tensor_tensor(
                            out=ss[hg][:, :, 1:1 + nk], in0=ps_s[i2][:, :, 0:nk],
                            in1=bm[:, 4 * hg:4 * hg + 4, 256 - nk:256], op=ALU.add),
                            reads=["ps_s%d" % i2, "bm"], writes=["ss%db" % hg])
                    for hg, i2 in grp:
                        sc.add("vector", lambda e, hg=hg, i2=i2, nk=nk: e.tensor_reduce(
                            out=st[i2][:, 0:4], in_=ss[hg][:, :, 0:1 + nk], axis=AX.X, op=ALU.max, negate=True),
                            reads=["ss%db" % hg, "ss%d" % hg], writes=["negm%d" % i2])
                    for hg, i2 in grp:
                        for g in range(4):
                            sc.add("scalar", lambda e, g=g, hg=hg, i2=i2, nk=nk: e.activation(
                                out=pp[i2][:, g, 0:1 + nk], in_=ss[hg][:, g, 0:1 + nk], func=AF.Exp,
                                bias=st[i2][:, g:g + 1], accum_out=st[i2][:, 4 + g:5 + g]),
                                reads=["ss%db" % hg, "ss%d" % hg, "negm%d" % i2],
                                writes=["pp%d_%d" % (i2, g), "rsum%d_%d" % (i2, g)])
                    for hg, i2 in grp:
                        sc.add("vector", lambda e, i2=i2: e.reciprocal(st[i2][:, 8:12], st[i2][:, 4:8]),
                               reads=["rsum%d_%d" % (i2, g) for g in range(4)], writes=["rden%d" % i2])
                        for g in range(4):
                            sc.add("scalar", lambda e, i2=i2, nk=nk, g=g: e.activation(
                                out=pn[i2][:, g, 0:nk], in_=pp[i2][:, g, 1:1 + nk], func=AF.Copy,
                                scale=st[i2][:, 8 + g:9 + g]),
                                reads=["rden%d" % i2, "pp%d_%d" % (i2, g)], writes=["pn%d_%d" % (i2, g)])
                    for hg, i2 in grp:
                        def trp(e, i2=i2, nkb=nkb):
                            ins = None
                            for g in range(4):
                                for kb in range(nkb):
                                    ins = e.transpose(ps_pT[i2][:, g, kb, :], pn[i2][:, g, kb * 128:(kb + 1) * 128],
                                                      cx.identb[:])
                            return ins

                        sc.add("tensor", trp, reads=["pn%d_%d" % (i2, g) for g in range(4)] + ["identb"],
                               writes=["ps_pT%d" % i2])
                    for hg, i2 in grp:
                        eng = "scalar" if i2 == 0 else "vector"
                        if eng == "scalar":
                            sc.add("scalar", lambda e, i2=i2, nkb=nkb: e.copy(pT[i2][:, :, 0:nkb, :],
                                                                             ps_pT[i2][:, :, 0:nkb, :]),
                                   reads=["ps_pT%d" % i2], writes=["pT%d" % i2])
                        else:
                            sc.add("vector", lambda e, i2=i2, nkb=nkb: e.tensor_copy(pT[i2][:, :, 0:nkb, :],
                                                                                    ps_pT[i2][:, :, 0:nkb, :]),
                                   reads=["ps_pT%d" % i2], writes=["pT%d" % i2])
                    for hg, i2 in grp:
                        def mmpv(e, hg=hg, i2=i2, nkb=nkb, blk0=blk0):
                            ins = None
                            for g in range(4):
                                h = 4 * hg + g
                                c, hf = h // 2, h % 2
                                for kb in range(nkb):
                                    ins = e.matmul(ps_oT[hf * 64:(hf + 1) * 64, c, :],
                                                   vv[:, blk0 + kb, hg * 64:(hg + 1) * 64],
                                                   pT[i2][:, g, kb, :], start=(kb == 0), stop=(kb == nkb - 1))
                            return ins

                        sc.add("tensor", mmpv, reads=["pT%d" % i2, "vv"], writes=["ps_oT%d" % hg, "po%d" % (hg // 2)])
                sc.add("scalar", lambda e: e.copy(oT[:], ps_oT[:]), reads=["ps_oT%d" % hg for hg in range(4)],
                       writes=["oT"])
                sl = n % 2
                for nh in range(2):
                    def mmo(e, nh=nh):
                        ins = None
                        for c in range(8):
                            ins = e.matmul(ps_of[:, nh * 512:(nh + 1) * 512], oT[:, c, :],
                                           wo[:, c, nh * 512:(nh + 1) * 512], start=(c == 0), stop=(c == 7))
                        return ins

                    sc.add("tensor", mmo, reads=["oT", "wo"], writes=["po%d" % nh])
                    sc.add("vector", lambda e, nh=nh, sl=sl: e.tensor_tensor(
                        out=ho[sl][:, nh * 512:(nh + 1) * 512], in0=ps_of[:, nh * 512:(nh + 1) * 512],
                        in1=hb[sl][:, nh * 512:(nh + 1) * 512], op=ALU.add),
                        reads=["po%d" % nh, "hb%d" % sl], writes=["ho%d_%d" % (sl, nh)])
                sc.dma("sync", h_out[n * 128:(n + 1) * 128, :], ho[sl][:], reads=["ho%d_0" % sl, "ho%d_1" % sl],
                       chan="ho%d" % sl)
            sc.flush()


def make_biasmask():
    slopes = 2.0 ** (-8.0 * np.arange(1, 17, dtype=np.float64) / 16.0)
    q = np.arange(128)[:, None]
    kj = np.arange(256)[None, :]
    dist = q + 128 - kj
    valid = (dist >= 0) & (dist < 128)
    bm = np.where(valid[:, None, :], -slopes[None, :, None] * dist[:, None, :], -30000.0)
    return np.ascontiguousarray(bm.astype(np.float32))


A_IN = 4112


def delta_proj_phase(sc, nc, cx, x_in, g_ap, win_ap, cw_ap, alog_ap, dtb_ap, QF, KF, KT, VT, GT, BG, ntiles=None):
    T = 256
    NT = (S // T) if ntiles is None else ntiles
    with ExitStack() as es:
        def sb(name, shape, dt):
            return es.enter_context(nc.sbuf_tensor(_uname(name), shape, dt))

        def psum(name, shape, dt):
            return es.enter_context(nc.psum_tensor(_uname(name), shape, dt))

        win = sb("win", [128, 8, A_IN], BF16)
        hb = [sb("hb%d" % i, [128, 2, D], F32) for i in range(2)]
        gbc = sb("gbc", [128, D], F32)
        junk = sb("junk", [128, D], F32)
        xn = [sb("xn%d" % i, [128, D], BF16) for i in range(2)]
        stat = sb("stat", [128, 16], F32)
        xTs = [sb("xT%d" % i, [128, 8, T + 3], BF16) for i in range(2)]
        sF = sb("sF", [128, 24, T], F32)
        yb = [sb("yb%d" % i, [128, T], F32) for i in range(4)]
        cw = sb("cw", [128, 24, 4], F32)
        sqs = [sb("sq%d" % i, [128, 2, T], F32) for i in range(2)]
        rns = [sb("rn%d" % i, [128, 2, T], F32) for i in range(2)]
        ones = sb("ones", [128, 128], F32)
        oneb = sb("oneb", [128, 1], F32)
        ot = [sb("ot%d" % i, [128, D], F32) for i in range(2)]
        go = [sb("go%d" % i, [128, D], F32) for i in range(2)]
        bg = [sb("bg%d" % i, [128, 16], F32) for i in range(2)]
        ba = sb("ba", [128, 32], F32)
        negA = sb("negA", [128, 8], F32)
        dtb = sb("dtb", [128, 8], F32)
        ps_tr = psum("ps_tr", [128, D], BF16)
        ps_p = [psum("ps_p%d" % i, [128, 512], F32) for i in range(3)]
        ps_n = psum("ps_n", [128, 2, T], F32)
        ps_T = psum("ps_T", [128, 8, 128], F32)
        ps_g = psum("ps_g", [128, 512], F32)
        ps_b = ps_n[:].rearrange("p a t -> p (a t)")

        sc.dma("sync", gbc[:], _bc(g_ap, 128), writes=["gbc"], chan="const")
        sc.dma("sync", negA[:], _bc(alog_ap, 128), writes=["negA"], chan="const")
        sc.dma("sync", dtb[:], _bc(dtb_ap, 128), writes=["dtb"], chan="const")
        sc.add("scalar", lambda e: e.activation(out=negA[:], in_=negA[:], func=AF.Exp), reads=["negA"], writes=["negA"])
        sc.add("vector", lambda e: e.tensor_scalar(negA[:], negA[:], -1.0, None, op0=ALU.mult), reads=["negA"],
               writes=["negA"])
        sc.add("vector", lambda e: e.memset(ones[:], 1.0), writes=["ones"])
        sc.add("vector", lambda e: e.memset(oneb[:], 1.0), writes=["oneb"])
        lnq = sb("lnq", [128, 1], F32)
        lnk = sb("lnk", [128, 1], F32)
        sc.add("vector", lambda e: e.memset(lnq[:], -2.4260151319598084), writes=["lnsc"])
        sc.add("vector", lambda e: e.memset(lnk[:], 0.0), writes=["lnsc"])
        sc.add("vector", lambda e: e.memset(xTs[0][:, :, 0:3], 0.0), writes=["xT0_lead"])
        load_rows_transposed(sc, es, nc, cx, "cwa", cw_ap, 4, 24, ps_p[0], "ps_p0", cw[:], "cw")
        for kc in range(8):
            for part in range(2):
                lo = part * 2056
                sc.dma("gpsimd", win[:, kc, lo:lo + 2056], win_ap[kc * 128:(kc + 1) * 128, lo:lo + 2056],
                       writes=["win"], chan="w%d" % ((kc * 2 + part) % 4))

        def load(t):
            sl = t % 2
            sc.dma("sync", hb[sl][:], x_in[t * T:(t + 1) * T, :].rearrange("(s p) d -> p s d", p=128),
                   writes=["hb%d" % sl], chan="hb%d" % sl)

        def do_norm(t):
            sl = t % 2
            xT = xTs[sl]
            if t > 0:
                sc.add("vector", lambda e, sl=sl: e.tensor_copy(xTs[sl][:, :, 0:3], xTs[1 - sl][:, :, T:T + 3]),
                       reads=["xT%d_1" % (1 - sl)], writes=["xT%d_lead" % sl])
            for s in range(2):
                norm_transpose(sc, cx, "n%d" % s, hb[sl][:, s, :], "hb%d" % sl, gbc[:], "gbc",
                               stat[:, s:s + 1], stat[:, 4 + s:5 + s], junk[:], xn[s][:],
                               xT[:, :, 3 + s * 128:3 + (s + 1) * 128], "xT%d_%d" % (sl, s), ps_tr[:], "ps_tr")

        load(0)
        nst = 0
        nout = 0
        for t in range(NT):
            sl = t % 2
            t0 = t * T
            if t + 1 < NT:
                load(t + 1)
            if t == 0:
                do_norm(0)
            xT = xTs[sl]
            XK = "xT%d" % sl
            pend = None
            for c in range(24):
                pi = nst % 3
                si = nst % 4
                nst += 1
                pst = ps_p[pi]

                def mm(e, c=c, pst=pst, xT=xT):
                    ins = None
                    for kc in range(8):
                        ins = e.matmul(pst[:, 0:T + 3], win[:, kc, c * 128:(c + 1) * 128], xT[:, kc, :],
                                       start=(kc == 0), stop=(kc == 7))
                    return ins

                sc.add("tensor", mm, reads=["win", XK + "_0", XK + "_1", XK + "_lead"], writes=["ps_p%d" % pi])
                y_ = yb[si]
                sc.add("scalar", lambda e, pst=pst, y_=y_, c=c: e.activation(
                    out=y_[:], in_=pst[:, 3:T + 3], func=AF.Copy, scale=cw[:, c, 3:4]),
                    reads=["ps_p%d" % pi, "cw"], writes=["yb%d" % si])
                for j in (2, 1, 0):
                    sc.add("vector", lambda e, pst=pst, y_=y_, c=c, j=j: e.scalar_tensor_tensor(
                        out=y_[:], in0=pst[:, j:T + j], scalar=cw[:, c, j:j + 1], in1=y_[:], op0=ALU.mult, op1=ALU.add),
                        reads=["ps_p%d" % pi, "cw", "yb%d" % si], writes=["yb%d" % si])
                if pend is not None:
                    pend()
                pend = (lambda y_=y_, c=c, si=si: sc.add(
                    "scalar", lambda e: e.activation(out=sF[:, c, :], in_=y_[:], func=AF.Silu),
                    reads=["yb%d" % si], writes=["sF%d" % c]))
                if c == 20 and t + 1 < NT:
                    do_norm(t + 1)
            pend()
            def l2_sq(c):
                li = (c // 2) % 2
                sq = sqs[li]
                sc.add("scalar", lambda e, c=c, sq=sq: e.activation(out=sq[:], in_=sF[:, c:c + 2, :], func=AF.Square),
                       reads=["sF%d" % c, "sF%d" % (c + 1)], writes=["sq%d" % li])

            def l2_mm(c):
                li = (c // 2) % 2
                sq = sqs[li]

                def mmn(e, sq=sq):
                    ins = None
                    for i in range(2):
                        ins = e.matmul(ps_n[:, i, :], ones[:], sq[:, i, :], start=True, stop=True)
                    return ins

                sc.add("tensor", mmn, reads=["sq%d" % li, "ones"], writes=["ps_n"])

            def l2_fin(c):
                li = (c // 2) % 2
                rn = rns[li]
                sc.add("scalar", lambda e, rn=rn: e.activation(out=rn[:], in_=ps_n[:], func=AF.Ln, bias=cx.epsb[:],
                                                              scale=1.0),
                       reads=["ps_n", "epsb"], writes=["rn%d" % li])
                lb = lnq if c < 8 else lnk
                sc.add("scalar", lambda e, rn=rn, lb=lb: e.activation(out=rn[:], in_=rn[:], func=AF.Exp, bias=lb[:],
                                                                     scale=-0.5),
                       reads=["rn%d" % li, "lnsc"], writes=["rn%d" % li])
                sc.add("gpsimd", lambda e, c=c, rn=rn: e.tensor_tensor(
                    out=sF[:, c:c + 2, :], in0=sF[:, c:c + 2, :], in1=rn[:], op=ALU.mult),
                    reads=["rn%d" % li, "sF%d" % c, "sF%d" % (c + 1)], writes=["sF%d" % c, "sF%d" % (c + 1)])

            l2_sq(0)
            for c in range(0, 16, 2):
                l2_mm(c)
                if c + 2 < 16:
                    l2_sq(c + 2)
                l2_fin(c)
            sc.dma("sync", QF[:, :, t0:t0 + T].rearrange("h d t -> d h t"), sF[:, 0:8, :],
                   reads=["sF%d" % c for c in range(8)], chan="qf")
            sc.dma("sync", KF[:, :, t0:t0 + T].rearrange("h d t -> d h t"), sF[:, 8:16, :],
                   reads=["sF%d" % c for c in range(8, 16)], chan="kf")
            for s in range(2):
                for (base, dst, nm) in ((8, KT, "k"), (16, VT, "v")):
                    oi = nout % 2
                    nout += 1

                    def trT(e, base=base, s=s):
                        ins = None
                        for h in range(8):
                            ins = e.transpose(ps_T[:, h, :], sF[:, base + h, s * 128:(s + 1) * 128], cx.ident[:])
                        return ins

                    sc.add("tensor", trT, reads=["sF%d" % (base + h) for h in range(8)] + ["ident"], writes=["ps_T"])
                    sc.add("scalar", lambda e, oi=oi: e.copy(ot[oi][:], ps_T[:].rearrange("p h d -> p (h d)")),
                           reads=["ps_T"], writes=["ot%d" % oi])
                    sc.dma("sync", dst[t0 + s * 128:t0 + (s + 1) * 128, :], ot[oi][:], reads=["ot%d" % oi],
                           chan="ot%d" % oi)
            for s in range(2):
                gi = s
                for nh in range(2):
                    def mmg(e, s=s, nh=nh, xT=xT):
                        ins = None
                        for kc in range(8):
                            ins = e.matmul(ps_g[:], xT[:, kc, 3 + s * 128:3 + (s + 1) * 128],
                                           win[:, kc, 3072 + nh * 512:3072 + (nh + 1) * 512],
                                           start=(kc == 0), stop=(kc == 7))
                        return ins

                    sc.add("tensor", mmg, reads=["win", XK + "_%d" % s], writes=["ps_g"])
                    sc.add("scalar", lambda e, gi=gi, nh=nh: e.activation(out=go[gi][:, nh * 512:(nh + 1) * 512],
                                                                         in_=ps_g[:], func=AF.Silu),
                           reads=["ps_g"], writes=["go%d_%d" % (gi, nh)])
                sc.dma("sync", GT[t0 + s * 128:t0 + (s + 1) * 128, :], go[gi][:], reads=["go%d_0" % gi, "go%d_1" % gi],
                       chan="go%d" % gi)

                def mmb(e, s=s, xT=xT):
                    ins = None
                    for kc in range(8):
                        ins = e.matmul(ps_b[:, 0:16], xT[:, kc, 3 + s * 128:3 + (s + 1) * 128], win[:, kc, 4096:4112],
                                       start=(kc == 0), stop=(kc == 7))
                    return ins

                sc.add("tensor", mmb, reads=["win", XK + "_%d" % s], writes=["ps_n"])
                bg_ = bg[s]
                sc.add("scalar", lambda e, bg_=bg_: e.activation(out=bg_[:, 0:8], in_=ps_b[:, 0:8], func=AF.Sigmoid),
                       reads=["ps_n"], writes=["bg%d_b" % s])
                sc.add("vector", lambda e: e.tensor_tensor(out=ba[:, 0:8], in0=ps_b[:, 8:16], in1=dtb[:], op=ALU.add),
                       reads=["ps_n", "dtb", "bg%d_b" % s], writes=["ba0"])
                sc.add("scalar", lambda e: e.activation(out=ba[:, 8:16], in_=ba[:, 0:8], func=AF.Exp),
                       reads=["ba0"], writes=["ba1"])
                sc.add("scalar", lambda e: e.activation(out=ba[:, 16:24], in_=ba[:, 8:16], func=AF.Ln, bias=oneb[:],
                                                        scale=1.0),
                       reads=["ba1", "oneb"], writes=["ba2"])
                sc.add("vector", lambda e, bg_=bg_: e.tensor_tensor(out=bg_[:, 8:16], in0=ba[:, 16:24], in1=negA[:],
                                                                  op=ALU.mult),
                       reads=["ba2", "negA"], writes=["bg%d_g" % s])
                sc.dma("sync", BG[t0 + s * 128:t0 + (s + 1) * 128, :], bg_[:], reads=["bg%d_b" % s, "bg%d_g" % s],
                       chan="bg%d" % s)
        sc.flush()


def make_delta_consts():
    j = np.arange(64)[:, None]
    i = np.arange(64)[None, :]
    c = np.zeros((64, 3, 64), np.float32)
    c[:, 0, :] = (j <= i)
    c[:, 1, :] = np.where(i <= j, 0.0, -30000.0)
    c[:, 2, :] = (i < j)
    return c


def delta_phase(sc, nc, cx, x_in, h_out, QF, KF, KT, VT, GT, BG, og_ap, wout_ap, cst_ap, nchunks=None):
    NCHK = (S // 64) if nchunks is None else nchunks
    with ExitStack() as es:
        def sb(name, shape, dt):
            return es.enter_context(nc.sbuf_tensor(_uname(name), shape, dt))

        def psum(name, shape, dt):
            return es.enter_context(nc.psum_tensor(_uname(name), shape, dt))

        A = sc.add
        wout = sb("wout", [128, 8, D], BF16)
        cst = sb("cst", [128, 3, 64], F32)
        onesP = sb("onesP", [128, 128], F32)
        ogb = sb("ogb", [64, 128], F32)
        Sst = sb("Sst", [128, 8, 128], F32)
        Sb = sb("Sb", [128, 8, 128], BF16)
        Stmp = sb("Stmp", [128, 8, 128], F32)
        qFs = [sb("qFs%d" % i, [128, 8, 256], F32) for i in range(2)]
        kFs = [sb("kFs%d" % i, [128, 8, 256], F32) for i in range(2)]
        kTok = [sb("kTok%d" % i, [64, 8, 128], F32) for i in range(2)]
        vTok = [sb("vTok%d" % i, [64, 8, 128], F32) for i in range(2)]
        gTk = [sb("gTk%d" % i, [64, 8, 128], F32) for i in range(2)]
        xres = [sb("xres%d" % i, [64, D], F32) for i in range(2)]
        bgt = [sb("bgt%d" % i, [128, 16], F32) for i in range(2)]
        smM = [sb("sm%d" % i, [128, 64], F32) for i in range(2)]
        dGCM = [sb("dGC%d" % i, [128, 8, 64], F32) for i in range(2)]
        eRM = [sb("eR%d" % i, [128, 8, 64], F32) for i in range(2)]
        qdecM = [sb("qdec%d" % i, [128, 8, 64], BF16) for i in range(2)]
        DtM = [sb("Dt%d" % i, [64, 8, 64], F32) for i in range(2)]
        decM = [sb("dec%d" % i, [64, 8, 64], F32) for i in range(2)]
        t1M = [sb("t1%d" % i, [64, 8, 64], F32) for i in range(2)]
        attnM = [sb("attn%d" % i, [128, 8, 64], F32) for i in range(2)]
        PmM = [[sb("Pm%d_%d" % (m, i), [128, 8, 64], F32) for i in range(2)] for m in range(2)]
        PTmM = [[sb("PTm%d_%d" % (m, i), [128, 8, 64], F32) for i in range(2)] for m in range(2)]
        TTmM = [[sb("TTm%d_%d" % (m, i), [128, 8, 64], F32) for i in range(2)] for m in range(2)]
        TT16M = [sb("TT16%d" % i, [128, 8, 64], BF16) for i in range(2)]
        aT16M = [sb("aT16%d" % i, [128, 8, 64], BF16) for i in range(2)]
        VBbM = [sb("VBb%d" % i, [128, 8, 128], BF16) for i in range(2)]
        RKbM = [sb("RKb%d" % i, [128, 8, 128], BF16) for i in range(2)]
        KDbM = [sb("KDb%d" % i, [128, 8, 128], BF16) for i in range(2)]
        ggM = [sb("gg%d" % i, [64, 8, 128], F32) for i in range(2)]
        vnb = sb("vnb", [128, 8, 128], BF16)
        nk16 = sb("nk16", [128, 8, 64], BF16)
        osq = sb("osq", [64, 8, 128], F32)
        o1 = sb("o1", [128, 8, 128], F32)
        smc = sb("smc", [64, 8], F32)
        oT = sb("oT", [128, 8, 64], BF16)
        ho = [sb("ho%d" % i, [64, D], F32) for i in range(2)]
        pP = [psum("pP%d" % i, [128, 1024], F32) for i in range(4)]

        def v8(ap):
            return ap.rearrange("p (h j) -> p h j", h=8)

        kcd_ps = v8(pP[3][:, 512:1024])
        vn_ps = v8(pP[0][0:64, :])
        o_ps = v8(pP[1][0:64, :])
        Sn_ps = v8(pP[2][:, :])
        oT_ps = v8(pP[3][:, 0:512])
        out_ps = [pP[0][0:64, 0:512], pP[0][0:64, 512:1024]]

        for kc in range(8):
            sc.dma("gpsimd", wout[:, kc, :], wout_ap[kc * 128:(kc + 1) * 128, :], writes=["wout"], chan="w%d" % (kc % 4))
        A("vector", lambda e: e.memset(cst[:], 0.0), writes=["cst"])
        sc.dma("sync", cst[0:64, :, :], cst_ap, reads=[], writes=["cst"], chan="const")
        sc.dma("sync", ogb[:], _bc(og_ap, 64), writes=["ogb"], chan="const")
        A("vector", lambda e: e.memset(onesP[:], 0.0), writes=["onesP"])
        A("vector", lambda e: e.memset(onesP[0:64, :], 1.0), writes=["onesP"])
        A("vector", lambda e: e.memset(Sst[:], 0.0), writes=["Sst"])
        A("vector", lambda e: e.memset(Sb[:], 0.0), writes=["Sb"])
        zlist = []
        for m in range(2):
            zlist += [(PmM[m][0], "Pm%d_0" % m), (PmM[m][1], "Pm%d_1" % m), (PTmM[m][0], "PTm%d_0" % m),
                      (PTmM[m][1], "PTm%d_1" % m), (TTmM[m][0], "TTm%d_0" % m), (TTmM[m][1], "TTm%d_1" % m),
                      (TT16M[m], "TT16%d" % m), (aT16M[m], "aT16%d" % m), (VBbM[m], "VBb%d" % m),
                      (RKbM[m], "RKb%d" % m), (KDbM[m], "KDb%d" % m), (dGCM[m], "dGC%d" % m), (bgt[m], "bgt%d" % m),
                      (attnM[m], "attn%d" % m)]
        zlist += [(vnb, "vnb"), (o1, "o1")]
        zkeys = {}
        for i, (tl, nme) in enumerate(zlist):
            A("gpsimd", lambda e, tl=tl: e.memset(tl[:], 0.0), writes=["z%d" % i])
            zkeys[nme] = "z%d" % i

        U = cst[:, 0, :]
        mbi = cst[0:64, 1, :]
        st01 = cst[0:64, 2, :]
        id64 = cx.ident[0:64, 0:64]

        def bc_h(ap2d, n):
            return ap2d.unsqueeze(1).to_broadcast([n, 8, ap2d.shape[-1]])

        def bc_f(ap2d, n, f):
            return ap2d.unsqueeze(2).to_broadcast([n, 8, f])

        def load_super(st):
            b = st % 2
            t0 = st * 256
            sc.dma("sync", qFs[b][:], QF[:, :, t0:t0 + 256].rearrange("h d t -> d h t"), writes=["qFs%d" % b],
                   chan="qFs%d" % b)
            sc.dma("sync", kFs[b][:], KF[:, :, t0:t0 + 256].rearrange("h d t -> d h t"), writes=["kFs%d" % b],
                   chan="kFs%d" % b)

        def load_chunk(ch):
            b = ch % 2
            t0 = ch * 64
            sc.dma("sync", kTok[b][:], KT[t0:t0 + 64, :].rearrange("t (h d) -> t h d", h=8), writes=["kTok%d" % b],
                   chan="kTok%d" % b)
            sc.dma("sync", vTok[b][:], VT[t0:t0 + 64, :].rearrange("t (h d) -> t h d", h=8), writes=["vTok%d" % b],
                   chan="vTok%d" % b)
            sc.dma("sync", gTk[b][:], GT[t0:t0 + 64, :].rearrange("t (h d) -> t h d", h=8), writes=["gTk%d" % b],
                   chan="gTk%d" % b)
            sc.dma("sync", bgt[b][0:64, :], BG[t0:t0 + 64, :], reads=[zkeys["bgt%d" % b]], writes=["bgt%d" % b],
                   chan="bgt%d" % b)

        def load_xres(ch):
            b = ch % 2
            t0 = ch * 64
            sc.dma("sync", xres[b][:], x_in[t0:t0 + 64, :], writes=["xres%d" % b], chan="xres%d" % b)

        def pre(ch, m):
            st = ch // 4
            sbi = st % 2
            tl = (ch % 4) * 64
            qF = qFs[sbi][:, :, tl:tl + 64]
            kF = kFs[sbi][:, :, tl:tl + 64]
            kq = ["qFs%d" % sbi, "kFs%d" % sbi]
            bg_ = bgt[m]
            beta = bg_[0:64, 0:8]
            BGK = "bgt%d" % m
            sm, dGC, eR, qdec, Dt, dec, t1, attn = smM[m], dGCM[m], eRM[m], qdecM[m], DtM[m], decM[m], t1M[m], attnM[m]
            Pm, PTm, TTm = PmM[m], PTmM[m], TTmM[m]
            TT16, aT16, VBb, RKb, KDb, gg = TT16M[m], aT16M[m], VBbM[m], RKbM[m], KDbM[m], ggM[m]
            X = pP[2 * m][:, 0:512]
            psA = X[:, 0:16]
            Tup_ps = v8(pP[2 * m][0:64, 0:512])
            R_ps = v8(pP[2 * m][:, 512:1024])
            KK_ps = v8(pP[2 * m + 1][0:64, 0:512])
            QK_ps = v8(pP[2 * m + 1][0:64, 512:1024])
            bX, bR, bK, bQ = "b%d" % (4 * m), "b%d" % (4 * m + 1), "b%d" % (4 * m + 2), "b%d" % (4 * m + 3)
            M = str(m)

            def mmKK(e):
                ins = None
                for h in range(8):
                    ins = e.matmul(KK_ps[:, h, :], kF[:, h, :], kF[:, h, :], start=True, stop=True)
                for h in range(8):
                    ins = e.matmul(QK_ps[:, h, :], qF[:, h, :], kF[:, h, :], start=True, stop=True)
                return ins

            def mm1(e):
                e.matmul(psA[0:64, 0:8], U, bg_[:, 8:16], start=True, stop=True)
                return e.matmul(psA[:, 8:16], onesP[:], bg_[:, 8:16], start=True, stop=True)

            A("tensor", mm1, reads=[BGK, "cst", "onesP"], writes=[bX])
            A("tensor", mmKK, reads=kq, writes=[bK, bQ])
            yield
            A("vector", lambda e: e.tensor_copy(sm[0:64, 0:8], psA[0:64, 0:8]), reads=[bX], writes=["sm_g" + M])
            A("vector", lambda e: e.tensor_copy(sm[:, 8:16], psA[:, 8:16]), reads=[bX], writes=["sm_g2" + M])
            yield
            A("scalar", lambda e: e.activation(out=sm[0:64, 16:24], in_=sm[0:64, 0:8], func=AF.Exp), reads=["sm_g" + M],
              writes=["sm_e" + M])
            A("scalar", lambda e: e.activation(out=sm[:, 24:32], in_=sm[:, 8:16], func=AF.Exp), reads=["sm_g2" + M],
              writes=["sm_e2" + M])
            gc = sm[0:64, 0:8]
            A("vector", lambda e: e.tensor_tensor(out=dGC[0:64], in0=bc_h(id64, 64), in1=bc_f(gc, 64, 64), op=ALU.mult),
              reads=["sm_g" + M, "ident", zkeys["dGC" + M]], writes=["dGC" + M])
            yield
            A("tensor", lambda e: e.matmul(R_ps, onesP[:], dGC[:], start=True, stop=True), reads=["dGC" + M, "onesP"],
              writes=[bR])
            A("vector", lambda e: e.tensor_tensor(out=sm[0:64, 32:40], in0=sm[0:64, 8:16], in1=sm[0:64, 0:8],
                                                  op=ALU.subtract), reads=["sm_g" + M, "sm_g2" + M], writes=["sm_d" + M])
            A("vector", lambda e: e.tensor_tensor(out=sm[0:64, 48:56], in0=beta, in1=sm[0:64, 16:24], op=ALU.mult),
              reads=[BGK, "sm_e" + M], writes=["sm_b" + M])
            yield
            A("scalar", lambda e: e.activation(out=sm[0:64, 40:48], in_=sm[0:64, 32:40], func=AF.Exp),
              reads=["sm_d" + M], writes=["sm_k" + M])
            A("scalar", lambda e: e.activation(out=eR[:], in_=R_ps, func=AF.Exp), reads=[bR], writes=["eR" + M])
            yield
            A("vector", lambda e: e.tensor_tensor(out=Dt[:], in0=bc_f(gc, 64, 64), in1=R_ps[0:64], op=ALU.subtract),
              reads=[bR, "sm_g" + M, "eR" + M], writes=["Dt" + M])
            A("vector", lambda e: e.scalar_tensor_tensor(out=Dt[:], in0=Dt[:], scalar=0.0, in1=bc_h(mbi, 64),
                                                         op0=ALU.min, op1=ALU.add), reads=["Dt" + M, "cst"],
              writes=["Dt" + M])
            yield
            A("scalar", lambda e: e.activation(out=dec[:], in_=Dt[:], func=AF.Exp), reads=["Dt" + M], writes=["dec" + M])
            A("vector", lambda e: e.tensor_tensor(out=qdec[:], in0=qF, in1=eR[:], op=ALU.mult),
              reads=["eR" + M, kq[0]], writes=["qdec" + M])
            yield
            A("gpsimd", lambda e: e.tensor_tensor(out=t1[:], in0=dec[:], in1=bc_h(st01, 64), op=ALU.mult),
              reads=["dec" + M, "cst"], writes=["t1" + M])
            A("gpsimd", lambda e: e.tensor_tensor(out=t1[:], in0=t1[:], in1=bc_f(beta, 64, 64), op=ALU.mult),
              reads=["t1" + M, BGK], writes=["t1" + M])
            A("vector", lambda e: e.tensor_tensor(out=attn[0:64], in0=QK_ps, in1=dec[:], op=ALU.mult),
              reads=[bQ, "dec" + M, zkeys["attn" + M]], writes=["attn" + M])
            yield
            L = Pm[0]
            A("vector", lambda e: e.tensor_tensor(out=L[0:64], in0=KK_ps, in1=t1[:], op=ALU.mult),
              reads=[bK, "t1" + M, zkeys["Pm%d_0" % m]], writes=["Pm%d_0" % m])
            yield
            LT_ps, aT_ps = KK_ps, QK_ps

            def trL(e):
                ins = None
                for h in range(8):
                    ins = e.matmul(LT_ps[:, h, :], L[:, h, :], cx.ident[:, 0:64], start=True, stop=True)
                for h in range(8):
                    ins = e.matmul(aT_ps[:, h, :], attn[:, h, :], cx.ident[:, 0:64], start=True, stop=True)
                return ins

            A("tensor", trL, reads=["Pm%d_0" % m, "attn" + M, "ident"], writes=[bK, bQ])
            yield
            A("scalar", lambda e: e.copy(PTm[0][0:64], LT_ps), reads=[bK, zkeys["PTm%d_0" % m]], writes=["PTm%d_0" % m])
            yield
            A("vector", lambda e: e.tensor_tensor(out=TTm[0][0:64], in0=bc_h(id64, 64), in1=PTm[0][0:64],
                                                  op=ALU.subtract),
              reads=["PTm%d_0" % m, "ident", zkeys["TTm%d_0" % m]], writes=["TTm%d_0" % m])
            A("scalar", lambda e: e.copy(aT16[0:64], aT_ps), reads=[bQ, zkeys["aT16" + M]], writes=["aT16" + M])
            yield
            P2_ps, PT2_ps = KK_ps, QK_ps
            ci = 0
            for lvl in range(5):
                P, PT, Tc = Pm[ci], PTm[ci], TTm[ci]
                Pn, PTn, Tn = Pm[1 - ci], PTm[1 - ci], TTm[1 - ci]
                kP, kPT, kT_ = "Pm%d_%d" % (m, ci), "PTm%d_%d" % (m, ci), "TTm%d_%d" % (m, ci)
                kPn, kPTn, kTn = "Pm%d_%d" % (m, 1 - ci), "PTm%d_%d" % (m, 1 - ci), "TTm%d_%d" % (m, 1 - ci)
                last = lvl == 4

                def mmsq(e, P=P, PT=PT, last=last):
                    ins = None
                    for h in range(8):
                        ins = e.matmul(P2_ps[:, h, :], PT[:, h, :], P[:, h, :], start=True, stop=True)
                    if not last:
                        for h in range(8):
                            ins = e.matmul(PT2_ps[:, h, :], P[:, h, :], PT[:, h, :], start=True, stop=True)
                    return ins

                A("tensor", mmsq, reads=[kP, kPT], writes=[bK, bQ])
                yield
                A("scalar", lambda e, Pn=Pn: e.copy(Pn[0:64], P2_ps), reads=[bK, zkeys[kPn]], writes=[kPn])
                if not last:
                    A("vector", lambda e, PTn=PTn: e.tensor_copy(PTn[0:64], PT2_ps), reads=[bQ, zkeys[kPTn]],
                      writes=[kPTn])
                yield

                def mmT(e, Pn=Pn, Tc=Tc):
                    ins = None
                    for h in range(8):
                        ins = e.matmul(Tup_ps[:, h, :], Pn[:, h, :], Tc[:, h, :], start=True, stop=True)
                    return ins

                A("tensor", mmT, reads=[kPn, kT_], writes=[bX])
                yield
                A("vector", lambda e, Tn=Tn, Tc=Tc: e.tensor_tensor(out=Tn[0:64], in0=Tc[0:64], in1=Tup_ps, op=ALU.add),
                  reads=[bX, kT_, zkeys[kTn]], writes=[kTn])
                yield
                ci = 1 - ci
            Tfin = TTm[ci]
            kTf = "TTm%d_%d" % (m, ci)
            A("scalar", lambda e: e.copy(TT16[0:64], Tfin[0:64]), reads=[kTf, zkeys["TT16" + M]], writes=["TT16" + M])
            kT_b, vT_b = kTok[m], vTok[m]
            A("gpsimd", lambda e: e.tensor_tensor(out=VBb[0:64], in0=vT_b[:], in1=bc_f(beta, 64, 128), op=ALU.mult),
              reads=["vTok%d" % m, BGK, zkeys["VBb" + M]], writes=["VBb" + M])
            A("gpsimd", lambda e: e.tensor_tensor(out=RKb[0:64], in0=kT_b[:], in1=bc_f(sm[0:64, 48:56], 64, 128),
                                                  op=ALU.mult),
              reads=["kTok%d" % m, "sm_b" + M, zkeys["RKb" + M]], writes=["RKb" + M])
            A("gpsimd", lambda e: e.tensor_tensor(out=KDb[0:64], in0=kT_b[:], in1=bc_f(sm[0:64, 40:48], 64, 128),
                                                  op=ALU.mult),
              reads=["kTok%d" % m, "sm_k" + M, zkeys["KDb" + M]], writes=["KDb" + M])
            A("gpsimd", lambda e: e.tensor_tensor(out=gg[:], in0=gTk[m][:], in1=bc_h(ogb[:], 64), op=ALU.mult),
              reads=["gTk%d" % m, "ogb"], writes=["gg" + M])
            yield

        def chain(ch, m):
            sm = smM[m]
            M = str(m)
            TT16, aT16, VBb, RKb, KDb, gg, qdec = TT16M[m], aT16M[m], VBbM[m], RKbM[m], KDbM[m], ggM[m], qdecM[m]
            A("gpsimd", lambda e: e.tensor_tensor(out=Stmp[:], in0=Sst[:], in1=bc_f(sm[:, 24:32], 128, 128), op=ALU.mult),
              reads=["Sst", "sm_e2" + M], writes=["Stmp"])

            def mmkcd(e):
                ins = None
                for h in range(8):
                    ins = e.matmul(kcd_ps[:, h, :], RKb[:, h, :], TT16[:, h, :], start=True, stop=True)
                return ins

            A("tensor", mmkcd, reads=["RKb" + M, "TT16" + M], writes=["b7"])
            A("scalar", lambda e: e.activation(out=nk16[:], in_=kcd_ps, func=AF.Copy, scale=-1.0), reads=["b7"],
              writes=["nk16"])

            def mmvn(e):
                ins = None
                for h in range(8):
                    e.matmul(vn_ps[:, h, :], TT16[:, h, :], VBb[:, h, :], start=True, stop=False)
                    ins = e.matmul(vn_ps[:, h, :], nk16[:, h, :], Sb[:, h, :], start=False, stop=True)
                return ins

            A("tensor", mmvn, reads=["TT16" + M, "VBb" + M, "nk16", "Sb"], writes=["b0", "b1"])
            A("vector", lambda e: e.tensor_copy(vnb[0:64], vn_ps), reads=["b0", "b1", zkeys["vnb"]], writes=["vnb"])

            def mmo(e):
                ins = None
                for h in range(8):
                    e.matmul(o_ps[:, h, :], qdec[:, h, :], Sb[:, h, :], start=True, stop=False)
                    ins = e.matmul(o_ps[:, h, :], aT16[:, h, :], vnb[:, h, :], start=False, stop=True)
                return ins

            A("tensor", mmo, reads=["qdec" + M, "Sb", "aT16" + M, "vnb"], writes=["b2", "b3"])

            def mmS(e):
                ins = None
                for h in range(8):
                    ins = e.matmul(Sn_ps[:, h, :], KDb[:, h, :], vnb[:, h, :], start=True, stop=True)
                return ins

            A("tensor", mmS, reads=["KDb" + M, "vnb"], writes=["b4", "b5"])
            A("vector", lambda e: e.tensor_tensor(out=Sst[:], in0=Stmp[:], in1=Sn_ps, op=ALU.add),
              reads=["Stmp", "b4", "b5"], writes=["Sst"])
            A("scalar", lambda e: e.copy(Sb[:], Sst[:]), reads=["Sst"], writes=["Sb"])
            A("scalar", lambda e: e.activation(out=osq[:], in_=o_ps, func=AF.Square), reads=["b2", "b3"], writes=["osq"])
            A("vector", lambda e: e.tensor_reduce(out=smc[:], in_=osq[:], axis=AX.X, op=ALU.add),
              reads=["osq"], writes=["sm_o"])
            A("scalar", lambda e: e.activation(out=smc[:], in_=smc[:], func=AF.Sqrt, bias=cx.epsb[0:64, :],
                                               scale=1.0 / 128), reads=["sm_o", "epsb"], writes=["sm_o"])
            A("vector", lambda e: e.reciprocal(smc[:], smc[:]), reads=["sm_o"], writes=["sm_o"])
            A("vector", lambda e: e.tensor_tensor(out=o1[0:64], in0=o_ps, in1=bc_f(smc[:], 64, 128), op=ALU.mult),
              reads=["b2", "b3", "sm_o", zkeys["o1"]], writes=["o1"])
            A("vector", lambda e: e.tensor_tensor(out=o1[0:64], in0=o1[0:64], in1=gg[:], op=ALU.mult),
              reads=["o1", "gg" + M], writes=["o1"])

            def trO(e):
                ins = None
                for c in range(8):
                    ins = e.matmul(oT_ps[:, c, :], o1[:, c, :], cx.ident[:, 0:64], start=True, stop=True)
                return ins

            A("tensor", trO, reads=["o1", "ident"], writes=["b6"])
            A("scalar", lambda e: e.copy(oT[:], oT_ps), reads=["b6"], writes=["oT"])
            hob = ho[m]
            for nh in range(2):
                def mmout(e, nh=nh):
                    ins = None
                    for c in range(8):
                        ins = e.matmul(out_ps[nh], oT[:, c, :], wout[:, c, nh * 512:(nh + 1) * 512],
                                       start=(c == 0), stop=(c == 7))
                    return ins

                A("tensor", mmout, reads=["oT", "wout"], writes=["b%d" % nh])
                A("vector", lambda e, nh=nh: e.tensor_tensor(
                    out=hob[:, nh * 512:(nh + 1) * 512], in0=out_ps[nh], in1=xres[m][:, nh * 512:(nh + 1) * 512],
                    op=ALU.add), reads=["b%d" % nh, "xres%d" % m], writes=["ho%d_%d" % (m, nh)])
            sc.dma("sync", h_out[ch * 64:(ch + 1) * 64, :], hob[:], reads=["ho%d_0" % m, "ho%d_1" % m], chan="ho%d" % m)

        load_super(0)
        for c0 in range(min(2, NCHK)):
            load_chunk(c0)
            load_xres(c0)
        for pr in range(0, NCHK, 2):
            chs = [c for c in (pr, pr + 1) if c < NCHK]
            st = pr // 4
            if pr % 4 == 0 and (st + 1) * 4 < NCHK:
                load_super(st + 1)
            gens = [pre(c, c % 2) for c in chs]
            alive = list(gens)
            while alive:
                nxt = []
                for g in alive:
                    try:
                        next(g)
                        nxt.append(g)
                    except StopIteration:
                        pass
                alive = nxt
            for c in chs:
                if c + 2 < NCHK:
                    load_chunk(c + 2)
            for c in chs:
                chain(c, c % 2)
                if c + 2 < NCHK:
                    load_xres(c + 2)
        sc.flush()


_W_SHAPES = {
    "a_norm": [1, D], "a_w_in": [1, D, A_IN], "a_conv_w": [1, 4, 3072], "a_A_log": [1, 8], "a_dt_bias": [1, 8],
    "a_onorm": [1, 128], "a_w_out": [1, D, D], "kv_norm": [D], "kv_w": [D, 512], "kv_b": [512],
    "b_norm": [1, D], "b_w_q": [1, D, D], "b_b_q": [1, D], "b_sinks": [1, 16], "b_w_o": [1, D, D], "b_b_o": [1, D],
    "f_norm": [2, D], "f_w_up": [2, D, 2 * FF], "f_conv_w": [2, 3, 2 * FF], "f_conv_b": [2, 2 * FF],
    "f_w_down": [2, FF, D], "final_norm": [D],
}


def build_program():
    nc = bass.Bass("TRN2", target_bir_lowering=False)

    def din(name, shape):
        return nc.dram_tensor(name, shape, F32, kind="ExternalInput").ap()

    def dint(name, shape):
        return nc.dram_tensor(name, shape, F32, kind="Internal").ap()

    x = din("x", [S, D])
    w = {k: din(k, shp) for k, shp in _W_SHAPES.items()}
    ident = din("c_ident", [128, 128])
    bm = din("c_bm", [128, 16, 256])
    cst = din("c_delta", [64, 3, 64])
    out = nc.dram_tensor("out", [S, D], F32, kind="ExternalOutput").ap()
    QF = dint("s_QF", [8, 128, S])
    KF = dint("s_KF", [8, 128, S])
    KT = dint("s_KT", [S, D])
    VT = dint("s_VT", [S, D])
    GT = dint("s_GT", [S, D])
    BG = dint("s_BG", [S, 16])
    h1 = dint("s_h1", [S, D])
    h2 = dint("s_h2", [S, D])
    h3 = dint("s_h3", [S, D])
    with ExitStack() as es:
        block = es.enter_context(nc.Block())
        sc = Sched(nc, block, es)
        cx = Ctx()
        load_consts(sc, es, nc, cx, ident)
        delta_proj_phase(sc, nc, cx, x, w["a_norm"][0], w["a_w_in"][0], w["a_conv_w"][0], w["a_A_log"][0],
                         w["a_dt_bias"][0], QF, KF, KT, VT, GT, BG)
        delta_phase(sc, nc, cx, x, h1, QF, KF, KT, VT, GT, BG, w["a_onorm"][0], w["a_w_out"][0], cst)
        ffn_phase(sc, nc, cx, h1, h2, w["f_norm"][0], w["f_w_up"][0], w["f_conv_w"][0], w["f_conv_b"][0],
                  w["f_w_down"][0])
        attn_phase(sc, nc, cx, h2, h3, w["kv_norm"], w["kv_w"], w["kv_b"], w["b_norm"][0], w["b_w_q"][0],
                   w["b_b_q"][0], w["b_sinks"][0], w["b_w_o"][0], w["b_b_o"][0], bm)
        ffn_phase(sc, nc, cx, h3, out, w["f_norm"][1], w["f_w_up"][1], w["f_conv_w"][1], w["f_conv_b"][1],
                  w["f_w_down"][1], fin_ap=w["final_norm"])
    return nc


def kernel(**inputs):
    x = np.ascontiguousarray(np.asarray(inputs["x"], dtype=np.float32))
    consts = {
        "c_ident": np.eye(128, dtype=np.float32),
        "c_bm": make_biasmask(),
        "c_delta": make_delta_consts(),
    }
    shared = {k: np.ascontiguousarray(np.asarray(inputs[k], dtype=np.float32)) for k in _W_SHAPES}
    shared.update(consts)
    nc = build_program()
    in_maps = []
    for b in range(NB):
        m = dict(shared)
        m["x"] = np.ascontiguousarray(x[b])
        in_maps.append(m)
    res = run_bass_kernel_spmd(nc, in_maps, core_ids=list(range(NB)))
    return np.stack([np.asarray(r["out"], dtype=np.float32) for r in res.results], axis=0)
```

```python
import numpy as np
from contextlib import ExitStack
import concourse.bass as bass
import concourse.mybir as mybir
from concourse.bass_utils import run_bass_kernel_spmd

F32 = mybir.dt.float32
BF16 = mybir.dt.bfloat16
AF = mybir.ActivationFunctionType
ALU = mybir.AluOpType
AX = mybir.AxisListType

D = 1024
S = 4096
NB = 8
FF = 2816
EPS = 1e-6
ENGS = ["sync", "scalar", "vector", "gpsimd", "tensor"]


class Op:
    __slots__ = ("eng", "fn", "deps", "needed", "idx", "chan", "chan_val")

    def __init__(self, eng, fn):
        self.eng = eng
        self.fn = fn
        self.deps = []
        self.needed = False
        self.idx = None
        self.chan = None
        self.chan_val = None


class Sched:
    def __init__(self, nc, block, es):
        self.nc = nc
        self.block = block
        self.es = es
        self.sem = {e: es.enter_context(nc.semaphore("s_" + e)) for e in ENGS}
        self.count = {e: 0 for e in ENGS}
        self.chans = {}
        self.waited = {e: {} for e in ENGS}
        self._reset()

    def _reset(self):
        self.ops = {e: [] for e in ENGS}
        self.last_w = {}
        self.readers = {}

    def chan(self, name):
        if name not in self.chans:
            self.chans[name] = [self.es.enter_context(self.nc.semaphore("c_" + name)), 0, None]
        return self.chans[name]

    def add(self, eng, fn, reads=(), writes=(), chan=None):
        op = Op(eng, fn)
        deps = []
        for r in reads:
            w = self.last_w.get(r)
            if w is not None:
                deps.append(w)
        for w_ in writes:
            w = self.last_w.get(w_)
            if w is not None:
                deps.append(w)
            deps.extend(self.readers.get(w_, ()))
        if chan is not None:
            ch = self.chan(chan)
            if ch[2] is not None:
                deps.append(ch[2])
            ch[1] += 16
            ch[2] = op
            op.chan = ch
            op.chan_val = ch[1]
        seen = set()
        for d in deps:
            if id(d) in seen or d is op:
                continue
            seen.add(id(d))
            if d.eng == "tensor" and eng == "tensor" and d.chan is None:
                continue
            d.needed = True
            op.deps.append(d)
        for r in reads:
            self.readers.setdefault(r, []).append(op)
        for w_ in writes:
            self.last_w[w_] = op
            self.readers[w_] = []
        self.ops[eng].append(op)
        return op

    def dma(self, eng, out, in_, reads=(), writes=(), chan=None, **kw):
        assert chan is not None
        return self.add(eng, lambda e: e.dma_start(out=out, in_=in_, **kw), reads, writes, chan=chan)

    def flush(self):
        for e in ENGS:
            comp = [op for op in self.ops[e] if op.chan is None]
            if comp:
                comp[-1].needed = True
            for op in self.ops[e]:
                if op.chan is None and op.needed:
                    self.count[e] += 1
                    op.idx = self.count[e]
        finals = [(self.sem[e], self.count[e]) for e in ENGS if self.count[e] > 0]
        finals += [(ch[0], ch[1]) for ch in self.chans.values() if ch[1] > 0]

        def emit(ename):
            ops = self.ops[ename]
            waited = self.waited[ename]

            def body(eng):
                for op in ops:
                    for d in op.deps:
                        if d.chan is not None:
                            sem, val = d.chan[0], d.chan_val
                        else:
                            sem, val = self.sem[d.eng], d.idx
                        if waited.get(sem.num if hasattr(sem, "num") else id(sem), 0) < val:
                            eng.wait_ge(sem, val)
                            waited[sem.num if hasattr(sem, "num") else id(sem)] = val
                    ins = op.fn(eng)
                    if op.chan is not None:
                        ins.then_inc(op.chan[0], 16)
                    elif op.needed:
                        ins.then_inc(self.sem[ename], 1)
                for sem, val in finals:
                    key = sem.num if hasattr(sem, "num") else id(sem)
                    if waited.get(key, 0) < val:
                        eng.wait_ge(sem, val)
                        waited[key] = val

            getattr(self.block, ename)(body)

        for e in ENGS:
            emit(e)
        self._reset()


_UNIQ = [0]
_DBG = [0, 0]


def _uname(name):
    _UNIQ[0] += 1
    return "t%d_%s" % (_UNIQ[0], name)


def _bc(ap, n):
    return ap.partition_broadcast(n)


class Ctx:
    pass


def load_consts(sc, es, nc, cx, ident_ap):
    cx.ident = es.enter_context(nc.sbuf_tensor("sb_ident", [128, 128], F32))
    cx.identb = es.enter_context(nc.sbuf_tensor("sb_identb", [128, 128], BF16))
    cx.epsb = es.enter_context(nc.sbuf_tensor("sb_epsb", [128, 1], F32))
    sc.dma("sync", cx.ident[:], ident_ap, writes=["ident"], chan="const")
    sc.add("vector", lambda e: e.tensor_copy(cx.identb[:], cx.ident[:]), reads=["ident"], writes=["identb"])
    sc.add("vector", lambda e: e.memset(cx.epsb[:], EPS), writes=["epsb"])


def norm_transpose(sc, cx, key, x_ap, xkey, gbc, gkey, ssq, rstd, junk, xn, xT_dst, xTkey, ps_bf, pskey,
                   evac_eng="scalar"):
    sc.add("scalar", lambda e: e.activation(out=junk, in_=x_ap, func=AF.Square, accum_out=ssq),
           reads=[xkey], writes=["junk", key + "ssq"])
    sc.add("scalar", lambda e: e.activation(out=rstd, in_=ssq, func=AF.Sqrt, bias=cx.epsb[:], scale=1.0 / D),
           reads=[key + "ssq", "epsb"], writes=[key + "rstd"])
    sc.add("vector", lambda e: e.reciprocal(rstd, rstd), reads=[key + "rstd"], writes=[key + "rstd"])
    sc.add("vector", lambda e: e.scalar_tensor_tensor(out=xn, in0=x_ap, scalar=rstd, in1=gbc,
                                                      op0=ALU.mult, op1=ALU.mult),
           reads=[xkey, key + "rstd", gkey], writes=[key + "xn"])

    def tr(e):
        ins = None
        for c in range(8):
            ins = e.transpose(ps_bf[:, c * 128:(c + 1) * 128], xn[:, c * 128:(c + 1) * 128], cx.identb[:])
        return ins

    sc.add("tensor", tr, reads=[key + "xn", "identb"], writes=[pskey])
    src = ps_bf.rearrange("p (c t) -> p c t", c=8)
    if evac_eng == "scalar":
        sc.add("scalar", lambda e: e.copy(xT_dst, src), reads=[pskey], writes=[xTkey])
    else:
        sc.add("vector", lambda e: e.tensor_copy(xT_dst, src), reads=[pskey], writes=[xTkey])


def load_rows_transposed(sc, es, nc, cx, name, src_ap, nrow, nchunk, ps, pskey, dst, dstkey):
    tmp = es.enter_context(nc.sbuf_tensor(_uname(name + "_tmp"), [nchunk, nrow, 128], F32))
    for j in range(nrow):
        sc.dma("sync", tmp[:, j, :], src_ap[j].rearrange("(c p) -> c p", p=128), writes=[name + "_tmp%d" % j],
               chan="const")

    def tr(e):
        ins = None
        for j in range(nrow):
            ins = e.transpose(ps[:, j * nchunk:(j + 1) * nchunk], tmp[:, j, :], cx.ident[0:nchunk, 0:nchunk])
        return ins

    sc.add("tensor", tr, reads=[name + "_tmp%d" % j for j in range(nrow)] + ["ident"], writes=[pskey])
    sc.add("vector", lambda e: e.tensor_copy(dst.rearrange("p c j -> p j c"),
                                             ps[:, 0:nrow * nchunk].rearrange("p (j c) -> p j c", j=nrow)),
           reads=[pskey], writes=[dstkey])


def ffn_phase(sc, nc, cx, h_in, h_out, g_ap, wup_ap, cw_ap, cb_ap, wdn_ap, fin_ap=None, ntiles=None):
    T = 256
    NT = (S // T) if ntiles is None else ntiles
    NCH = 2 * FF // 128
    NG = FF // 128
    with ExitStack() as es:
        def sb(name, shape, dt):
            return es.enter_context(nc.sbuf_tensor(_uname(name), shape, dt))

        def psum(name, shape, dt):
            return es.enter_context(nc.psum_tensor(_uname(name), shape, dt))

        wup = sb("wup", [128, 8, 2 * FF], BF16)
        wdn = sb("wdn", [128, NG, D], BF16)
        hb = [sb("hb%d" % i, [128, 2, D], F32) for i in range(2)]
        gbc = sb("gbc", [128, D], F32)
        fgbc = sb("fgbc", [128, D], F32) if fin_ap is not None else None
        xn = [sb("xn%d" % i, [128, D], BF16) for i in range(2)]
        junk = sb("junk", [128, D], F32)
        stat = sb("stat", [128, 16], F32)
        xT = [sb("xT%d" % i, [128, 8, T + 2], BF16) for i in range(2)]
        yb = [sb("yb%d" % i, [128, T], F32) for i in range(4)]
        sg = [sb("sg%d" % i, [128, T], F32) for i in range(4)]
        act = [sb("act%d" % i, [128, NG, T], BF16) for i in range(2)]
        cw = sb("cw", [128, NCH, 3], F32)
        cb = sb("cb", [128, NCH, 1], F32)
        ps_tr = [psum("ps_tr0", [128, D], BF16)] * 2
        ps_up = [psum("ps_up%d" % i, [128, 512], F32) for i in range(4)]
        ps_dn = [psum("ps_dn%d" % i, [128, 512], F32) for i in range(3)]

        sc.dma("sync", gbc[:], _bc(g_ap, 128), writes=["gbc"], chan="const")
        if fin_ap is not None:
            sc.dma("sync", fgbc[:], _bc(fin_ap, 128), writes=["fgbc"], chan="const")
        load_rows_transposed(sc, es, nc, cx, "cw", cw_ap, 3, NCH, ps_up[0], "ps_up0", cw[:], "cw")
        load_rows_transposed(sc, es, nc, cx, "cb", cb_ap.rearrange("(o n) -> o n", o=1), 1, NCH, ps_up[1],
                             "ps_up1", cb[:], "cb")
        wsrc = wup_ap.rearrange("(kc p) n -> p kc n", p=128)
        nwb = 0
        for blk in range(0, FF, 512):
            w_ = min(512, FF - blk)
            for half in range(2):
                lo = half * FF + blk
                sc.dma("gpsimd", wup[:, :, lo:lo + w_], wsrc[:, :, lo:lo + w_],
                       writes=["wup_%d_%d" % (half, blk // 512)], chan="w%d" % (nwb % 4))
                nwb += 1
        for c in range(NG):
            sc.dma("gpsimd", wdn[:, c, :], wdn_ap[c * 128:(c + 1) * 128, :], writes=["wdn"], chan="w%d" % (c % 4))
        sc.add("vector", lambda e: e.memset(xT[0][:, :, 0:2], 0.0), writes=["xT0_lead"])

        def load(t):
            sl = t % 2
            sc.dma("sync", hb[sl][:], h_in[t * T:(t + 1) * T, :].rearrange("(s p) d -> p s d", p=128),
                   writes=["hb%d_0" % sl, "hb%d_1" % sl], chan="hb%d" % sl)

        def down_group(t, s):
            sl = t % 2
            act_ = act[sl]
            hk = "hb%d_%d" % (sl, s)
            for n in range(2):
                pi = (s * 2 + n) % 3
                psd = ps_dn[pi]

                def mmd(e, s=s, n=n, psd=psd, act_=act_):
                    ins = None
                    for c in range(NG):
                        ins = e.matmul(psd[:], act_[:, c, s * 128:(s + 1) * 128], wdn[:, c, n * 512:(n + 1) * 512],
                                       start=(c == 0), stop=(c == NG - 1))
                    return ins

                sc.add("tensor", mmd, reads=["wdn"] + ["act%d_%d" % (sl, c) for c in range(NG)], writes=["ps_dn%d" % pi])
                sc.add("vector", lambda e, s=s, n=n, psd=psd, sl=sl: e.tensor_tensor(
                    out=hb[sl][:, s, n * 512:(n + 1) * 512], in0=psd[:], in1=hb[sl][:, s, n * 512:(n + 1) * 512],
                    op=ALU.add),
                    reads=["ps_dn%d" % pi, hk], writes=[hk])
            row0 = t * T + s * 128
            if fin_ap is not None:
                k = "f%d" % s
                sc.add("scalar", lambda e, sl=sl, s=s: e.activation(out=junk[:], in_=hb[sl][:, s, :], func=AF.Square,
                                                                    accum_out=stat[:, 8 + s:9 + s]),
                       reads=[hk], writes=["junk", k + "ssq"])
                sc.add("scalar", lambda e, s=s: e.activation(out=stat[:, 12 + s:13 + s], in_=stat[:, 8 + s:9 + s],
                                                             func=AF.Sqrt, bias=cx.epsb[:], scale=1.0 / D),
                       reads=[k + "ssq", "epsb"], writes=[k + "rstd"])
                sc.add("vector", lambda e, s=s: e.reciprocal(stat[:, 12 + s:13 + s], stat[:, 12 + s:13 + s]),
                       reads=[k + "rstd"], writes=[k + "rstd"])
                sc.add("vector", lambda e, sl=sl, s=s: e.scalar_tensor_tensor(
                    out=hb[sl][:, s, :], in0=hb[sl][:, s, :], scalar=stat[:, 12 + s:13 + s], in1=fgbc[:],
                    op0=ALU.mult, op1=ALU.mult),
                    reads=[hk, k + "rstd", "fgbc"], writes=[hk])
            sc.dma("sync", h_out[row0:row0 + 128, :], hb[sl][:, s, :], reads=[hk], chan="ho%d_%d" % (sl, s))

        def do_norm(t):
            sl = t % 2
            for s in range(2):
                k = "n%d" % s
                norm_transpose(sc, cx, k, hb[sl][:, s, :], "hb%d_%d" % (sl, s), gbc[:], "gbc",
                               stat[:, s:s + 1], stat[:, 4 + s:5 + s], junk[:], xn[s][:],
                               xT[sl][:, :, 2 + s * 128:2 + (s + 1) * 128], "xT%d_%d" % (sl, s),
                               ps_tr[s][:], "ps_tr0")
            if t > 0:
                sc.add("vector", lambda e, sl=sl: e.tensor_copy(xT[sl][:, :, 0:2], xT[1 - sl][:, :, T:T + 2]),
                       reads=["xT%d_1" % (1 - sl)], writes=["xT%d_lead" % sl])

        load(0)
        if NT > 1:
            load(1)
        nst = 0
        for t in range(NT):
            sl = t % 2
            if t == 0:
                do_norm(0)
            for j_ in range(NG):
                late = []
                for which in range(2):
                    c = j_ + which * NG
                    pi = nst % 4
                    si = nst % 4
                    nst += 1
                    pst = ps_up[pi]

                    def mm(e, c=c, pst=pst, sl=sl):
                        ins = None
                        for kc in range(8):
                            ins = e.matmul(pst[:, 0:T + 2], wup[:, kc, c * 128:(c + 1) * 128], xT[sl][:, kc, :],
                                           start=(kc == 0), stop=(kc == 7))
                        return ins

                    sc.add("tensor", mm, reads=["wup_%d_%d" % (which, (j_ * 128) // 512), "xT%d_0" % sl, "xT%d_1" % sl,
                                                "xT%d_lead" % sl],
                           writes=["ps_up%d" % pi])
                    y_ = yb[si]
                    sc.add("scalar", lambda e, pst=pst, y_=y_, c=c: e.activation(
                        out=y_[:], in_=pst[:, 2:T + 2], func=AF.Identity, bias=cb[:, c, 0:1], scale=cw[:, c, 2:3]),
                        reads=["ps_up%d" % pi, "cw", "cb"], writes=["yb%d" % si])
                    for j in (1, 0):
                        sc.add("vector", lambda e, pst=pst, y_=y_, c=c, j=j: e.scalar_tensor_tensor(
                            out=y_[:], in0=pst[:, j:T + j], scalar=cw[:, c, j:j + 1], in1=y_[:],
                            op0=ALU.mult, op1=ALU.add),
                            reads=["ps_up%d" % pi, "cw", "yb%d" % si], writes=["yb%d" % si])
                    if which == 0:
                        gi = j_ % 4
                        sg_ = sg[gi]
                        late.append(lambda sg_=sg_, y_=y_, si=si, gi=gi: sc.add(
                            "scalar", lambda e: e.activation(out=sg_[:], in_=y_[:], func=AF.Silu),
                            reads=["yb%d" % si], writes=["sg%d" % gi]))
                    else:
                        gi = j_ % 4
                        sg_ = sg[gi]
                        late.append(lambda sg_=sg_, y_=y_, j_=j_, sl=sl, si=si, gi=gi: sc.add(
                            "gpsimd", lambda e: e.tensor_tensor(out=act[sl][:, j_, :], in0=sg_[:], in1=y_[:], op=ALU.mult),
                            reads=["sg%d" % gi, "yb%d" % si], writes=["act%d_%d" % (sl, j_)]))
                for fn_ in late:
                    fn_()
                if t >= 1 and j_ in (5, 15):
                    down_group(t - 1, 0 if j_ == 5 else 1)
                    if j_ == 15 and t + 1 < NT:
                        load(t + 1)
                if j_ == 18 and t + 1 < NT:
                    do_norm(t + 1)
        for s in range(2):
            down_group(NT - 1, s)
        sc.flush()


def attn_phase(sc, nc, cx, h_in, h_out, gkv_ap, wkv_ap, bkv_ap, gq_ap, wq_ap, bq_ap, sinks_ap, wo_ap, bo_ap,
               bm_ap, nblk=None, debug_stage=0):
    NBLK = (S // 128) if nblk is None else nblk
    SL = NBLK * 128
    T = 256
    NT = SL // T
    with ExitStack() as es0:
        def sb0(name, shape, dt):
            return es0.enter_context(nc.sbuf_tensor(_uname(name), shape, dt))

        qT = sb0("qT", [128, 8, SL], BF16)
        kT = sb0("kT", [128, 4, SL], BF16)
        vv = sb0("vv", [128, NBLK, 256], BF16)
        with ExitStack() as es:
            def sb(name, shape, dt):
                return es.enter_context(nc.sbuf_tensor(_uname(name), shape, dt))

            def psum(name, shape, dt):
                return es.enter_context(nc.psum_tensor(_uname(name), shape, dt))

            wq = sb("wq", [128, 8, D], BF16)
            wkv = sb("wkv", [128, 8, 512], BF16)
            hb = [sb("hb%d" % i, [128, 2, D], F32) for i in range(2)]
            gq = sb("gq", [128, D], F32)
            gkv = sb("gkv", [128, D], F32)
            junk = sb("junk", [128, D], F32)
            xnq = [sb("xnq%d" % i, [128, D], BF16) for i in range(2)]
            xnk = [sb("xnk%d" % i, [128, D], BF16) for i in range(2)]
            xTq = sb("xTq", [128, 8, T], BF16)
            xTk = sb("xTk", [128, 8, T], BF16)
            stat = sb("stat", [128, 8], F32)
            bq8 = sb("bq8", [128, 8], F32)
            kbd = sb("kbd", [128, 4], F32)
            vbb = sb("vbb", [128, 256], F32)
            ps_tr = [psum("ps_tr%d" % i, [128, D], BF16) for i in range(2)]
            ps_q = [psum("ps_q%d" % i, [128, 512], F32) for i in range(2)]
            ps_k = [psum("ps_k%d" % i, [128, 512], F32) for i in range(2)]
            ps_v = [psum("ps_v%d" % i, [128, 512], F32) for i in range(2)]

            sc.dma("sync", gq[:], _bc(gq_ap, 128), writes=["gq"], chan="const")
            sc.dma("sync", gkv[:], _bc(gkv_ap, 128), writes=["gkv"], chan="const")
            sc.dma("sync", vbb[:], _bc(bkv_ap[256:512], 128), writes=["vbb"], chan="const")
            sc.dma("sync", bq8[:], bq_ap.rearrange("(c p) -> p c", p=128), writes=["bq8"], chan="const",
                   allow_slow_non_contiguous=True)
            for dup in range(2):
                sc.dma("sync", kbd[dup * 64:(dup + 1) * 64, :], bkv_ap[0:256].rearrange("(k p) -> p k", p=64),
                       writes=["kbd%d" % dup], chan="const", allow_slow_non_contiguous=True)
            sc.add("vector", lambda e: e.tensor_scalar(bq8[:], bq8[:], 0.125, None, op0=ALU.mult),
                   reads=["bq8"], writes=["bq8"])
            for kc in range(8):
                sc.dma("gpsimd", wq[:, kc, :], wq_ap[kc * 128:(kc + 1) * 128, :], writes=["wq"], chan="w%d" % (kc % 4))
            for kc in range(8):
                sc.dma("gpsimd", wkv[:, kc, :], wkv_ap[kc * 128:(kc + 1) * 128, :], writes=["wkv"],
                       chan="w%d" % (kc % 4))

            def load(t):
                sl = t % 2
                sc.dma("sync", hb[sl][:], h_in[t * T:(t + 1) * T, :].rearrange("(s p) d -> p s d", p=128),
                       writes=["hb%d" % sl], chan="hb%d" % sl)

            load(0)
            for t in range(NT):
                sl = t % 2
                if t + 1 < NT:
                    load(t + 1)
                for s in range(2):
                    x_ap = hb[sl][:, s, :]
                    ssq = stat[:, s:s + 1]
                    rstd = stat[:, 4 + s:5 + s]
                    k = "n%d" % s
                    sc.add("scalar", lambda e, x_ap=x_ap, ssq=ssq: e.activation(out=junk[:], in_=x_ap, func=AF.Square,
                                                                               accum_out=ssq),
                           reads=["hb%d" % sl], writes=["junk", k + "ssq"])
                    sc.add("scalar", lambda e, ssq=ssq, rstd=rstd: e.activation(out=rstd, in_=ssq, func=AF.Sqrt,
                                                                               bias=cx.epsb[:], scale=1.0 / D),
                           reads=[k + "ssq", "epsb"], writes=[k + "rstd"])
                    sc.add("vector", lambda e, rstd=rstd: e.reciprocal(rstd, rstd), reads=[k + "rstd"],
                           writes=[k + "rstd"])
                    sc.add("vector", lambda e, s=s, x_ap=x_ap, rstd=rstd: e.scalar_tensor_tensor(
                        out=xnq[s][:], in0=x_ap, scalar=rstd, in1=gq[:], op0=ALU.mult, op1=ALU.mult),
                        reads=["hb%d" % sl, k + "rstd", "gq"], writes=["xnq%d" % s])
                    sc.add("vector", lambda e, s=s, x_ap=x_ap, rstd=rstd: e.scalar_tensor_tensor(
                        out=xnk[s][:], in0=x_ap, scalar=rstd, in1=gkv[:], op0=ALU.mult, op1=ALU.mult),
                        reads=["hb%d" % sl, k + "rstd", "gkv"], writes=["xnk%d" % s])
                    for (xn_, xT_, nm, pi) in ((xnq[s], xTq, "q", 0), (xnk[s], xTk, "k", 1)):
                        def tr(e, xn_=xn_, pi=pi):
                            ins = None
                            for c in range(8):
                                ins = e.transpose(ps_tr[pi][:, c * 128:(c + 1) * 128], xn_[:, c * 128:(c + 1) * 128],
                                                  cx.identb[:])
                            return ins

                        sc.add("tensor", tr, reads=["xn%s%d" % (nm, s), "identb"], writes=["ps_tr%d" % pi])
                        sc.add("scalar", lambda e, xT_=xT_, pi=pi, s=s: e.copy(
                            xT_[:, :, s * 128:(s + 1) * 128], ps_tr[pi][:].rearrange("p (c t) -> p c t", c=8)),
                            reads=["ps_tr%d" % pi], writes=["xT%s_%d" % (nm, s)])
                for c in range(8):
                    pq = ps_q[c % 2]

                    def mmq(e, c=c, pq=pq):
                        ins = None
                        for kc in range(8):
                            ins = e.matmul(pq[:, 0:T], wq[:, kc, c * 128:(c + 1) * 128], xTq[:, kc, :],
                                           start=(kc == 0), stop=(kc == 7))
                        return ins

                    sc.add("tensor", mmq, reads=["wq", "xTq_0", "xTq_1"], writes=["ps_q%d" % (c % 2)])
                    sc.add("scalar", lambda e, c=c, pq=pq, t=t: e.activation(
                        out=qT[:, c, t * T:(t + 1) * T], in_=pq[:, 0:T], func=AF.Identity, bias=bq8[:, c:c + 1],
                        scale=0.125),
                        reads=["ps_q%d" % (c % 2), "bq8"], writes=["qT"])
                for kvh in range(4):
                    pk = ps_k[kvh % 2]

                    def mmk(e, kvh=kvh, pk=pk):
                        ins = None
                        for dup in range(2):
                            for kc in range(8):
                                ins = e.matmul(pk[dup * 64:(dup + 1) * 64, 0:T], wkv[:, kc, kvh * 64:(kvh + 1) * 64],
                                               xTk[:, kc, :], start=(kc == 0), stop=(kc == 7))
                        return ins

                    sc.add("tensor", mmk, reads=["wkv", "xTk_0", "xTk_1"], writes=["ps_k%d" % (kvh % 2)])
                    sc.add("vector", lambda e, kvh=kvh, pk=pk, t=t: e.tensor_scalar(
                        kT[:, kvh, t * T:(t + 1) * T], pk[:, 0:T], kbd[:, kvh:kvh + 1], None, op0=ALU.add),
                        reads=["ps_k%d" % (kvh % 2), "kbd0", "kbd1"], writes=["kT"])
                for s in range(2):
                    pv = ps_v[s]

                    def mmv(e, s=s, pv=pv):
                        ins = None
                        for kc in range(8):
                            ins = e.matmul(pv[:, 0:256], xTk[:, kc, s * 128:(s + 1) * 128], wkv[:, kc, 256:512],
                                           start=(kc == 0), stop=(kc == 7))
                        return ins

                    sc.add("tensor", mmv, reads=["wkv", "xTk_%d" % s], writes=["ps_v%d" % s])
                    sc.add("vector", lambda e, s=s, pv=pv, t=t: e.tensor_tensor(
                        out=vv[:, t * 2 + s, :], in0=pv[:, 0:256], in1=vbb[:], op=ALU.add),
                        reads=["ps_v%d" % s, "vbb"], writes=["vv"])
            sc.flush()
        if debug_stage == 1:
            return
        with ExitStack() as es:
            def sb(name, shape, dt):
                return es.enter_context(nc.sbuf_tensor(_uname(name), shape, dt))

            def psum(name, shape, dt):
                return es.enter_context(nc.psum_tensor(_uname(name), shape, dt))

            wo = sb("wo", [128, 8, D], BF16)
            bm = sb("bm", [128, 16, 256], F32)
            bob = sb("bob", [128, D], F32)
            sink = sb("sink", [128, 16], F32)
            hb = [sb("hb%d" % i, [128, D], F32) for i in range(2)]
            ss = [sb("ss%d" % i, [128, 4, 260], F32) for i in range(4)]
            pp = [sb("pp%d" % i, [128, 4, 260], BF16) for i in range(2)]
            pn = [sb("pn%d" % i, [128, 4, 256], BF16) for i in range(2)]
            pT = [sb("pT%d" % i, [128, 4, 2, 128], BF16) for i in range(2)]
            oT = sb("oT", [128, 8, 128], BF16)
            ho = [sb("ho%d" % i, [128, D], F32) for i in range(2)]
            st = [sb("st%d" % i, [128, 16], F32) for i in range(2)]
            kpl = [sb("kpl%d" % i, [128, 4, 256], BF16) for i in range(2)]
            kph = [sb("kph%d" % i, [128, 4, 256], BF16) for i in range(2)]
            ps_s = [psum("ps_s%d" % i, [128, 4, 256], F32) for i in range(2)]
            ps_pT = [psum("ps_pT%d" % i, [128, 4, 2, 128], BF16) for i in range(2)]
            ps_oT = psum("ps_oT", [128, 8, 128], F32)
            ps_of = ps_oT[:].rearrange("p c t -> p (c t)")

            for i in range(2):
                sc.add("gpsimd", lambda e, i=i: e.memset(kpl[i][:], 0.0), writes=["kpl%d" % i])
                sc.add("gpsimd", lambda e, i=i: e.memset(kph[i][:], 0.0), writes=["kph%d" % i])
            for kc in range(8):
                sc.dma("gpsimd", wo[:, kc, :], wo_ap[kc * 128:(kc + 1) * 128, :], writes=["wo"], chan="w%d" % (kc % 4))
            sc.dma("sync", bm[:], bm_ap, writes=["bm"], chan="const")
            sc.dma("sync", bob[:], _bc(bo_ap, 128), writes=["bob"], chan="const")
            sc.dma("sync", sink[:], _bc(sinks_ap, 128), writes=["sink"], chan="const")
            for hg in range(4):
                sc.add("vector", lambda e, hg=hg: e.tensor_copy(ss[hg][:, :, 0:1], sink[:, 4 * hg:4 * hg + 4].unsqueeze(2)),
                       reads=["sink"], writes=["ss%d" % hg])

            def load2(n):
                sl = n % 2
                sc.dma("sync", hb[sl][:], h_in[n * 128:(n + 1) * 128, :], writes=["hb%d" % sl], chan="hb%d" % sl)
                sc.add("gpsimd", lambda e, sl=sl: e.tensor_tensor(out=hb[sl][:], in0=hb[sl][:], in1=bob[:], op=ALU.add),
                       reads=["hb%d" % sl, "bob"], writes=["hb%d" % sl])

            def kpcopy(n):
                nk = 128 if n == 0 else 256
                k0 = 0 if n == 0 else (n - 1) * 128
                kp_ = (kpl[n % 2], kph[n % 2])
                sc.add("gpsimd", lambda e, kp_=kp_, nk=nk, k0=k0: e.tensor_copy(kp_[0][0:64, :, 0:nk],
                                                                            kT[0:64, :, k0:k0 + nk]),
                       reads=["kT"], writes=["kpl%d" % (n % 2)])
                sc.add("gpsimd", lambda e, kp_=kp_, nk=nk, k0=k0: e.tensor_copy(kp_[1][64:128, :, 0:nk],
                                                                            kT[64:128, :, k0:k0 + nk]),
                       reads=["kT"], writes=["kph%d" % (n % 2)])

            load2(0)
            kpcopy(0)
            for n in range(NBLK):
                if n + 1 < NBLK:
                    load2(n + 1)
                    kpcopy(n + 1)
                nk = 128 if n == 0 else 256
                nkb = nk // 128
                k0 = 0 if n == 0 else (n - 1) * 128
                blk0 = 0 if n == 0 else n - 1
                kp_ = (kpl[n % 2], kph[n % 2])
                for pair in range(2):
                    grp = [(2 * pair + j, j) for j in range(2)]

                    for hg, i2 in grp:
                        def mmqk(e, hg=hg, pss=ps_s[i2], n=n, nk=nk, kp_=kp_):
                            ins = None
                            for g in range(4):
                                h = 4 * hg + g
                                c, hf = h // 2, h % 2
                                ins = e.matmul(pss[:, g, 0:nk], qT[:, c, n * 128:(n + 1) * 128],
                                               kp_[hf][:, hg, 0:nk], start=True, stop=True)
                            return ins

                        sc.add("tensor", mmqk, reads=["qT", "kpl%d" % (n % 2), "kph%d" % (n % 2)],
                               writes=["ps_s%d" % i2])
                    for hg, i2 in grp:
                        sc.add("vector", lambda e, hg=hg, i2=i2, nk=nk: e.tensor_tensor(
                            out=ss[hg][:, :, 1:1 + nk], in0=ps_s[i2][:, :, 0:nk],
                            in1=bm[:, 4 * hg:4 * hg + 4, 256 - nk:256], op=ALU.add),
                            reads=["ps_s%d" % i2, "bm"], writes=["ss%db" % hg])
                    for hg, i2 in grp:
                        sc.add("vector", lambda e, hg=hg, i2=i2, nk=nk: e.tensor_reduce(
                            out=st[i2][:, 0:4], in_=ss[hg][:, :, 0:1 + nk], axis=AX.X, op=ALU.max, negate=True),
                            reads=["ss%db" % hg, "ss%d" % hg], writes=["negm%d" % i2])
                    for hg, i2 in grp:
                        for g in range(4):
                            sc.add("scalar", lambda e, g=g, hg=hg, i2=i2, nk=nk: e.activation(
                                out=pp[i2][:, g, 0:1 + nk], in_=ss[hg][:, g, 0:1 + nk], func=AF.Exp,
                                bias=st[i2][:, g:g + 1], accum_out=st[i2][:, 4 + g:5 + g]),
                                reads=["ss%db" % hg, "ss%d" % hg, "negm%d" % i2],
                                writes=["pp%d_%d" % (i2, g), "rsum%d_%d" % (i2, g)])
                    for hg, i2 in grp:
                        sc.add("vector", lambda e, i2=i2: e.reciprocal(st[i2][:, 8:12], st[i2][:, 4:8]),
                               reads=["rsum%d_%d" % (i2, g) for g in range(4)], writes=["rden%d" % i2])
                        for g in range(4):
                            sc.add("scalar", lambda e, i2=i2, nk=nk, g=g: e.activation(
                                out=pn[i2][:, g, 0:nk], in_=pp[i2][:, g, 1:1 + nk], func=AF.Copy,
                                scale=st[i2][:, 8 + g:9 + g]),
                                reads=["rden%d" % i2, "pp%d_%d" % (i2, g)], writes=["pn%d_%d" % (i2, g)])
                    for hg, i2 in grp:
                        def trp(e, i2=i2, nkb=nkb):
                            ins = None
                            for g in range(4):
                                for kb in range(nkb):
                                    ins = e.transpose(ps_pT[i2][:, g, kb, :], pn[i2][:, g, kb * 128:(kb + 1) * 128],
                                                      cx.identb[:])
                            return ins

                        sc.add("tensor", trp, reads=["pn%d_%d" % (i2, g) for g in range(4)] + ["identb"],
                               writes=["ps_pT%d" % i2])
                    for hg, i2 in grp:
                        eng = "scalar" if i2 == 0 else "vector"
                        if eng == "scalar":
                            sc.add("scalar", lambda e, i2=i2, nkb=nkb: e.copy(pT[i2][:, :, 0:nkb, :],
                                                                             ps_pT[i2][:, :, 0:nkb, :]),
                                   reads=["ps_pT%d" % i2], writes=["pT%d" % i2])
                        else:
                            sc.add("vector", lambda e, i2=i2, nkb=nkb: e.tensor_copy(pT[i2][:, :, 0:nkb, :],
                                                                                    ps_pT[i2][:, :, 0:nkb, :]),
                                   reads=["ps_pT%d" % i2], writes=["pT%d" % i2])
                    for hg, i2 in grp:
                        def mmpv(e, hg=hg, i2=i2, nkb=nkb, blk0=blk0):
                            ins = None
                            for g in range(4):
                                h = 4 * hg + g
                                c, hf = h // 2, h % 2
                                for kb in range(nkb):
                                    ins = e.matmul(ps_oT[hf * 64:(hf + 1) * 64, c, :],
                                                   vv[:, blk0 + kb, hg * 64:(hg + 1) * 64],
                                                   pT[i2][:, g, kb, :], start=(kb == 0), stop=(kb == nkb - 1))
                            return ins

                        sc.add("tensor", mmpv, reads=["pT%d" % i2, "vv"], writes=["ps_oT%d" % hg, "po%d" % (hg // 2)])
                sc.add("scalar", lambda e: e.copy(oT[:], ps_oT[:]), reads=["ps_oT%d" % hg for hg in range(4)],
                       writes=["oT"])
                sl = n % 2
                for nh in range(2):
                    def mmo(e, nh=nh):
                        ins = None
                        for c in range(8):
                            ins = e.matmul(ps_of[:, nh * 512:(nh + 1) * 512], oT[:, c, :],
                                           wo[:, c, nh * 512:(nh + 1) * 512], start=(c == 0), stop=(c == 7))
                        return ins

                    sc.add("tensor", mmo, reads=["oT", "wo"], writes=["po%d" % nh])
                    sc.add("vector", lambda e, nh=nh, sl=sl: e.tensor_tensor(
                        out=ho[sl][:, nh * 512:(nh + 1) * 512], in0=ps_of[:, nh * 512:(nh + 1) * 512],
                        in1=hb[sl][:, nh * 512:(nh + 1) * 512], op=ALU.add),
                        reads=["po%d" % nh, "hb%d" % sl], writes=["ho%d_%d" % (sl, nh)])
                sc.dma("sync", h_out[n * 128:(n + 1) * 128, :], ho[sl][:], reads=["ho%d_0" % sl, "ho%d_1" % sl],
                       chan="ho%d" % sl)
            sc.flush()


def make_biasmask():
    slopes = 2.0 ** (-8.0 * np.arange(1, 17, dtype=np.float64) / 16.0)
    q = np.arange(128)[:, None]
    kj = np.arange(256)[None, :]
    dist = q + 128 - kj
    valid = (dist >= 0) & (dist < 128)
    bm = np.where(valid[:, None, :], -slopes[None, :, None] * dist[:, None, :], -30000.0)
    return np.ascontiguousarray(bm.astype(np.float32))


A_IN = 4112


def delta_proj_phase(sc, nc, cx, x_in, g_ap, win_ap, cw_ap, alog_ap, dtb_ap, QF, KF, KT, VT, GT, BG, ntiles=None):
    T = 256
    NT = (S // T) if ntiles is None else ntiles
    with ExitStack() as es:
        def sb(name, shape, dt):
            return es.enter_context(nc.sbuf_tensor(_uname(name), shape, dt))

        def psum(name, shape, dt):
            return es.enter_context(nc.psum_tensor(_uname(name), shape, dt))

        win = sb("win", [128, 8, A_IN], BF16)
        hb = [sb("hb%d" % i, [128, 2, D], F32) for i in range(2)]
        gbc = sb("gbc", [128, D], F32)
        junk = sb("junk", [128, D], F32)
        xn = [sb("xn%d" % i, [128, D], BF16) for i in range(2)]
        stat = sb("stat", [128, 16], F32)
        xTs = [sb("xT%d" % i, [128, 8, T + 3], BF16) for i in range(2)]
        sF = sb("sF", [128, 24, T], F32)
        yb = [sb("yb%d" % i, [128, T], F32) for i in range(4)]
        cw = sb("cw", [128, 24, 4], F32)
        sqs = [sb("sq%d" % i, [128, 2, T], F32) for i in range(2)]
        rns = [sb("rn%d" % i, [128, 2, T], F32) for i in range(2)]
        ones = sb("ones", [128, 128], F32)
        oneb = sb("oneb", [128, 1], F32)
        ot = [sb("ot%d" % i, [128, D], F32) for i in range(2)]
        go = [sb("go%d" % i, [128, D], F32) for i in range(2)]
        bg = [sb("bg%d" % i, [128, 16], F32) for i in range(2)]
        ba = sb("ba", [128, 32], F32)
        negA = sb("negA", [128, 8], F32)
        dtb = sb("dtb", [128, 8], F32)
        ps_tr = psum("ps_tr", [128, D], BF16)
        ps_p = [psum("ps_p%d" % i, [128, 512], F32) for i in range(3)]
        ps_n = psum("ps_n", [128, 2, T], F32)
        ps_T = psum("ps_T", [128, 8, 128], F32)
        ps_g = psum("ps_g", [128, 512], F32)
        ps_b = ps_n[:].rearrange("p a t -> p (a t)")

        sc.dma("sync", gbc[:], _bc(g_ap, 128), writes=["gbc"], chan="const")
        sc.dma("sync", negA[:], _bc(alog_ap, 128), writes=["negA"], chan="const")
        sc.dma("sync", dtb[:], _bc(dtb_ap, 128), writes=["dtb"], chan="const")
        sc.add("scalar", lambda e: e.activation(out=negA[:], in_=negA[:], func=AF.Exp), reads=["negA"], writes=["negA"])
        sc.add("vector", lambda e: e.tensor_scalar(negA[:], negA[:], -1.0, None, op0=ALU.mult), reads=["negA"],
               writes=["negA"])
        sc.add("vector", lambda e: e.memset(ones[:], 1.0), writes=["ones"])
        sc.add("vector", lambda e: e.memset(oneb[:], 1.0), writes=["oneb"])
        lnq = sb("lnq", [128, 1], F32)
        lnk = sb("lnk", [128, 1], F32)
        sc.add("vector", lambda e: e.memset(lnq[:], -2.4260151319598084), writes=["lnsc"])
        sc.add("vector", lambda e: e.memset(lnk[:], 0.0), writes=["lnsc"])
        sc.add("vector", lambda e: e.memset(xTs[0][:, :, 0:3], 0.0), writes=["xT0_lead"])
        load_rows_transposed(sc, es, nc, cx, "cwa", cw_ap, 4, 24, ps_p[0], "ps_p0", cw[:], "cw")
        for kc in range(8):
            for part in range(2):
                lo = part * 2056
                sc.dma("gpsimd", win[:, kc, lo:lo + 2056], win_ap[kc * 128:(kc + 1) * 128, lo:lo + 2056],
                       writes=["win"], chan="w%d" % ((kc * 2 + part) % 4))

        def load(t):
            sl = t % 2
            sc.dma("sync", hb[sl][:], x_in[t * T:(t + 1) * T, :].rearrange("(s p) d -> p s d", p=128),
                   writes=["hb%d" % sl], chan="hb%d" % sl)

        def do_norm(t):
            sl = t % 2
            xT = xTs[sl]
            if t > 0:
                sc.add("vector", lambda e, sl=sl: e.tensor_copy(xTs[sl][:, :, 0:3], xTs[1 - sl][:, :, T:T + 3]),
                       reads=["xT%d_1" % (1 - sl)], writes=["xT%d_lead" % sl])
            for s in range(2):
                norm_transpose(sc, cx, "n%d" % s, hb[sl][:, s, :], "hb%d" % sl, gbc[:], "gbc",
                               stat[:, s:s + 1], stat[:, 4 + s:5 + s], junk[:], xn[s][:],
                               xT[:, :, 3 + s * 128:3 + (s + 1) * 128], "xT%d_%d" % (sl, s), ps_tr[:], "ps_tr")

        load(0)
        nst = 0
        nout = 0
        for t in range(NT):
            sl = t % 2
            t0 = t * T
            if t + 1 < NT:
                load(t + 1)
            if t == 0:
                do_norm(0)
            xT = xTs[sl]
            XK = "xT%d" % sl
            pend = None
            for c in range(24):
                pi = nst % 3
                si = nst % 4
                nst += 1
                pst = ps_p[pi]

                def mm(e, c=c, pst=pst, xT=xT):
                    ins = None
                    for kc in range(8):
                        ins = e.matmul(pst[:, 0:T + 3], win[:, kc, c * 128:(c + 1) * 128], xT[:, kc, :],
                                       start=(kc == 0), stop=(kc == 7))
                    return ins

                sc.add("tensor", mm, reads=["win", XK + "_0", XK + "_1", XK + "_lead"], writes=["ps_p%d" % pi])
                y_ = yb[si]
                sc.add("scalar", lambda e, pst=pst, y_=y_, c=c: e.activation(
                    out=y_[:], in_=pst[:, 3:T + 3], func=AF.Copy, scale=cw[:, c, 3:4]),
                    reads=["ps_p%d" % pi, "cw"], writes=["yb%d" % si])
                for j in (2, 1, 0):
                    sc.add("vector", lambda e, pst=pst, y_=y_, c=c, j=j: e.scalar_tensor_tensor(
                        out=y_[:], in0=pst[:, j:T + j], scalar=cw[:, c, j:j + 1], in1=y_[:], op0=ALU.mult, op1=ALU.add),
                        reads=["ps_p%d" % pi, "cw", "yb%d" % si], writes=["yb%d" % si])
                if pend is not None:
                    pend()
                pend = (lambda y_=y_, c=c, si=si: sc.add(
                    "scalar", lambda e: e.activation(out=sF[:, c, :], in_=y_[:], func=AF.Silu),
                    reads=["yb%d" % si], writes=["sF%d" % c]))
                if c == 20 and t + 1 < NT:
                    do_norm(t + 1)
            pend()
            def l2_sq(c):
                li = (c // 2) % 2
                sq = sqs[li]
                sc.add("scalar", lambda e, c=c, sq=sq: e.activation(out=sq[:], in_=sF[:, c:c + 2, :], func=AF.Square),
                       reads=["sF%d" % c, "sF%d" % (c + 1)], writes=["sq%d" % li])

            def l2_mm(c):
                li = (c // 2) % 2
                sq = sqs[li]

                def mmn(e, sq=sq):
                    ins = None
                    for i in range(2):
                        ins = e.matmul(ps_n[:, i, :], ones[:], sq[:, i, :], start=True, stop=True)
                    return ins

                sc.add("tensor", mmn, reads=["sq%d" % li, "ones"], writes=["ps_n"])

            def l2_fin(c):
                li = (c // 2) % 2
                rn = rns[li]
                sc.add("scalar", lambda e, rn=rn: e.activation(out=rn[:], in_=ps_n[:], func=AF.Ln, bias=cx.epsb[:],
                                                              scale=1.0),
                       reads=["ps_n", "epsb"], writes=["rn%d" % li])
                lb = lnq if c < 8 else lnk
                sc.add("scalar", lambda e, rn=rn, lb=lb: e.activation(out=rn[:], in_=rn[:], func=AF.Exp, bias=lb[:],
                                                                     scale=-0.5),
                       reads=["rn%d" % li, "lnsc"], writes=["rn%d" % li])
                sc.add("gpsimd", lambda e, c=c, rn=rn: e.tensor_tensor(
                    out=sF[:, c:c + 2, :], in0=sF[:, c:c + 2, :], in1=rn[:], op=ALU.mult),
                    reads=["rn%d" % li, "sF%d" % c, "sF%d" % (c + 1)], writes=["sF%d" % c, "sF%d" % (c + 1)])

            l2_sq(0)
            for c in range(0, 16, 2):
                l2_mm(c)
                if c + 2 < 16:
                    l2_sq(c + 2)
                l2_fin(c)
            sc.dma("sync", QF[:, :, t0:t0 + T].rearrange("h d t -> d h t"), sF[:, 0:8, :],
                   reads=["sF%d" % c for c in range(8)], chan="qf")
            sc.dma("sync", KF[:, :, t0:t0 + T].rearrange("h d t -> d h t"), sF[:, 8:16, :],
                   reads=["sF%d" % c for c in range(8, 16)], chan="kf")
            for s in range(2):
                for (base, dst, nm) in ((8, KT, "k"), (16, VT, "v")):
                    oi = nout % 2
                    nout += 1

                    def trT(e, base=base, s=s):
                        ins = None
                        for h in range(8):
                            ins = e.transpose(ps_T[:, h, :], sF[:, base + h, s * 128:(s + 1) * 128], cx.ident[:])
                        return ins

                    sc.add("tensor", trT, reads=["sF%d" % (base + h) for h in range(8)] + ["ident"], writes=["ps_T"])
                    sc.add("scalar", lambda e, oi=oi: e.copy(ot[oi][:], ps_T[:].rearrange("p h d -> p (h d)")),
                           reads=["ps_T"], writes=["ot%d" % oi])
                    sc.dma("sync", dst[t0 + s * 128:t0 + (s + 1) * 128, :], ot[oi][:], reads=["ot%d" % oi],
                           chan="ot%d" % oi)
            for s in range(2):
                gi = s
                for nh in range(2):
                    def mmg(e, s=s, nh=nh, xT=xT):
                        ins = None
                        for kc in range(8):
                            ins = e.matmul(ps_g[:], xT[:, kc, 3 + s * 128:3 + (s + 1) * 128],
                                           win[:, kc, 3072 + nh * 512:3072 + (nh + 1) * 512],
                                           start=(kc == 0), stop=(kc == 7))
                        return ins

                    sc.add("tensor", mmg, reads=["win", XK + "_%d" % s], writes=["ps_g"])
                    sc.add("scalar", lambda e, gi=gi, nh=nh: e.activation(out=go[gi][:, nh * 512:(nh + 1) * 512],
                                                                         in_=ps_g[:], func=AF.Silu),
                           reads=["ps_g"], writes=["go%d_%d" % (gi, nh)])
                sc.dma("sync", GT[t0 + s * 128:t0 + (s + 1) * 128, :], go[gi][:], reads=["go%d_0" % gi, "go%d_1" % gi],
                       chan="go%d" % gi)

                def mmb(e, s=s, xT=xT):
                    ins = None
                    for kc in range(8):
                        ins = e.matmul(ps_b[:, 0:16], xT[:, kc, 3 + s * 128:3 + (s + 1) * 128], win[:, kc, 4096:4112],
                                       start=(kc == 0), stop=(kc == 7))
                    return ins

                sc.add("tensor", mmb, reads=["win", XK + "_%d" % s], writes=["ps_n"])
                bg_ = bg[s]
                sc.add("scalar", lambda e, bg_=bg_: e.activation(out=bg_[:, 0:8], in_=ps_b[:, 0:8], func=AF.Sigmoid),
                       reads=["ps_n"], writes=["bg%d_b" % s])
                sc.add("vector", lambda e: e.tensor_tensor(out=ba[:, 0:8], in0=ps_b[:, 8:16], in1=dtb[:], op=ALU.add),
                       reads=["ps_n", "dtb", "bg%d_b" % s], writes=["ba0"])
                sc.add("scalar", lambda e: e.activation(out=ba[:, 8:16], in_=ba[:, 0:8], func=AF.Exp),
                       reads=["ba0"], writes=["ba1"])
                sc.add("scalar", lambda e: e.activation(out=ba[:, 16:24], in_=ba[:, 8:16], func=AF.Ln, bias=oneb[:],
                                                        scale=1.0),
                       reads=["ba1", "oneb"], writes=["ba2"])
                sc.add("vector", lambda e, bg_=bg_: e.tensor_tensor(out=bg_[:, 8:16], in0=ba[:, 16:24], in1=negA[:],
                                                                  op=ALU.mult),
                       reads=["ba2", "negA"], writes=["bg%d_g" % s])
                sc.dma("sync", BG[t0 + s * 128:t0 + (s + 1) * 128, :], bg_[:], reads=["bg%d_b" % s, "bg%d_g" % s],
                       chan="bg%d" % s)
        sc.flush()


def make_delta_consts():
    j = np.arange(64)[:, None]
    i = np.arange(64)[None, :]
    c = np.zeros((64, 3, 64), np.float32)
    c[:, 0, :] = (j <= i)
    c[:, 1, :] = np.where(i <= j, 0.0, -30000.0)
    c[:, 2, :] = (i < j)
    return c


def delta_phase(sc, nc, cx, x_in, h_out, QF, KF, KT, VT, GT, BG, og_ap, wout_ap, cst_ap, nchunks=None):
    NCHK = (S // 64) if nchunks is None else nchunks
    with ExitStack() as es:
        def sb(name, shape, dt):
            return es.enter_context(nc.sbuf_tensor(_uname(name), shape, dt))

        def psum(name, shape, dt):
            return es.enter_context(nc.psum_tensor(_uname(name), shape, dt))

        A = sc.add
        wout = sb("wout", [128, 8, D], BF16)
        cst = sb("cst", [128, 3, 64], F32)
        onesP = sb("onesP", [128, 128], F32)
        ogb = sb("ogb", [64, 128], F32)
        Sst = sb("Sst", [128, 8, 128], F32)
        Sb = sb("Sb", [128, 8, 128], BF16)
        Stmp = sb("Stmp", [128, 8, 128], F32)
        qFs = [sb("qFs%d" % i, [128, 8, 256], F32) for i in range(2)]
        kFs = [sb("kFs%d" % i, [128, 8, 256], F32) for i in range(2)]
        kTok = [sb("kTok%d" % i, [64, 8, 128], F32) for i in range(2)]
        vTok = [sb("vTok%d" % i, [64, 8, 128], F32) for i in range(2)]
        gTk = [sb("gTk%d" % i, [64, 8, 128], F32) for i in range(2)]
        xres = [sb("xres%d" % i, [128, D], F32) for i in range(2)]
        bgt = [sb("bgt%d" % i, [128, 16], F32) for i in range(2)]
        smM = [sb("sm%d" % i, [128, 64], F32) for i in range(2)]
        dGCM = [sb("dGC%d" % i, [128, 8, 64], F32) for i in range(2)]
        eRM = [sb("eR%d" % i, [128, 8, 64], F32) for i in range(2)]
        qdecM = [sb("qdec%d" % i, [128, 8, 64], BF16) for i in range(2)]
        DtM = [sb("Dt%d" % i, [64, 8, 64], F32) for i in range(2)]
        decM = [sb("dec%d" % i, [64, 8, 64], F32) for i in range(2)]
        t1M = [sb("t1%d" % i, [64, 8, 64], F32) for i in range(2)]
        attnM = [sb("attn%d" % i, [128, 8, 64], F32) for i in range(2)]
        PmM = [[sb("Pm%d_%d" % (m, i), [128, 8, 64], F32) for i in range(2)] for m in range(2)]
        PTmM = [[sb("PTm%d_%d" % (m, i), [128, 8, 64], F32) for i in range(2)] for m in range(2)]
        TTmM = [[sb("TTm%d_%d" % (m, i), [128, 8, 64], F32) for i in range(2)] for m in range(2)]
        TT16M = [sb("TT16%d" % i, [128, 8, 64], BF16) for i in range(2)]
        aT16M = [sb("aT16%d" % i, [128, 8, 64], BF16) for i in range(2)]
        VBbM = [sb("VBb%d" % i, [128, 8, 128], BF16) for i in range(2)]
        RKbM = [sb("RKb%d" % i, [128, 8, 128], BF16) for i in range(2)]
        KDbM = [sb("KDb%d" % i, [128, 8, 128], BF16) for i in range(2)]
        ggM = [sb("gg%d" % i, [64, 8, 128], F32) for i in range(2)]
        vnb = sb("vnb", [128, 8, 128], BF16)
        nk16 = sb("nk16", [128, 8, 64], BF16)
        osq = sb("osq", [64, 8, 128], F32)
        o1 = sb("o1", [128, 8, 128], F32)
        smc = sb("smc", [64, 8], F32)
        oT = sb("oT", [128, 8, 128], BF16)
        ho = sb("ho", [128, D], F32)
        pP = [psum("pP%d" % i, [128, 1024], F32) for i in range(4)]

        def v8(ap):
            return ap.rearrange("p (h j) -> p h j", h=8)

        kcd_ps = v8(pP[3][:, 512:1024])
        vn_ps = v8(pP[0][0:64, :])
        o_ps = v8(pP[1][0:64, :])
        Sn_ps = v8(pP[2][:, :])
        oT_ps = v8(pP[3][:, 0:512])
        out_ps = [pP[0][:, 0:512], pP[0][:, 512:1024]]

        for kc in range(8):
            sc.dma("gpsimd", wout[:, kc, :], wout_ap[kc * 128:(kc + 1) * 128, :], writes=["wout"], chan="w%d" % (kc % 4))
        A("vector", lambda e: e.memset(cst[:], 0.0), writes=["cst"])
        sc.dma("sync", cst[0:64, :, :], cst_ap, reads=[], writes=["cst"], chan="const")
        sc.dma("sync", ogb[:], _bc(og_ap, 64), writes=["ogb"], chan="const")
        A("vector", lambda e: e.memset(onesP[:], 0.0), writes=["onesP"])
        A("vector", lambda e: e.memset(onesP[0:64, :], 1.0), writes=["onesP"])
        A("vector", lambda e: e.memset(Sst[:], 0.0), writes=["Sst"])
        A("vector", lambda e: e.memset(Sb[:], 0.0), writes=["Sb"])
        zlist = []
        for m in range(2):
            zlist += [(PmM[m][0], "Pm%d_0" % m), (PmM[m][1], "Pm%d_1" % m), (PTmM[m][0], "PTm%d_0" % m),
                      (PTmM[m][1], "PTm%d_1" % m), (TTmM[m][0], "TTm%d_0" % m), (TTmM[m][1], "TTm%d_1" % m),
                      (TT16M[m], "TT16%d" % m), (aT16M[m], "aT16%d" % m), (VBbM[m], "VBb%d" % m),
                      (RKbM[m], "RKb%d" % m), (KDbM[m], "KDb%d" % m), (dGCM[m], "dGC%d" % m), (bgt[m], "bgt%d" % m),
                      (attnM[m], "attn%d" % m)]
        zlist += [(vnb, "vnb"), (o1, "o1")]
        zkeys = {}
        for i, (tl, nme) in enumerate(zlist):
            A("gpsimd", lambda e, tl=tl: e.memset(tl[:], 0.0), writes=["z%d" % i])
            zkeys[nme] = "z%d" % i

        U = cst[:, 0, :]
        mbi = cst[0:64, 1, :]
        st01 = cst[0:64, 2, :]
        id64 = cx.ident[0:64, 0:64]

        def bc_h(ap2d, n):
            return ap2d.unsqueeze(1).to_broadcast([n, 8, ap2d.shape[-1]])

        def bc_f(ap2d, n, f):
            return ap2d.unsqueeze(2).to_broadcast([n, 8, f])

        def load_super(st):
            b = st % 2
            t0 = st * 256
            sc.dma("sync", qFs[b][:], QF[:, :, t0:t0 + 256].rearrange("h d t -> d h t"), writes=["qFs%d" % b],
                   chan="qFs%d" % b)
            sc.dma("sync", kFs[b][:], KF[:, :, t0:t0 + 256].rearrange("h d t -> d h t"), writes=["kFs%d" % b],
                   chan="kFs%d" % b)

        def load_chunk(ch):
            b = ch % 2
            t0 = ch * 64
            sc.dma("sync", kTok[b][:], KT[t0:t0 + 64, :].rearrange("t (h d) -> t h d", h=8), writes=["kTok%d" % b],
                   chan="kTok%d" % b)
            sc.dma("sync", vTok[b][:], VT[t0:t0 + 64, :].rearrange("t (h d) -> t h d", h=8), writes=["vTok%d" % b],
                   chan="vTok%d" % b)
            sc.dma("sync", gTk[b][:], GT[t0:t0 + 64, :].rearrange("t (h d) -> t h d", h=8), writes=["gTk%d" % b],
                   chan="gTk%d" % b)
            sc.dma("sync", bgt[b][0:64, :], BG[t0:t0 + 64, :], reads=[zkeys["bgt%d" % b]], writes=["bgt%d" % b],
                   chan="bgt%d" % b)

        def load_xres(pr):
            b = (pr // 2) % 2
            t0 = pr * 64
            sc.dma("sync", xres[b][:], x_in[t0:t0 + 128, :], writes=["xres%d" % b], chan="xres%d" % b)

        def pre(ch, m):
            st = ch // 4
            sbi = st % 2
            tl = (ch % 4) * 64
            qF = qFs[sbi][:, :, tl:tl + 64]
            kF = kFs[sbi][:, :, tl:tl + 64]
            kq = ["qFs%d" % sbi, "kFs%d" % sbi]
            bg_ = bgt[m]
            beta = bg_[0:64, 0:8]
            BGK = "bgt%d" % m
            sm, dGC, eR, qdec, Dt, dec, t1, attn = smM[m], dGCM[m], eRM[m], qdecM[m], DtM[m], decM[m], t1M[m], attnM[m]
            Pm, PTm, TTm = PmM[m], PTmM[m], TTmM[m]
            TT16, aT16, VBb, RKb, KDb, gg = TT16M[m], aT16M[m], VBbM[m], RKbM[m], KDbM[m], ggM[m]
            X = pP[2 * m][:, 0:512]
            psA = X[:, 0:16]
            Tup_ps = v8(pP[2 * m][0:64, 0:512])
            R_ps = v8(pP[2 * m][:, 512:1024])
            KK_ps = v8(pP[2 * m + 1][0:64, 0:512])
            QK_ps = v8(pP[2 * m + 1][0:64, 512:1024])
            bX, bR, bK, bQ = "b%d" % (4 * m), "b%d" % (4 * m + 1), "b%d" % (4 * m + 2), "b%d" % (4 * m + 3)
            M = str(m)

            def mmKK(e):
                ins = None
                for h in range(8):
                    ins = e.matmul(KK_ps[:, h, :], kF[:, h, :], kF[:, h, :], start=True, stop=True)
                for h in range(8):
                    ins = e.matmul(QK_ps[:, h, :], qF[:, h, :], kF[:, h, :], start=True, stop=True)
                return ins

            def mm1(e):
                e.matmul(psA[0:64, 0:8], U, bg_[:, 8:16], start=True, stop=True)
                return e.matmul(psA[:, 8:16], onesP[:], bg_[:, 8:16], start=True, stop=True)

            A("tensor", mm1, reads=[BGK, "cst", "onesP"], writes=[bX])
            A("tensor", mmKK, reads=kq, writes=[bK, bQ])
            yield
            A("vector", lambda e: e.tensor_copy(sm[0:64, 0:8], psA[0:64, 0:8]), reads=[bX], writes=["sm_g" + M])
            A("vector", lambda e: e.tensor_copy(sm[:, 8:16], psA[:, 8:16]), reads=[bX], writes=["sm_g2" + M])
            yield
            A("scalar", lambda e: e.activation(out=sm[0:64, 16:24], in_=sm[0:64, 0:8], func=AF.Exp), reads=["sm_g" + M],
              writes=["sm_e" + M])
            A("scalar", lambda e: e.activation(out=sm[:, 24:32], in_=sm[:, 8:16], func=AF.Exp), reads=["sm_g2" + M],
              writes=["sm_e2" + M])
            gc = sm[0:64, 0:8]
            A("vector", lambda e: e.tensor_tensor(out=dGC[0:64], in0=bc_h(id64, 64), in1=bc_f(gc, 64, 64), op=ALU.mult),
              reads=["sm_g" + M, "ident", zkeys["dGC" + M]], writes=["dGC" + M])
            yield
            A("tensor", lambda e: e.matmul(R_ps, onesP[:], dGC[:], start=True, stop=True), reads=["dGC" + M, "onesP"],
              writes=[bR])
            A("vector", lambda e: e.tensor_tensor(out=sm[0:64, 32:40], in0=sm[0:64, 8:16], in1=sm[0:64, 0:8],
                                                  op=ALU.subtract), reads=["sm_g" + M, "sm_g2" + M], writes=["sm_d" + M])
            A("vector", lambda e: e.tensor_tensor(out=sm[0:64, 48:56], in0=beta, in1=sm[0:64, 16:24], op=ALU.mult),
              reads=[BGK, "sm_e" + M], writes=["sm_b" + M])
            yield
            A("scalar", lambda e: e.activation(out=sm[0:64, 40:48], in_=sm[0:64, 32:40], func=AF.Exp),
              reads=["sm_d" + M], writes=["sm_k" + M])
            A("scalar", lambda e: e.activation(out=eR[:], in_=R_ps, func=AF.Exp), reads=[bR], writes=["eR" + M])
            yield
            A("vector", lambda e: e.tensor_tensor(out=Dt[:], in0=bc_f(gc, 64, 64), in1=R_ps[0:64], op=ALU.subtract),
              reads=[bR, "sm_g" + M, "eR" + M], writes=["Dt" + M])
            A("vector", lambda e: e.scalar_tensor_tensor(out=Dt[:], in0=Dt[:], scalar=0.0, in1=bc_h(mbi, 64),
                                                         op0=ALU.min, op1=ALU.add), reads=["Dt" + M, "cst"],
              writes=["Dt" + M])
            yield
            A("scalar", lambda e: e.activation(out=dec[:], in_=Dt[:], func=AF.Exp), reads=["Dt" + M], writes=["dec" + M])
            A("vector", lambda e: e.tensor_tensor(out=qdec[:], in0=qF, in1=eR[:], op=ALU.mult),
              reads=["eR" + M, kq[0]], writes=["qdec" + M])
            yield
            A("gpsimd", lambda e: e.tensor_tensor(out=t1[:], in0=dec[:], in1=bc_h(st01, 64), op=ALU.mult),
              reads=["dec" + M, "cst"], writes=["t1" + M])
            A("gpsimd", lambda e: e.tensor_tensor(out=t1[:], in0=t1[:], in1=bc_f(beta, 64, 64), op=ALU.mult),
              reads=["t1" + M, BGK], writes=["t1" + M])
            A("vector", lambda e: e.tensor_tensor(out=attn[0:64], in0=QK_ps, in1=dec[:], op=ALU.mult),
              reads=[bQ, "dec" + M, zkeys["attn" + M]], writes=["attn" + M])
            yield
            L = Pm[0]
            A("vector", lambda e: e.tensor_tensor(out=L[0:64], in0=KK_ps, in1=t1[:], op=ALU.mult),
              reads=[bK, "t1" + M, zkeys["Pm%d_0" % m]], writes=["Pm%d_0" % m])
            yield
            LT_ps, aT_ps = KK_ps, QK_ps

            def trL(e):
                ins = None
                for h in range(8):
                    ins = e.matmul(LT_ps[:, h, :], L[:, h, :], cx.ident[:, 0:64], start=True, stop=True)
                for h in range(8):
                    ins = e.matmul(aT_ps[:, h, :], attn[:, h, :], cx.ident[:, 0:64], start=True, stop=True)
                return ins

            A("tensor", trL, reads=["Pm%d_0" % m, "attn" + M, "ident"], writes=[bK, bQ])
            yield
            A("scalar", lambda e: e.copy(PTm[0][0:64], LT_ps), reads=[bK, zkeys["PTm%d_0" % m]], writes=["PTm%d_0" % m])
            yield
            A("vector", lambda e: e.tensor_tensor(out=TTm[0][0:64], in0=bc_h(id64, 64), in1=PTm[0][0:64],
                                                  op=ALU.subtract),
              reads=["PTm%d_0" % m, "ident", zkeys["TTm%d_0" % m]], writes=["TTm%d_0" % m])
            A("scalar", lambda e: e.copy(aT16[0:64], aT_ps), reads=[bQ, zkeys["aT16" + M]], writes=["aT16" + M])
            yield
            P2_ps, PT2_ps = KK_ps, QK_ps
            ci = 0
            for lvl in range(5):
                P, PT, Tc = Pm[ci], PTm[ci], TTm[ci]
                Pn, PTn, Tn = Pm[1 - ci], PTm[1 - ci], TTm[1 - ci]
                kP, kPT, kT_ = "Pm%d_%d" % (m, ci), "PTm%d_%d" % (m, ci), "TTm%d_%d" % (m, ci)
                kPn, kPTn, kTn = "Pm%d_%d" % (m, 1 - ci), "PTm%d_%d" % (m, 1 - ci), "TTm%d_%d" % (m, 1 - ci)
                last = lvl == 4

                def mmsq(e, P=P, PT=PT, last=last):
                    ins = None
                    for h in range(8):
                        ins = e.matmul(P2_ps[:, h, :], PT[:, h, :], P[:, h, :], start=True, stop=True)
                    if not last:
                        for h in range(8):
                            ins = e.matmul(PT2_ps[:, h, :], P[:, h, :], PT[:, h, :], start=True, stop=True)
                    return ins

                A("tensor", mmsq, reads=[kP, kPT], writes=[bK, bQ])
                yield
                A("scalar", lambda e, Pn=Pn: e.copy(Pn[0:64], P2_ps), reads=[bK, zkeys[kPn]], writes=[kPn])
                if not last:
                    A("vector", lambda e, PTn=PTn: e.tensor_copy(PTn[0:64], PT2_ps), reads=[bQ, zkeys[kPTn]],
                      writes=[kPTn])
                yield

                def mmT(e, Pn=Pn, Tc=Tc):
                    ins = None
                    for h in range(8):
                        ins = e.matmul(Tup_ps[:, h, :], Pn[:, h, :], Tc[:, h, :], start=True, stop=True)
                    return ins

                A("tensor", mmT, reads=[kPn, kT_], writes=[bX])
                yield
                A("vector", lambda e, Tn=Tn, Tc=Tc: e.tensor_tensor(out=Tn[0:64], in0=Tc[0:64], in1=Tup_ps, op=ALU.add),
                  reads=[bX, kT_, zkeys[kTn]], writes=[kTn])
                yield
                ci = 1 - ci
            Tfin = TTm[ci]
            kTf = "TTm%d_%d" % (m, ci)
            A("scalar", lambda e: e.copy(TT16[0:64], Tfin[0:64]), reads=[kTf, zkeys["TT16" + M]], writes=["TT16" + M])
            kT_b, vT_b = kTok[m], vTok[m]
            A("gpsimd", lambda e: e.tensor_tensor(out=VBb[0:64], in0=vT_b[:], in1=bc_f(beta, 64, 128), op=ALU.mult),
              reads=["vTok%d" % m, BGK, zkeys["VBb" + M]], writes=["VBb" + M])
            A("gpsimd", lambda e: e.tensor_tensor(out=RKb[0:64], in0=kT_b[:], in1=bc_f(sm[0:64, 48:56], 64, 128),
                                                  op=ALU.mult),
              reads=["kTok%d" % m, "sm_b" + M, zkeys["RKb" + M]], writes=["RKb" + M])
            A("gpsimd", lambda e: e.tensor_tensor(out=KDb[0:64], in0=kT_b[:], in1=bc_f(sm[0:64, 40:48], 64, 128),
                                                  op=ALU.mult),
              reads=["kTok%d" % m, "sm_k" + M, zkeys["KDb" + M]], writes=["KDb" + M])
            A("gpsimd", lambda e: e.tensor_tensor(out=gg[:], in0=gTk[m][:], in1=bc_h(ogb[:], 64), op=ALU.mult),
              reads=["gTk%d" % m, "ogb"], writes=["gg" + M])
            yield

        def chain(ch, m):
            sm = smM[m]
            M = str(m)
            TT16, aT16, VBb, RKb, KDb, gg, qdec = TT16M[m], aT16M[m], VBbM[m], RKbM[m], KDbM[m], ggM[m], qdecM[m]
            A("gpsimd", lambda e: e.tensor_tensor(out=Stmp[:], in0=Sst[:], in1=bc_f(sm[:, 24:32], 128, 128), op=ALU.mult),
              reads=["Sst", "sm_e2" + M], writes=["Stmp"])

            def mmkcd(e):
                ins = None
                for h in range(8):
                    ins = e.matmul(kcd_ps[:, h, :], RKb[:, h, :], TT16[:, h, :], start=True, stop=True)
                return ins

            A("tensor", mmkcd, reads=["RKb" + M, "TT16" + M], writes=["b7"])
            A("scalar", lambda e: e.activation(out=nk16[:], in_=kcd_ps, func=AF.Copy, scale=-1.0), reads=["b7"],
              writes=["nk16"])

            def mmvn(e):
                ins = None
                for h in range(8):
                    e.matmul(vn_ps[:, h, :], TT16[:, h, :], VBb[:, h, :], start=True, stop=False)
                    ins = e.matmul(vn_ps[:, h, :], nk16[:, h, :], Sb[:, h, :], start=False, stop=True)
                return ins

            A("tensor", mmvn, reads=["TT16" + M, "VBb" + M, "nk16", "Sb"], writes=["b0", "b1"])
            A("vector", lambda e: e.tensor_copy(vnb[0:64], vn_ps), reads=["b0", "b1", zkeys["vnb"]], writes=["vnb"])

            def mmo(e):
                ins = None
                for h in range(8):
                    e.matmul(o_ps[:, h, :], qdec[:, h, :], Sb[:, h, :], start=True, stop=False)
                    ins = e.matmul(o_ps[:, h, :], aT16[:, h, :], vnb[:, h, :], start=False, stop=True)
                return ins

            A("tensor", mmo, reads=["qdec" + M, "Sb", "aT16" + M, "vnb"], writes=["b2", "b3"])

            def mmS(e):
                ins = None
                for h in range(8):
                    ins = e.matmul(Sn_ps[:, h, :], KDb[:, h, :], vnb[:, h, :], start=True, stop=True)
                return ins

            A("tensor", mmS, reads=["KDb" + M, "vnb"], writes=["b4", "b5"])
            A("vector", lambda e: e.tensor_tensor(out=Sst[:], in0=Stmp[:], in1=Sn_ps, op=ALU.add),
              reads=["Stmp", "b4", "b5"], writes=["Sst"])
            A("scalar", lambda e: e.copy(Sb[:], Sst[:]), reads=["Sst"], writes=["Sb"])
            A("scalar", lambda e: e.activation(out=osq[:], in_=o_ps, func=AF.Square), reads=["b2", "b3"], writes=["osq"])
            A("vector", lambda e: e.tensor_reduce(out=smc[:], in_=osq[:], axis=AX.X, op=ALU.add),
              reads=["osq"], writes=["sm_o"])
            A("scalar", lambda e: e.activation(out=smc[:], in_=smc[:], func=AF.Sqrt, bias=cx.epsb[0:64, :],
                                               scale=1.0 / 128), reads=["sm_o", "epsb"], writes=["sm_o"])
            A("vector", lambda e: e.reciprocal(smc[:], smc[:]), reads=["sm_o"], writes=["sm_o"])
            A("vector", lambda e: e.tensor_tensor(out=o1[0:64], in0=o_ps, in1=bc_f(smc[:], 64, 128), op=ALU.mult),
              reads=["b2", "b3", "sm_o", zkeys["o1"]], writes=["o1"])
            A("vector", lambda e: e.tensor_tensor(out=o1[0:64], in0=o1[0:64], in1=gg[:], op=ALU.mult),
              reads=["o1", "gg" + M], writes=["o1"])

            def trO(e):
                ins = None
                for c in range(8):
                    ins = e.matmul(oT_ps[:, c, :], o1[:, c, :], cx.ident[:, 0:64], start=True, stop=True)
                return ins

            A("tensor", trO, reads=["o1", "ident"], writes=["b6"])
            A("scalar", lambda e: e.copy(oT[:, :, m * 64:(m + 1) * 64], oT_ps), reads=["b6"], writes=["oT%d" % m])

        def out_proj(pr):
            xb = (pr // 2) % 2
            for nh in range(2):
                def mmout(e, nh=nh):
                    ins = None
                    for c in range(8):
                        ins = e.matmul(out_ps[nh], oT[:, c, :], wout[:, c, nh * 512:(nh + 1) * 512],
                                       start=(c == 0), stop=(c == 7))
                    return ins

                A("tensor", mmout, reads=["oT0", "oT1", "wout"], writes=["b%d" % nh])
                A("vector", lambda e, nh=nh: e.tensor_tensor(
                    out=ho[:, nh * 512:(nh + 1) * 512], in0=out_ps[nh], in1=xres[xb][:, nh * 512:(nh + 1) * 512],
                    op=ALU.add), reads=["b%d" % nh, "xres%d" % xb], writes=["ho_%d" % nh])
            sc.dma("sync", h_out[pr * 64:pr * 64 + 128, :], ho[:], reads=["ho_0", "ho_1"], chan="ho")

        assert NCHK % 2 == 0
        load_super(0)
        for c0 in range(2):
            load_chunk(c0)
        load_xres(0)
        for pr in range(0, NCHK, 2):
            chs = [pr, pr + 1]
            st = pr // 4
            if pr % 4 == 0 and (st + 1) * 4 < NCHK:
                load_super(st + 1)
            gens = [pre(c, c % 2) for c in chs]
            alive = list(gens)
            while alive:
                nxt = []
                for g in alive:
                    try:
                        next(g)
                        nxt.append(g)
                    except StopIteration:
                        pass
                alive = nxt
            for c in chs:
                if c + 2 < NCHK:
                    load_chunk(c + 2)
            if pr + 2 < NCHK:
                load_xres(pr + 2)
            for c in chs:
                chain(c, c % 2)
            out_proj(pr)
        sc.flush()


_W_SHAPES = {
    "a_norm": [1, D], "a_w_in": [1, D, A_IN], "a_conv_w": [1, 4, 3072], "a_A_log": [1, 8], "a_dt_bias": [1, 8],
    "a_onorm": [1, 128], "a_w_out": [1, D, D], "kv_norm": [D], "kv_w": [D, 512], "kv_b": [512],
    "b_norm": [1, D], "b_w_q": [1, D, D], "b_b_q": [1, D], "b_sinks": [1, 16], "b_w_o": [1, D, D], "b_b_o": [1, D],
    "f_norm": [2, D], "f_w_up": [2, D, 2 * FF], "f_conv_w": [2, 3, 2 * FF], "f_conv_b": [2, 2 * FF],
    "f_w_down": [2, FF, D], "final_norm": [D],
}


def build_program():
    nc = bass.Bass("TRN2", target_bir_lowering=False)

    def din(name, shape):
        return nc.dram_tensor(name, shape, F32, kind="ExternalInput").ap()

    def dint(name, shape):
        return nc.dram_tensor(name, shape, F32, kind="Internal").ap()

    x = din("x", [S, D])
    w = {k: din(k, shp) for k, shp in _W_SHAPES.items()}
    ident = din("c_ident", [128, 128])
    bm = din("c_bm", [128, 16, 256])
    cst = din("c_delta", [64, 3, 64])
    out = nc.dram_tensor("out", [S, D], F32, kind="ExternalOutput").ap()
    QF = dint("s_QF", [8, 128, S])
    KF = dint("s_KF", [8, 128, S])
    KT = dint("s_KT", [S, D])
    VT = dint("s_VT", [S, D])
    GT = dint("s_GT", [S, D])
    BG = dint("s_BG", [S, 16])
    h1 = dint("s_h1", [S, D])
    h2 = dint("s_h2", [S, D])
    h3 = dint("s_h3", [S, D])
    with ExitStack() as es:
        block = es.enter_context(nc.Block())
        sc = Sched(nc, block, es)
        cx = Ctx()
        load_consts(sc, es, nc, cx, ident)
        delta_proj_phase(sc, nc, cx, x, w["a_norm"][0], w["a_w_in"][0], w["a_conv_w"][0], w["a_A_log"][0],
                         w["a_dt_bias"][0], QF, KF, KT, VT, GT, BG)
        delta_phase(sc, nc, cx, x, h1, QF, KF, KT, VT, GT, BG, w["a_onorm"][0], w["a_w_out"][0], cst)
        ffn_phase(sc, nc, cx, h1, h2, w["f_norm"][0], w["f_w_up"][0], w["f_conv_w"][0], w["f_conv_b"][0],
                  w["f_w_down"][0])
        attn_phase(sc, nc, cx, h2, h3, w["kv_norm"], w["kv_w"], w["kv_b"], w["b_norm"][0], w["b_w_q"][0],
                   w["b_b_q"][0], w["b_sinks"][0], w["b_w_o"][0], w["b_b_o"][0], bm)
        ffn_phase(sc, nc, cx, h3, out, w["f_norm"][1], w["f_w_up"][1], w["f_conv_w"][1], w["f_conv_b"][1],
                  w["f_w_down"][1], fin_ap=w["final_norm"])
    return nc


def kernel(**inputs):
    x = np.ascontiguousarray(np.asarray(inputs["x"], dtype=np.float32))
    consts = {
        "c_ident": np.eye(128, dtype=np.float32),
        "c_bm": make_biasmask(),
        "c_delta": make_delta_consts(),
    }
    shared = {k: np.ascontiguousarray(np.asarray(inputs[k], dtype=np.float32)) for k in _W_SHAPES}
    shared.update(consts)
    nc = build_program()
    in_maps = []
    for b in range(NB):
        m = dict(shared)
        m["x"] = np.ascontiguousarray(x[b])
        in_maps.append(m)
    res = run_bass_kernel_spmd(nc, in_maps, core_ids=list(range(NB)))
    return np.stack([np.asarray(r["out"], dtype=np.float32) for r in res.results], axis=0)
```

```python
import numpy as np
from contextlib import ExitStack
import concourse.bass as bass
import concourse.mybir as mybir
from concourse.bass_utils import run_bass_kernel_spmd

F32 = mybir.dt.float32
BF16 = mybir.dt.bfloat16
AF = mybir.ActivationFunctionType
ALU = mybir.AluOpType
AX = mybir.AxisListType

D = 1024
S = 4096
NB = 8
FF = 2816
EPS = 1e-6
ENGS = ["sync", "scalar", "vector", "gpsimd", "tensor"]


class Op:
    __slots__ = ("eng", "fn", "deps", "needed", "idx", "chan", "chan_val")

    def __init__(self, eng, fn):
        self.eng = eng
        self.fn = fn
        self.deps = []
        self.needed = False
        self.idx = None
        self.chan = None
        self.chan_val = None


class Sched:
    def __init__(self, nc, block, es):
        self.nc = nc
        self.block = block
        self.es = es
        self.sem = {e: es.enter_context(nc.semaphore("s_" + e)) for e in ENGS}
        self.count = {e: 0 for e in ENGS}
        self.chans = {}
        self.waited = {e: {} for e in ENGS}
        self._reset()

    def _reset(self):
        self.ops = {e: [] for e in ENGS}
        self.last_w = {}
        self.readers = {}

    def chan(self, name):
        if name not in self.chans:
            self.chans[name] = [self.es.enter_context(self.nc.semaphore("c_" + name)), 0, None]
        return self.chans[name]

    def add(self, eng, fn, reads=(), writes=(), chan=None):
        op = Op(eng, fn)
        deps = []
        for r in reads:
            w = self.last_w.get(r)
            if w is not None:
                deps.append(w)
        for w_ in writes:
            w = self.last_w.get(w_)
            if w is not None:
                deps.append(w)
            deps.extend(self.readers.get(w_, ()))
        if chan is not None:
            ch = self.chan(chan)
            if ch[2] is not None:
                deps.append(ch[2])
            ch[1] += 16
            ch[2] = op
            op.chan = ch
            op.chan_val = ch[1]
        seen = set()
        for d in deps:
            if id(d) in seen or d is op:
                continue
            seen.add(id(d))
            if d.eng == "tensor" and eng == "tensor" and d.chan is None:
                continue
            d.needed = True
            op.deps.append(d)
        for r in reads:
            self.readers.setdefault(r, []).append(op)
        for w_ in writes:
            self.last_w[w_] = op
            self.readers[w_] = []
        self.ops[eng].append(op)
        return op

    def dma(self, eng, out, in_, reads=(), writes=(), chan=None, **kw):
        assert chan is not None
        return self.add(eng, lambda e: e.dma_start(out=out, in_=in_, **kw), reads, writes, chan=chan)

    def flush(self):
        for e in ENGS:
            comp = [op for op in self.ops[e] if op.chan is None]
            if comp:
                comp[-1].needed = True
            for op in self.ops[e]:
                if op.chan is None and op.needed:
                    self.count[e] += 1
                    op.idx = self.count[e]
        finals = [(self.sem[e], self.count[e]) for e in ENGS if self.count[e] > 0]
        finals += [(ch[0], ch[1]) for ch in self.chans.values() if ch[1] > 0]

        def emit(ename):
            ops = self.ops[ename]
            waited = self.waited[ename]

            def body(eng):
                for op in ops:
                    for d in op.deps:
                        if d.chan is not None:
                            sem, val = d.chan[0], d.chan_val
                        else:
                            sem, val = self.sem[d.eng], d.idx
                        if waited.get(sem.num if hasattr(sem, "num") else id(sem), 0) < val:
                            eng.wait_ge(sem, val)
                            waited[sem.num if hasattr(sem, "num") else id(sem)] = val
                    ins = op.fn(eng)
                    if op.chan is not None:
                        ins.then_inc(op.chan[0], 16)
                    elif op.needed:
                        ins.then_inc(self.sem[ename], 1)
                for sem, val in finals:
                    key = sem.num if hasattr(sem, "num") else id(sem)
                    if waited.get(key, 0) < val:
                        eng.wait_ge(sem, val)
                        waited[key] = val

            getattr(self.block, ename)(body)

        for e in ENGS:
            emit(e)
        self._reset()


_UNIQ = [0]
_DBG = [0, 0]


def _uname(name):
    _UNIQ[0] += 1
    return "t%d_%s" % (_UNIQ[0], name)


def _bc(ap, n):
    return ap.partition_broadcast(n)


class Ctx:
    pass


def load_consts(sc, es, nc, cx, ident_ap):
    cx.ident = es.enter_context(nc.sbuf_tensor("sb_ident", [128, 128], F32))
    cx.identb = es.enter_context(nc.sbuf_tensor("sb_identb", [128, 128], BF16))
    cx.epsb = es.enter_context(nc.sbuf_tensor("sb_epsb", [128, 1], F32))
    sc.dma("sync", cx.ident[:], ident_ap, writes=["ident"], chan="const")
    sc.add("vector", lambda e: e.tensor_copy(cx.identb[:], cx.ident[:]), reads=["ident"], writes=["identb"])
    sc.add("vector", lambda e: e.memset(cx.epsb[:], EPS), writes=["epsb"])


def norm_transpose(sc, cx, key, x_ap, xkey, gbc, gkey, ssq, rstd, junk, xn, xT_dst, xTkey, ps_bf, pskey,
                   evac_eng="scalar"):
    sc.add("scalar", lambda e: e.activation(out=junk, in_=x_ap, func=AF.Square, accum_out=ssq),
           reads=[xkey], writes=["junk", key + "ssq"])
    sc.add("scalar", lambda e: e.activation(out=rstd, in_=ssq, func=AF.Sqrt, bias=cx.epsb[:], scale=1.0 / D),
           reads=[key + "ssq", "epsb"], writes=[key + "rstd"])
    sc.add("vector", lambda e: e.reciprocal(rstd, rstd), reads=[key + "rstd"], writes=[key + "rstd"])
    sc.add("vector", lambda e: e.scalar_tensor_tensor(out=xn, in0=x_ap, scalar=rstd, in1=gbc,
                                                      op0=ALU.mult, op1=ALU.mult),
           reads=[xkey, key + "rstd", gkey], writes=[key + "xn"])

    def tr(e):
        ins = None
        for c in range(8):
            ins = e.transpose(ps_bf[:, c * 128:(c + 1) * 128], xn[:, c * 128:(c + 1) * 128], cx.identb[:])
        return ins

    sc.add("tensor", tr, reads=[key + "xn", "identb"], writes=[pskey])
    src = ps_bf.rearrange("p (c t) -> p c t", c=8)
    if evac_eng == "scalar":
        sc.add("scalar", lambda e: e.copy(xT_dst, src), reads=[pskey], writes=[xTkey])
    else:
        sc.add("vector", lambda e: e.tensor_copy(xT_dst, src), reads=[pskey], writes=[xTkey])


def load_rows_transposed(sc, es, nc, cx, name, src_ap, nrow, nchunk, ps, pskey, dst, dstkey):
    tmp = es.enter_context(nc.sbuf_tensor(_uname(name + "_tmp"), [nchunk, nrow, 128], F32))
    for j in range(nrow):
        sc.dma("sync", tmp[:, j, :], src_ap[j].rearrange("(c p) -> c p", p=128), writes=[name + "_tmp%d" % j],
               chan="const")

    def tr(e):
        ins = None
        for j in range(nrow):
            ins = e.transpose(ps[:, j * nchunk:(j + 1) * nchunk], tmp[:, j, :], cx.ident[0:nchunk, 0:nchunk])
        return ins

    sc.add("tensor", tr, reads=[name + "_tmp%d" % j for j in range(nrow)] + ["ident"], writes=[pskey])
    sc.add("vector", lambda e: e.tensor_copy(dst.rearrange("p c j -> p j c"),
                                             ps[:, 0:nrow * nchunk].rearrange("p (j c) -> p j c", j=nrow)),
           reads=[pskey], writes=[dstkey])


def ffn_phase(sc, nc, cx, h_in, h_out, g_ap, wup_ap, cw_ap, cb_ap, wdn_ap, fin_ap=None, ntiles=None):
    T = 256
    NT = (S // T) if ntiles is None else ntiles
    NCH = 2 * FF // 128
    NG = FF // 128
    with ExitStack() as es:
        def sb(name, shape, dt):
            return es.enter_context(nc.sbuf_tensor(_uname(name), shape, dt))

        def psum(name, shape, dt):
            return es.enter_context(nc.psum_tensor(_uname(name), shape, dt))

        wup = sb("wup", [128, 8, 2 * FF], BF16)
        wdn = sb("wdn", [128, NG, D], BF16)
        hb = [sb("hb%d" % i, [128, 2, D], F32) for i in range(2)]
        gbc = sb("gbc", [128, D], F32)
        fgbc = sb("fgbc", [128, D], F32) if fin_ap is not None else None
        xn = [sb("xn%d" % i, [128, D], BF16) for i in range(2)]
        junk = sb("junk", [128, D], F32)
        stat = sb("stat", [128, 16], F32)
        xT = [sb("xT%d" % i, [128, 8, T + 2], BF16) for i in range(2)]
        yb = [sb("yb%d" % i, [128, T], F32) for i in range(4)]
        sg = [sb("sg%d" % i, [128, T], F32) for i in range(4)]
        act = [sb("act%d" % i, [128, NG, T], BF16) for i in range(2)]
        cw = sb("cw", [128, NCH, 3], F32)
        cb = sb("cb", [128, NCH, 1], F32)
        ps_tr = [psum("ps_tr0", [128, D], BF16)] * 2
        ps_up = [psum("ps_up%d" % i, [128, 512], F32) for i in range(4)]
        ps_dn = [psum("ps_dn%d" % i, [128, 512], F32) for i in range(3)]

        sc.dma("sync", gbc[:], _bc(g_ap, 128), writes=["gbc"], chan="const")
        if fin_ap is not None:
            sc.dma("sync", fgbc[:], _bc(fin_ap, 128), writes=["fgbc"], chan="const")
        load_rows_transposed(sc, es, nc, cx, "cw", cw_ap, 3, NCH, ps_up[0], "ps_up0", cw[:], "cw")
        load_rows_transposed(sc, es, nc, cx, "cb", cb_ap.rearrange("(o n) -> o n", o=1), 1, NCH, ps_up[1],
                             "ps_up1", cb[:], "cb")
        wsrc = wup_ap.rearrange("(kc p) n -> p kc n", p=128)
        nwb = 0
        for blk in range(0, FF, 512):
            w_ = min(512, FF - blk)
            for half in range(2):
                lo = half * FF + blk
                sc.dma("gpsimd", wup[:, :, lo:lo + w_], wsrc[:, :, lo:lo + w_],
                       writes=["wup_%d_%d" % (half, blk // 512)], chan="w%d" % (nwb % 4))
                nwb += 1
        for c in range(NG):
            sc.dma("gpsimd", wdn[:, c, :], wdn_ap[c * 128:(c + 1) * 128, :], writes=["wdn"], chan="w%d" % (c % 4))
        sc.add("vector", lambda e: e.memset(xT[0][:, :, 0:2], 0.0), writes=["xT0_lead"])

        def load(t):
            sl = t % 2
            sc.dma("sync", hb[sl][:], h_in[t * T:(t + 1) * T, :].rearrange("(s p) d -> p s d", p=128),
                   writes=["hb%d_0" % sl, "hb%d_1" % sl], chan="hb%d" % sl)

        def down_group(t, s):
            sl = t % 2
            act_ = act[sl]
            hk = "hb%d_%d" % (sl, s)
            for n in range(2):
                pi = (s * 2 + n) % 3
                psd = ps_dn[pi]

                def mmd(e, s=s, n=n, psd=psd, act_=act_):
                    ins = None
                    for c in range(NG):
                        ins = e.matmul(psd[:], act_[:, c, s * 128:(s + 1) * 128], wdn[:, c, n * 512:(n + 1) * 512],
                                       start=(c == 0), stop=(c == NG - 1))
                    return ins

                sc.add("tensor", mmd, reads=["wdn"] + ["act%d_%d" % (sl, c) for c in range(NG)], writes=["ps_dn%d" % pi])
                sc.add("vector", lambda e, s=s, n=n, psd=psd, sl=sl: e.tensor_tensor(
                    out=hb[sl][:, s, n * 512:(n + 1) * 512], in0=psd[:], in1=hb[sl][:, s, n * 512:(n + 1) * 512],
                    op=ALU.add),
                    reads=["ps_dn%d" % pi, hk], writes=[hk])
            row0 = t * T + s * 128
            if fin_ap is not None:
                k = "f%d" % s
                sc.add("scalar", lambda e, sl=sl, s=s: e.activation(out=junk[:], in_=hb[sl][:, s, :], func=AF.Square,
                                                                    accum_out=stat[:, 8 + s:9 + s]),
                       reads=[hk], writes=["junk", k + "ssq"])
                sc.add("scalar", lambda e, s=s: e.activation(out=stat[:, 12 + s:13 + s], in_=stat[:, 8 + s:9 + s],
                                                             func=AF.Sqrt, bias=cx.epsb[:], scale=1.0 / D),
                       reads=[k + "ssq", "epsb"], writes=[k + "rstd"])
                sc.add("vector", lambda e, s=s: e.reciprocal(stat[:, 12 + s:13 + s], stat[:, 12 + s:13 + s]),
                       reads=[k + "rstd"], writes=[k + "rstd"])
                sc.add("vector", lambda e, sl=sl, s=s: e.scalar_tensor_tensor(
                    out=hb[sl][:, s, :], in0=hb[sl][:, s, :], scalar=stat[:, 12 + s:13 + s], in1=fgbc[:],
                    op0=ALU.mult, op1=ALU.mult),
                    reads=[hk, k + "rstd", "fgbc"], writes=[hk])
            sc.dma("sync", h_out[row0:row0 + 128, :], hb[sl][:, s, :], reads=[hk], chan="ho%d_%d" % (sl, s))

        def do_norm(t):
            sl = t % 2
            for s in range(2):
                k = "n%d" % s
                norm_transpose(sc, cx, k, hb[sl][:, s, :], "hb%d_%d" % (sl, s), gbc[:], "gbc",
                               stat[:, s:s + 1], stat[:, 4 + s:5 + s], junk[:], xn[s][:],
                               xT[sl][:, :, 2 + s * 128:2 + (s + 1) * 128], "xT%d_%d" % (sl, s),
                               ps_tr[s][:], "ps_tr0")
            if t > 0:
                sc.add("vector", lambda e, sl=sl: e.tensor_copy(xT[sl][:, :, 0:2], xT[1 - sl][:, :, T:T + 2]),
                       reads=["xT%d_1" % (1 - sl)], writes=["xT%d_lead" % sl])

        load(0)
        if NT > 1:
            load(1)
        nst = 0
        for t in range(NT):
            sl = t % 2
            if t == 0:
                do_norm(0)
            for j_ in range(NG):
                late = []
                for which in range(2):
                    c = j_ + which * NG
                    pi = nst % 4
                    si = nst % 4
                    nst += 1
                    pst = ps_up[pi]

                    def mm(e, c=c, pst=pst, sl=sl):
                        ins = None
                        for kc in range(8):
                            ins = e.matmul(pst[:, 0:T + 2], wup[:, kc, c * 128:(c + 1) * 128], xT[sl][:, kc, :],
                                           start=(kc == 0), stop=(kc == 7))
                        return ins

                    sc.add("tensor", mm, reads=["wup_%d_%d" % (which, (j_ * 128) // 512), "xT%d_0" % sl, "xT%d_1" % sl,
                                                "xT%d_lead" % sl],
                           writes=["ps_up%d" % pi])
                    y_ = yb[si]
                    sc.add("scalar", lambda e, pst=pst, y_=y_, c=c: e.activation(
                        out=y_[:], in_=pst[:, 2:T + 2], func=AF.Identity, bias=cb[:, c, 0:1], scale=cw[:, c, 2:3]),
                        reads=["ps_up%d" % pi, "cw", "cb"], writes=["yb%d" % si])
                    for j in (1, 0):
                        sc.add("vector", lambda e, pst=pst, y_=y_, c=c, j=j: e.scalar_tensor_tensor(
                            out=y_[:], in0=pst[:, j:T + j], scalar=cw[:, c, j:j + 1], in1=y_[:],
                            op0=ALU.mult, op1=ALU.add),
                            reads=["ps_up%d" % pi, "cw", "yb%d" % si], writes=["yb%d" % si])
                    if which == 0:
                        gi = j_ % 4
                        sg_ = sg[gi]
                        late.append(lambda sg_=sg_, y_=y_, si=si, gi=gi: sc.add(
                            "scalar", lambda e: e.activation(out=sg_[:], in_=y_[:], func=AF.Silu),
                            reads=["yb%d" % si], writes=["sg%d" % gi]))
                    else:
                        gi = j_ % 4
                        sg_ = sg[gi]
                        late.append(lambda sg_=sg_, y_=y_, j_=j_, sl=sl, si=si, gi=gi: sc.add(
                            "gpsimd", lambda e: e.tensor_tensor(out=act[sl][:, j_, :], in0=sg_[:], in1=y_[:], op=ALU.mult),
                            reads=["sg%d" % gi, "yb%d" % si], writes=["act%d_%d" % (sl, j_)]))
                for fn_ in late:
                    fn_()
                if t >= 1 and j_ in (5, 15):
                    down_group(t - 1, 0 if j_ == 5 else 1)
                    if j_ == 15 and t + 1 < NT:
                        load(t + 1)
                if j_ == 18 and t + 1 < NT:
                    do_norm(t + 1)
        for s in range(2):
            down_group(NT - 1, s)
        sc.flush()


def attn_phase(sc, nc, cx, h_in, h_out, gkv_ap, wkv_ap, bkv_ap, gq_ap, wq_ap, bq_ap, sinks_ap, wo_ap, bo_ap,
               bm_ap, nblk=None, debug_stage=0):
    NBLK = (S // 128) if nblk is None else nblk
    SL = NBLK * 128
    T = 256
    NT = SL // T
    with ExitStack() as es0:
        def sb0(name, shape, dt):
            return es0.enter_context(nc.sbuf_tensor(_uname(name), shape, dt))

        qT = sb0("qT", [128, 8, SL], BF16)
        kT = sb0("kT", [128, 4, SL], BF16)
        vv = sb0("vv", [128, NBLK, 256], BF16)
        with ExitStack() as es:
            def sb(name, shape, dt):
                return es.enter_context(nc.sbuf_tensor(_uname(name), shape, dt))

            def psum(name, shape, dt):
                return es.enter_context(nc.psum_tensor(_uname(name), shape, dt))

            wq = sb("wq", [128, 8, D], BF16)
            wkv = sb("wkv", [128, 8, 512], BF16)
            hb = [sb("hb%d" % i, [128, 2, D], F32) for i in range(2)]
            gq = sb("gq", [128, D], F32)
            gkv = sb("gkv", [128, D], F32)
            junk = sb("junk", [128, D], F32)
            xnq = [sb("xnq%d" % i, [128, D], BF16) for i in range(2)]
            xnk = [sb("xnk%d" % i, [128, D], BF16) for i in range(2)]
            xTq = sb("xTq", [128, 8, T], BF16)
            xTk = sb("xTk", [128, 8, T], BF16)
            stat = sb("stat", [128, 8], F32)
            bq8 = sb("bq8", [128, 8], F32)
            kbd = sb("kbd", [128, 4], F32)
            vbb = sb("vbb", [128, 256], F32)
            ps_tr = [psum("ps_tr%d" % i, [128, D], BF16) for i in range(2)]
            ps_q = [psum("ps_q%d" % i, [128, 512], F32) for i in range(2)]
            ps_k = [psum("ps_k%d" % i, [128, 512], F32) for i in range(2)]
            ps_v = [psum("ps_v%d" % i, [128, 512], F32) for i in range(2)]

            sc.dma("sync", gq[:], _bc(gq_ap, 128), writes=["gq"], chan="const")
            sc.dma("sync", gkv[:], _bc(gkv_ap, 128), writes=["gkv"], chan="const")
            sc.dma("sync", vbb[:], _bc(bkv_ap[256:512], 128), writes=["vbb"], chan="const")
            sc.dma("sync", bq8[:], bq_ap.rearrange("(c p) -> p c", p=128), writes=["bq8"], chan="const",
                   allow_slow_non_contiguous=True)
            for dup in range(2):
                sc.dma("sync", kbd[dup * 64:(dup + 1) * 64, :], bkv_ap[0:256].rearrange("(k p) -> p k", p=64),
                       writes=["kbd%d" % dup], chan="const", allow_slow_non_contiguous=True)
            sc.add("vector", lambda e: e.tensor_scalar(bq8[:], bq8[:], 0.125, None, op0=ALU.mult),
                   reads=["bq8"], writes=["bq8"])
            for kc in range(8):
                sc.dma("gpsimd", wq[:, kc, :], wq_ap[kc * 128:(kc + 1) * 128, :], writes=["wq"], chan="w%d" % (kc % 4))
            for kc in range(8):
                sc.dma("gpsimd", wkv[:, kc, :], wkv_ap[kc * 128:(kc + 1) * 128, :], writes=["wkv"],
                       chan="w%d" % (kc % 4))

            def load(t):
                sl = t % 2
                sc.dma("sync", hb[sl][:], h_in[t * T:(t + 1) * T, :].rearrange("(s p) d -> p s d", p=128),
                       writes=["hb%d" % sl], chan="hb%d" % sl)

            load(0)
            for t in range(NT):
                sl = t % 2
                if t + 1 < NT:
                    load(t + 1)
                for s in range(2):
                    x_ap = hb[sl][:, s, :]
                    ssq = stat[:, s:s + 1]
                    rstd = stat[:, 4 + s:5 + s]
                    k = "n%d" % s
                    sc.add("scalar", lambda e, x_ap=x_ap, ssq=ssq: e.activation(out=junk[:], in_=x_ap, func=AF.Square,
                                                                               accum_out=ssq),
                           reads=["hb%d" % sl], writes=["junk", k + "ssq"])
                    sc.add("scalar", lambda e, ssq=ssq, rstd=rstd: e.activation(out=rstd, in_=ssq, func=AF.Sqrt,
                                                                               bias=cx.epsb[:], scale=1.0 / D),
                           reads=[k + "ssq", "epsb"], writes=[k + "rstd"])
                    sc.add("vector", lambda e, rstd=rstd: e.reciprocal(rstd, rstd), reads=[k + "rstd"],
                           writes=[k + "rstd"])
                    sc.add("vector", lambda e, s=s, x_ap=x_ap, rstd=rstd: e.scalar_tensor_tensor(
                        out=xnq[s][:], in0=x_ap, scalar=rstd, in1=gq[:], op0=ALU.mult, op1=ALU.mult),
                        reads=["hb%d" % sl, k + "rstd", "gq"], writes=["xnq%d" % s])
                    sc.add("vector", lambda e, s=s, x_ap=x_ap, rstd=rstd: e.scalar_tensor_tensor(
                        out=xnk[s][:], in0=x_ap, scalar=rstd, in1=gkv[:], op0=ALU.mult, op1=ALU.mult),
                        reads=["hb%d" % sl, k + "rstd", "gkv"], writes=["xnk%d" % s])
                    for (xn_, xT_, nm, pi) in ((xnq[s], xTq, "q", 0), (xnk[s], xTk, "k", 1)):
                        def tr(e, xn_=xn_, pi=pi):
                            ins = None
                            for c in range(8):
                                ins = e.transpose(ps_tr[pi][:, c * 128:(c + 1) * 128], xn_[:, c * 128:(c + 1) * 128],
                                                  cx.identb[:])
                            return ins

                        sc.add("tensor", tr, reads=["xn%s%d" % (nm, s), "identb"], writes=["ps_tr%d" % pi])
                        sc.add("scalar", lambda e, xT_=xT_, pi=pi, s=s: e.copy(
                            xT_[:, :, s * 128:(s + 1) * 128], ps_tr[pi][:].rearrange("p (c t) -> p c t", c=8)),
                            reads=["ps_tr%d" % pi], writes=["xT%s_%d" % (nm, s)])
                for c in range(8):
                    pq = ps_q[c % 2]

                    def mmq(e, c=c, pq=pq):
                        ins = None
                        for kc in range(8):
                            ins = e.matmul(pq[:, 0:T], wq[:, kc, c * 128:(c + 1) * 128], xTq[:, kc, :],
                                           start=(kc == 0), stop=(kc == 7))
                        return ins

                    sc.add("tensor", mmq, reads=["wq", "xTq_0", "xTq_1"], writes=["ps_q%d" % (c % 2)])
                    sc.add("scalar", lambda e, c=c, pq=pq, t=t: e.activation(
                        out=qT[:, c, t * T:(t + 1) * T], in_=pq[:, 0:T], func=AF.Identity, bias=bq8[:, c:c + 1],
                        scale=0.125),
                        reads=["ps_q%d" % (c % 2), "bq8"], writes=["qT"])
                for kvh in range(4):
                    pk = ps_k[kvh % 2]

                    def mmk(e, kvh=kvh, pk=pk):
                        ins = None
                        for dup in range(2):
                            for kc in range(8):
                                ins = e.matmul(pk[dup * 64:(dup + 1) * 64, 0:T], wkv[:, kc, kvh * 64:(kvh + 1) * 64],
                                               xTk[:, kc, :], start=(kc == 0), stop=(kc == 7))
                        return ins

                    sc.add("tensor", mmk, reads=["wkv", "xTk_0", "xTk_1"], writes=["ps_k%d" % (kvh % 2)])
                    sc.add("vector", lambda e, kvh=kvh, pk=pk, t=t: e.tensor_scalar(
                        kT[:, kvh, t * T:(t + 1) * T], pk[:, 0:T], kbd[:, kvh:kvh + 1], None, op0=ALU.add),
                        reads=["ps_k%d" % (kvh % 2), "kbd0", "kbd1"], writes=["kT"])
                for s in range(2):
                    pv = ps_v[s]

                    def mmv(e, s=s, pv=pv):
                        ins = None
                        for kc in range(8):
                            ins = e.matmul(pv[:, 0:256], xTk[:, kc, s * 128:(s + 1) * 128], wkv[:, kc, 256:512],
                                           start=(kc == 0), stop=(kc == 7))
                        return ins

                    sc.add("tensor", mmv, reads=["wkv", "xTk_%d" % s], writes=["ps_v%d" % s])
                    sc.add("vector", lambda e, s=s, pv=pv, t=t: e.tensor_tensor(
                        out=vv[:, t * 2 + s, :], in0=pv[:, 0:256], in1=vbb[:], op=ALU.add),
                        reads=["ps_v%d" % s, "vbb"], writes=["vv"])
            sc.flush()
        if debug_stage == 1:
            return
        with ExitStack() as es:
            def sb(name, shape, dt):
                return es.enter_context(nc.sbuf_tensor(_uname(name), shape, dt))

            def psum(name, shape, dt):
                return es.enter_context(nc.psum_tensor(_uname(name), shape, dt))

            wo = sb("wo", [128, 8, D], BF16)
            bm = sb("bm", [128, 16, 256], F32)
            bob = sb("bob", [128, D], F32)
            sink = sb("sink", [128, 16], F32)
            hb = [sb("hb%d" % i, [128, D], F32) for i in range(2)]
            ss = [sb("ss%d" % i, [128, 4, 260], F32) for i in range(4)]
            pp = [sb("pp%d" % i, [128, 4, 260], BF16) for i in range(2)]
            pn = [sb("pn%d" % i, [128, 4, 256], BF16) for i in range(2)]
            pT = [sb("pT%d" % i, [128, 4, 2, 128], BF16) for i in range(2)]
            oT = sb("oT", [128, 8, 128], BF16)
            ho = [sb("ho%d" % i, [128, D], F32) for i in range(2)]
            st = [sb("st%d" % i, [128, 16], F32) for i in range(2)]
            kpl = [sb("kpl%d" % i, [128, 4, 256], BF16) for i in range(2)]
            kph = [sb("kph%d" % i, [128, 4, 256], BF16) for i in range(2)]
            ps_s = [psum("ps_s%d" % i, [128, 4, 256], F32) for i in range(2)]
            ps_pT = [psum("ps_pT%d" % i, [128, 4, 2, 128], BF16) for i in range(2)]
            ps_oT = psum("ps_oT", [128, 8, 128], F32)
            ps_of = ps_oT[:].rearrange("p c t -> p (c t)")

            for i in range(2):
                sc.add("gpsimd", lambda e, i=i: e.memset(kpl[i][:], 0.0), writes=["kpl%d" % i])
                sc.add("gpsimd", lambda e, i=i: e.memset(kph[i][:], 0.0), writes=["kph%d" % i])
            for kc in range(8):
                sc.dma("gpsimd", wo[:, kc, :], wo_ap[kc * 128:(kc + 1) * 128, :], writes=["wo"], chan="w%d" % (kc % 4))
            sc.dma("sync", bm[:], bm_ap, writes=["bm"], chan="const")
            sc.dma("sync", bob[:], _bc(bo_ap, 128), writes=["bob"], chan="const")
            sc.dma("sync", sink[:], _bc(sinks_ap, 128), writes=["sink"], chan="const")
            for hg in range(4):
                sc.add("vector", lambda e, hg=hg: e.tensor_copy(ss[hg][:, :, 0:1], sink[:, 4 * hg:4 * hg + 4].unsqueeze(2)),
                       reads=["sink"], writes=["ss%d" % hg])

            def load2(n):
                sl = n % 2
                sc.dma("sync", hb[sl][:], h_in[n * 128:(n + 1) * 128, :], writes=["hb%d" % sl], chan="hb%d" % sl)
                sc.add("gpsimd", lambda e, sl=sl: e.tensor_tensor(out=hb[sl][:], in0=hb[sl][:], in1=bob[:], op=ALU.add),
                       reads=["hb%d" % sl, "bob"], writes=["hb%d" % sl])

            def kpcopy(n):
                nk = 128 if n == 0 else 256
                k0 = 0 if n == 0 else (n - 1) * 128
                kp_ = (kpl[n % 2], kph[n % 2])
                sc.add("gpsimd", lambda e, kp_=kp_, nk=nk, k0=k0: e.tensor_copy(kp_[0][0:64, :, 0:nk],
                                                                            kT[0:64, :, k0:k0 + nk]),
                       reads=["kT"], writes=["kpl%d" % (n % 2)])
                sc.add("gpsimd", lambda e, kp_=kp_, nk=nk, k0=k0: e.tensor_copy(kp_[1][64:128, :, 0:nk],
                                                                            kT[64:128, :, k0:k0 + nk]),
                       reads=["kT"], writes=["kph%d" % (n % 2)])

            load2(0)
            kpcopy(0)
            for n in range(NBLK):
                if n + 1 < NBLK:
                    load2(n + 1)
                    kpcopy(n + 1)
                nk = 128 if n == 0 else 256
                nkb = nk // 128
                k0 = 0 if n == 0 else (n - 1) * 128
                blk0 = 0 if n == 0 else n - 1
                kp_ = (kpl[n % 2], kph[n % 2])
                for pair in range(2):
                    grp = [(2 * pair + j, j) for j in range(2)]

                    for hg, i2 in grp:
                        def mmqk(e, hg=hg, pss=ps_s[i2], n=n, nk=nk, kp_=kp_):
                            ins = None
                            for g in range(4):
                                h = 4 * hg + g
                                c, hf = h // 2, h % 2
                                ins = e.matmul(pss[:, g, 0:nk], qT[:, c, n * 128:(n + 1) * 128],
                                               kp_[hf][:, hg, 0:nk], start=True, stop=True)
                            return ins

                        sc.add("tensor", mmqk, reads=["qT", "kpl%d" % (n % 2), "kph%d" % (n % 2)],
                               writes=["ps_s%d" % i2])
                    for hg, i2 in grp:
                        sc.add("vector", lambda e, hg=hg, i2=i2, nk=nk: e.tensor_tensor(
                            out=ss[hg][:, :, 1:1 + nk], in0=ps_s[i2][:, :, 0:nk],
                            in1=bm[:, 4 * hg:4 * hg + 4, 256 - nk:256], op=ALU.add),
                            reads=["ps_s%d" % i2, "bm"], writes=["ss%db" % hg])
                    for hg, i2 in grp:
                        sc.add("vector", lambda e, hg=hg, i2=i2, nk=nk: e.tensor_reduce(
                            out=st[i2][:, 0:4], in_=ss[hg][:, :, 0:1 + nk], axis=AX.X, op=ALU.max, negate=True),
                            reads=["ss%db" % hg, "ss%d" % hg], writes=["negm%d" % i2])
                    for hg, i2 in grp:
                        for g in range(4):
                            sc.add("scalar", lambda e, g=g, hg=hg, i2=i2, nk=nk: e.activation(
                                out=pp[i2][:, g, 0:1 + nk], in_=ss[hg][:, g, 0:1 + nk], func=AF.Exp,
                                bias=st[i2][:, g:g + 1], accum_out=st[i2][:, 4 + g:5 + g]),
                                reads=["ss%db" % hg, "ss%d" % hg, "negm%d" % i2],
                                writes=["pp%d_%d" % (i2, g), "rsum%d_%d" % (i2, g)])
                    for hg, i2 in grp:
                        sc.add("vector", lambda e, i2=i2: e.reciprocal(st[i2][:, 8:12], st[i2][:, 4:8]),
                               reads=["rsum%d_%d" % (i2, g) for g in range(4)], writes=["rden%d" % i2])
                        for g in range(4):
                            sc.add("scalar", lambda e, i2=i2, nk=nk, g=g: e.activation(
                                out=pn[i2][:, g, 0:nk], in_=pp[i2][:, g, 1:1 + nk], func=AF.Copy,
                                scale=st[i2][:, 8 + g:9 + g]),
                                reads=["rden%d" % i2, "pp%d_%d" % (i2, g)], writes=["pn%d_%d" % (i2, g)])
                    for hg, i2 in grp:
                        def trp(e, i2=i2, nkb=nkb):
                            ins = None
                            for g in range(4):
                                for kb in range(nkb):
                                    ins = e.transpose(ps_pT[i2][:, g, kb, :], pn[i2][:, g, kb * 128:(kb + 1) * 128],
                                                      cx.identb[:])
                            return ins

                        sc.add("tensor", trp, reads=["pn%d_%d" % (i2, g) for g in range(4)] + ["identb"],
                               writes=["ps_pT%d" % i2])
                    for hg, i2 in grp:
                        eng = "scalar" if i2 == 0 else "vector"
                        if eng == "scalar":
                            sc.add("scalar", lambda e, i2=i2, nkb=nkb: e.copy(pT[i2][:, :, 0:nkb, :],
                                                                             ps_pT[i2][:, :, 0:nkb, :]),
                                   reads=["ps_pT%d" % i2], writes=["pT%d" % i2])
                        else:
                            sc.add("vector", lambda e, i2=i2, nkb=nkb: e.tensor_copy(pT[i2][:, :, 0:nkb, :],
                                                                                    ps_pT[i2][:, :, 0:nkb, :]),
                                   reads=["ps_pT%d" % i2], writes=["pT%d" % i2])
                    for hg, i2 in grp:
                        def mmpv(e, hg=hg, i2=i2, nkb=nkb, blk0=blk0):
                            ins = None
                            for g in range(4):
                                h = 4 * hg + g
                                c, hf = h // 2, h % 2
                                for kb in range(nkb):
                                    ins = e.matmul(ps_oT[hf * 64:(hf + 1) * 64, c, :],
                                                   vv[:, blk0 + kb, hg * 64:(hg + 1) * 64],
                                                   pT[i2][:, g, kb, :], start=(kb == 0), stop=(kb == nkb - 1))
                            return ins

                        sc.add("tensor", mmpv, reads=["pT%d" % i2, "vv"], writes=["ps_oT%d" % hg, "po%d" % (hg // 2)])
                sc.add("scalar", lambda e: e.copy(oT[:], ps_oT[:]), reads=["ps_oT%d" % hg for hg in range(4)],
                       writes=["oT"])
                sl = n % 2
                for nh in range(2):
                    def mmo(e, nh=nh):
                        ins = None
                        for c in range(8):
                            ins = e.matmul(ps_of[:, nh * 512:(nh + 1) * 512], oT[:, c, :],
                                           wo[:, c, nh * 512:(nh + 1) * 512], start=(c == 0), stop=(c == 7))
                        return ins

                    sc.add("tensor", mmo, reads=["oT", "wo"], writes=["po%d" % nh])
                    sc.add("vector", lambda e, nh=nh, sl=sl: e.tensor_tensor(
                        out=ho[sl][:, nh * 512:(nh + 1) * 512], in0=ps_of[:, nh * 512:(nh + 1) * 512],
                        in1=hb[sl][:, nh * 512:(nh + 1) * 512], op=ALU.add),
                        reads=["po%d" % nh, "hb%d" % sl], writes=["ho%d_%d" % (sl, nh)])
                sc.dma("sync", h_out[n * 128:(n + 1) * 128, :], ho[sl][:], reads=["ho%d_0" % sl, "ho%d_1" % sl],
                       chan="ho%d" % sl)
            sc.flush()


def make_biasmask():
    slopes = 2.0 ** (-8.0 * np.arange(1, 17, dtype=np.float64) / 16.0)
    q = np.arange(128)[:, None]
    kj = np.arange(256)[None, :]
    dist = q + 128 - kj
    valid = (dist >= 0) & (dist < 128)
    bm = np.where(valid[:, None, :], -slopes[None, :, None] * dist[:, None, :], -30000.0)
    return np.ascontiguousarray(bm.astype(np.float32))


A_IN = 4112


def delta_proj_phase(sc, nc, cx, x_in, g_ap, win_ap, cw_ap, alog_ap, dtb_ap, QF, KF, KT, VT, GT, BG, ntiles=None):
    T = 256
    NT = (S // T) if ntiles is None else ntiles
    with ExitStack() as es:
        def sb(name, shape, dt):
            return es.enter_context(nc.sbuf_tensor(_uname(name), shape, dt))

        def psum(name, shape, dt):
            return es.enter_context(nc.psum_tensor(_uname(name), shape, dt))

        win = sb("win", [128, 8, A_IN], BF16)
        hb = [sb("hb%d" % i, [128, 2, D], F32) for i in range(2)]
        gbc = sb("gbc", [128, D], F32)
        junk = sb("junk", [128, D], F32)
        xn = [sb("xn%d" % i, [128, D], BF16) for i in range(2)]
        stat = sb("stat", [128, 16], F32)
        xTs = [sb("xT%d" % i, [128, 8, T + 3], BF16) for i in range(2)]
        sF = sb("sF", [128, 24, T], F32)
        yb = [sb("yb%d" % i, [128, T], F32) for i in range(4)]
        cw = sb("cw", [128, 24, 4], F32)
        sqs = [sb("sq%d" % i, [128, 2, T], F32) for i in range(2)]
        rns = [sb("rn%d" % i, [128, 2, T], F32) for i in range(2)]
        ones = sb("ones", [128, 128], F32)
        oneb = sb("oneb", [128, 1], F32)
        ot = [sb("ot%d" % i, [128, D], F32) for i in range(2)]
        go = [sb("go%d" % i, [128, D], F32) for i in range(2)]
        bg = [sb("bg%d" % i, [128, 16], F32) for i in range(2)]
        ba = sb("ba", [128, 32], F32)
        negA = sb("negA", [128, 8], F32)
        dtb = sb("dtb", [128, 8], F32)
        ps_tr = psum("ps_tr", [128, D], BF16)
        ps_p = [psum("ps_p%d" % i, [128, 512], F32) for i in range(3)]
        ps_n = psum("ps_n", [128, 2, T], F32)
        ps_T = psum("ps_T", [128, 8, 128], F32)
        ps_g = psum("ps_g", [128, 512], F32)
        ps_b = ps_n[:].rearrange("p a t -> p (a t)")

        sc.dma("sync", gbc[:], _bc(g_ap, 128), writes=["gbc"], chan="const")
        sc.dma("sync", negA[:], _bc(alog_ap, 128), writes=["negA"], chan="const")
        sc.dma("sync", dtb[:], _bc(dtb_ap, 128), writes=["dtb"], chan="const")
        sc.add("scalar", lambda e: e.activation(out=negA[:], in_=negA[:], func=AF.Exp), reads=["negA"], writes=["negA"])
        sc.add("vector", lambda e: e.tensor_scalar(negA[:], negA[:], -1.0, None, op0=ALU.mult), reads=["negA"],
               writes=["negA"])
        sc.add("vector", lambda e: e.memset(ones[:], 1.0), writes=["ones"])
        sc.add("vector", lambda e: e.memset(oneb[:], 1.0), writes=["oneb"])
        lnq = sb("lnq", [128, 1], F32)
        lnk = sb("lnk", [128, 1], F32)
        sc.add("vector", lambda e: e.memset(lnq[:], -2.4260151319598084), writes=["lnsc"])
        sc.add("vector", lambda e: e.memset(lnk[:], 0.0), writes=["lnsc"])
        sc.add("vector", lambda e: e.memset(xTs[0][:, :, 0:3], 0.0), writes=["xT0_lead"])
        load_rows_transposed(sc, es, nc, cx, "cwa", cw_ap, 4, 24, ps_p[0], "ps_p0", cw[:], "cw")
        wsrc = win_ap.rearrange("(kc p) n -> p kc n", p=128)
        for bi, lo in enumerate(range(0, A_IN, 512)):
            w_ = min(512, A_IN - lo)
            sc.dma("gpsimd", win[:, :, lo:lo + w_], wsrc[:, :, lo:lo + w_], writes=["win_%d" % bi], chan="w%d" % (bi % 4))

        def load(t):
            sl = t % 2
            sc.dma("sync", hb[sl][:], x_in[t * T:(t + 1) * T, :].rearrange("(s p) d -> p s d", p=128),
                   writes=["hb%d" % sl], chan="hb%d" % sl)

        def do_norm(t):
            sl = t % 2
            xT = xTs[sl]
            if t > 0:
                sc.add("vector", lambda e, sl=sl: e.tensor_copy(xTs[sl][:, :, 0:3], xTs[1 - sl][:, :, T:T + 3]),
                       reads=["xT%d_1" % (1 - sl)], writes=["xT%d_lead" % sl])
            for s in range(2):
                norm_transpose(sc, cx, "n%d" % s, hb[sl][:, s, :], "hb%d" % sl, gbc[:], "gbc",
                               stat[:, s:s + 1], stat[:, 4 + s:5 + s], junk[:], xn[s][:],
                               xT[:, :, 3 + s * 128:3 + (s + 1) * 128], "xT%d_%d" % (sl, s), ps_tr[:], "ps_tr")

        load(0)
        nst = 0
        nout = 0
        for t in range(NT):
            sl = t % 2
            t0 = t * T
            if t + 1 < NT:
                load(t + 1)
            if t == 0:
                do_norm(0)
            xT = xTs[sl]
            XK = "xT%d" % sl
            pend = None
            for c in range(24):
                pi = nst % 3
                si = nst % 4
                nst += 1
                pst = ps_p[pi]

                def mm(e, c=c, pst=pst, xT=xT):
                    ins = None
                    for kc in range(8):
                        ins = e.matmul(pst[:, 0:T + 3], win[:, kc, c * 128:(c + 1) * 128], xT[:, kc, :],
                                       start=(kc == 0), stop=(kc == 7))
                    return ins

                sc.add("tensor", mm, reads=["win_%d" % ((c * 128) // 512), XK + "_0", XK + "_1", XK + "_lead"],
                       writes=["ps_p%d" % pi])
                y_ = yb[si]
                sc.add("scalar", lambda e, pst=pst, y_=y_, c=c: e.activation(
                    out=y_[:], in_=pst[:, 3:T + 3], func=AF.Copy, scale=cw[:, c, 3:4]),
                    reads=["ps_p%d" % pi, "cw"], writes=["yb%d" % si])
                for j in (2, 1, 0):
                    sc.add("vector", lambda e, pst=pst, y_=y_, c=c, j=j: e.scalar_tensor_tensor(
                        out=y_[:], in0=pst[:, j:T + j], scalar=cw[:, c, j:j + 1], in1=y_[:], op0=ALU.mult, op1=ALU.add),
                        reads=["ps_p%d" % pi, "cw", "yb%d" % si], writes=["yb%d" % si])
                if pend is not None:
                    pend()
                pend = (lambda y_=y_, c=c, si=si: sc.add(
                    "scalar", lambda e: e.activation(out=sF[:, c, :], in_=y_[:], func=AF.Silu),
                    reads=["yb%d" % si], writes=["sF%d" % c]))
                if c == 20 and t + 1 < NT:
                    do_norm(t + 1)
            pend()
            def l2_sq(c):
                li = (c // 2) % 2
                sq = sqs[li]
                sc.add("scalar", lambda e, c=c, sq=sq: e.activation(out=sq[:], in_=sF[:, c:c + 2, :], func=AF.Square),
                       reads=["sF%d" % c, "sF%d" % (c + 1)], writes=["sq%d" % li])

            def l2_mm(c):
                li = (c // 2) % 2
                sq = sqs[li]

                def mmn(e, sq=sq):
                    ins = None
                    for i in range(2):
                        ins = e.matmul(ps_n[:, i, :], ones[:], sq[:, i, :], start=True, stop=True)
                    return ins

                sc.add("tensor", mmn, reads=["sq%d" % li, "ones"], writes=["ps_n"])

            def l2_fin(c):
                li = (c // 2) % 2
                rn = rns[li]
                sc.add("scalar", lambda e, rn=rn: e.activation(out=rn[:], in_=ps_n[:], func=AF.Ln, bias=cx.epsb[:],
                                                              scale=1.0),
                       reads=["ps_n", "epsb"], writes=["rn%d" % li])
                lb = lnq if c < 8 else lnk
                sc.add("scalar", lambda e, rn=rn, lb=lb: e.activation(out=rn[:], in_=rn[:], func=AF.Exp, bias=lb[:],
                                                                     scale=-0.5),
                       reads=["rn%d" % li, "lnsc"], writes=["rn%d" % li])
                sc.add("gpsimd", lambda e, c=c, rn=rn: e.tensor_tensor(
                    out=sF[:, c:c + 2, :], in0=sF[:, c:c + 2, :], in1=rn[:], op=ALU.mult),
                    reads=["rn%d" % li, "sF%d" % c, "sF%d" % (c + 1)], writes=["sF%d" % c, "sF%d" % (c + 1)])

            l2_sq(0)
            for c in range(0, 16, 2):
                l2_mm(c)
                if c + 2 < 16:
                    l2_sq(c + 2)
                l2_fin(c)
            sc.dma("sync", QF[:, :, t0:t0 + T].rearrange("h d t -> d h t"), sF[:, 0:8, :],
                   reads=["sF%d" % c for c in range(8)], chan="qf")
            sc.dma("sync", KF[:, :, t0:t0 + T].rearrange("h d t -> d h t"), sF[:, 8:16, :],
                   reads=["sF%d" % c for c in range(8, 16)], chan="kf")
            for s in range(2):
                for (base, dst, nm) in ((8, KT, "k"), (16, VT, "v")):
                    oi = nout % 2
                    nout += 1

                    def trT(e, base=base, s=s):
                        ins = None
                        for h in range(8):
                            ins = e.transpose(ps_T[:, h, :], sF[:, base + h, s * 128:(s + 1) * 128], cx.ident[:])
                        return ins

                    sc.add("tensor", trT, reads=["sF%d" % (base + h) for h in range(8)] + ["ident"], writes=["ps_T"])
                    sc.add("scalar", lambda e, oi=oi: e.copy(ot[oi][:], ps_T[:].rearrange("p h d -> p (h d)")),
                           reads=["ps_T"], writes=["ot%d" % oi])
                    sc.dma("sync", dst[t0 + s * 128:t0 + (s + 1) * 128, :], ot[oi][:], reads=["ot%d" % oi],
                           chan="ot%d" % oi)
            for s in range(2):
                gi = s
                for nh in range(2):
                    def mmg(e, s=s, nh=nh, xT=xT):
                        ins = None
                        for kc in range(8):
                            ins = e.matmul(ps_g[:], xT[:, kc, 3 + s * 128:3 + (s + 1) * 128],
                                           win[:, kc, 3072 + nh * 512:3072 + (nh + 1) * 512],
                                           start=(kc == 0), stop=(kc == 7))
                        return ins

                    sc.add("tensor", mmg, reads=["win_%d" % (6 + nh), XK + "_%d" % s], writes=["ps_g"])
                    sc.add("scalar", lambda e, gi=gi, nh=nh: e.activation(out=go[gi][:, nh * 512:(nh + 1) * 512],
                                                                         in_=ps_g[:], func=AF.Silu),
                           reads=["ps_g"], writes=["go%d_%d" % (gi, nh)])
                sc.dma("sync", GT[t0 + s * 128:t0 + (s + 1) * 128, :], go[gi][:], reads=["go%d_0" % gi, "go%d_1" % gi],
                       chan="go%d" % gi)

                def mmb(e, s=s, xT=xT):
                    ins = None
                    for kc in range(8):
                        ins = e.matmul(ps_b[:, 0:16], xT[:, kc, 3 + s * 128:3 + (s + 1) * 128], win[:, kc, 4096:4112],
                                       start=(kc == 0), stop=(kc == 7))
                    return ins

                sc.add("tensor", mmb, reads=["win_8", XK + "_%d" % s], writes=["ps_n"])
                bg_ = bg[s]
                sc.add("scalar", lambda e, bg_=bg_: e.activation(out=bg_[:, 0:8], in_=ps_b[:, 0:8], func=AF.Sigmoid),
                       reads=["ps_n"], writes=["bg%d_b" % s])
                sc.add("vector", lambda e: e.tensor_tensor(out=ba[:, 0:8], in0=ps_b[:, 8:16], in1=dtb[:], op=ALU.add),
                       reads=["ps_n", "dtb", "bg%d_b" % s], writes=["ba0"])
                sc.add("scalar", lambda e: e.activation(out=ba[:, 8:16], in_=ba[:, 0:8], func=AF.Exp),
                       reads=["ba0"], writes=["ba1"])
                sc.add("scalar", lambda e: e.activation(out=ba[:, 16:24], in_=ba[:, 8:16], func=AF.Ln, bias=oneb[:],
                                                        scale=1.0),
                       reads=["ba1", "oneb"], writes=["ba2"])
                sc.add("vector", lambda e, bg_=bg_: e.tensor_tensor(out=bg_[:, 8:16], in0=ba[:, 16:24], in1=negA[:],
                                                                  op=ALU.mult),
                       reads=["ba2", "negA"], writes=["bg%d_g" % s])
                sc.dma("sync", BG[t0 + s * 128:t0 + (s + 1) * 128, :], bg_[:], reads=["bg%d_b" % s, "bg%d_g" % s],
                       chan="bg%d" % s)
        sc.flush()


def make_delta_consts():
    j = np.arange(64)[:, None]
    i = np.arange(64)[None, :]
    c = np.zeros((64, 3, 64), np.float32)
    c[:, 0, :] = (j <= i)
    c[:, 1, :] = np.where(i <= j, 0.0, -30000.0)
    c[:, 2, :] = (i < j)
    return c


def delta_phase(sc, nc, cx, x_in, h_out, QF, KF, KT, VT, GT, BG, og_ap, wout_ap, cst_ap, nchunks=None):
    NCHK = (S // 64) if nchunks is None else nchunks
    with ExitStack() as es:
        def sb(name, shape, dt):
            return es.enter_context(nc.sbuf_tensor(_uname(name), shape, dt))

        def psum(name, shape, dt):
            return es.enter_context(nc.psum_tensor(_uname(name), shape, dt))

        A = sc.add
        wout = sb("wout", [128, 8, D], BF16)
        cst = sb("cst", [128, 3, 64], F32)
        onesP = sb("onesP", [128, 128], F32)
        ogb = sb("ogb", [64, 128], F32)
        Sst = sb("Sst", [128, 8, 128], F32)
        Sb = sb("Sb", [128, 8, 128], BF16)
        Stmp = sb("Stmp", [128, 8, 128], F32)
        qFs = [sb("qFs%d" % i, [128, 8, 256], F32) for i in range(2)]
        kFs = [sb("kFs%d" % i, [128, 8, 256], F32) for i in range(2)]
        kTok = [sb("kTok%d" % i, [64, 8, 128], F32) for i in range(2)]
        vTok = [sb("vTok%d" % i, [64, 8, 128], F32) for i in range(2)]
        gTk = [sb("gTk%d" % i, [64, 8, 128], F32) for i in range(2)]
        xres = [sb("xres%d" % i, [128, D], F32) for i in range(2)]
        bgt = [sb("bgt%d" % i, [128, 16], F32) for i in range(2)]
        smM = [sb("sm%d" % i, [128, 64], F32) for i in range(2)]
        dGCM = [sb("dGC%d" % i, [128, 8, 64], F32) for i in range(2)]
        eRM = [sb("eR%d" % i, [128, 8, 64], F32) for i in range(2)]
        qdecM = [sb("qdec%d" % i, [128, 8, 64], BF16) for i in range(2)]
        DtM = [sb("Dt%d" % i, [64, 8, 64], F32) for i in range(2)]
        decM = [sb("dec%d" % i, [64, 8, 64], F32) for i in range(2)]
        t1M = [sb("t1%d" % i, [64, 8, 64], F32) for i in range(2)]
        attnM = [sb("attn%d" % i, [128, 8, 64], F32) for i in range(2)]
        PmM = [[sb("Pm%d_%d" % (m, i), [128, 8, 64], F32) for i in range(2)] for m in range(2)]
        PTmM = [[sb("PTm%d_%d" % (m, i), [128, 8, 64], F32) for i in range(2)] for m in range(2)]
        TTmM = [[sb("TTm%d_%d" % (m, i), [128, 8, 64], F32) for i in range(2)] for m in range(2)]
        TT16M = [sb("TT16%d" % i, [128, 8, 64], BF16) for i in range(2)]
        aT16M = [sb("aT16%d" % i, [128, 8, 64], BF16) for i in range(2)]
        VBbM = [sb("VBb%d" % i, [128, 8, 128], BF16) for i in range(2)]
        RKbM = [sb("RKb%d" % i, [128, 8, 128], BF16) for i in range(2)]
        KDbM = [sb("KDb%d" % i, [128, 8, 128], BF16) for i in range(2)]
        ggM = [sb("gg%d" % i, [64, 8, 128], F32) for i in range(2)]
        vnb = sb("vnb", [128, 8, 128], BF16)
        nk16 = sb("nk16", [128, 8, 64], BF16)
        osq = sb("osq", [64, 8, 128], F32)
        o1 = sb("o1", [128, 8, 128], F32)
        smc = sb("smc", [64, 8], F32)
        oT = sb("oT", [128, 8, 128], BF16)
        ho = sb("ho", [128, D], F32)
        pP = [psum("pP%d" % i, [128, 1024], F32) for i in range(4)]

        def v8(ap):
            return ap.rearrange("p (h j) -> p h j", h=8)

        kcd_ps = v8(pP[3][:, 512:1024])
        vn_ps = v8(pP[0][0:64, :])
        o_ps = v8(pP[1][0:64, :])
        Sn_ps = v8(pP[2][:, :])
        oT_ps = v8(pP[3][:, 0:512])
        out_ps = [pP[0][:, 0:512], pP[0][:, 512:1024]]

        for kc in range(8):
            sc.dma("gpsimd", wout[:, kc, :], wout_ap[kc * 128:(kc + 1) * 128, :], writes=["wout"], chan="w%d" % (kc % 4))
        A("vector", lambda e: e.memset(cst[:], 0.0), writes=["cst"])
        sc.dma("sync", cst[0:64, :, :], cst_ap, reads=[], writes=["cst"], chan="const")
        sc.dma("sync", ogb[:], _bc(og_ap, 64), writes=["ogb"], chan="const")
        A("vector", lambda e: e.memset(onesP[:], 0.0), writes=["onesP"])
        A("vector", lambda e: e.memset(onesP[0:64, :], 1.0), writes=["onesP"])
        A("vector", lambda e: e.memset(Sst[:], 0.0), writes=["Sst"])
        A("vector", lambda e: e.memset(Sb[:], 0.0), writes=["Sb"])
        zlist = []
        for m in range(2):
            zlist += [(PmM[m][0], "Pm%d_0" % m), (PmM[m][1], "Pm%d_1" % m), (PTmM[m][0], "PTm%d_0" % m),
                      (PTmM[m][1], "PTm%d_1" % m), (TTmM[m][0], "TTm%d_0" % m), (TTmM[m][1], "TTm%d_1" % m),
                      (TT16M[m], "TT16%d" % m), (aT16M[m], "aT16%d" % m), (VBbM[m], "VBb%d" % m),
                      (RKbM[m], "RKb%d" % m), (KDbM[m], "KDb%d" % m), (dGCM[m], "dGC%d" % m), (bgt[m], "bgt%d" % m),
                      (attnM[m], "attn%d" % m)]
        zlist += [(vnb, "vnb"), (o1, "o1")]
        zkeys = {}
        for i, (tl, nme) in enumerate(zlist):
            A("gpsimd", lambda e, tl=tl: e.memset(tl[:], 0.0), writes=["z%d" % i])
            zkeys[nme] = "z%d" % i

        U = cst[:, 0, :]
        mbi = cst[0:64, 1, :]
        st01 = cst[0:64, 2, :]
        id64 = cx.ident[0:64, 0:64]

        def bc_h(ap2d, n):
            return ap2d.unsqueeze(1).to_broadcast([n, 8, ap2d.shape[-1]])

        def bc_f(ap2d, n, f):
            return ap2d.unsqueeze(2).to_broadcast([n, 8, f])

        def load_super(st):
            b = st % 2
            t0 = st * 256
            sc.dma("sync", qFs[b][:], QF[:, :, t0:t0 + 256].rearrange("h d t -> d h t"), writes=["qFs%d" % b],
                   chan="qFs%d" % b)
            sc.dma("sync", kFs[b][:], KF[:, :, t0:t0 + 256].rearrange("h d t -> d h t"), writes=["kFs%d" % b],
                   chan="kFs%d" % b)

        def load_chunk(ch):
            b = ch % 2
            t0 = ch * 64
            sc.dma("sync", kTok[b][:], KT[t0:t0 + 64, :].rearrange("t (h d) -> t h d", h=8), writes=["kTok%d" % b],
                   chan="kTok%d" % b)
            sc.dma("sync", vTok[b][:], VT[t0:t0 + 64, :].rearrange("t (h d) -> t h d", h=8), writes=["vTok%d" % b],
                   chan="vTok%d" % b)
            sc.dma("sync", gTk[b][:], GT[t0:t0 + 64, :].rearrange("t (h d) -> t h d", h=8), writes=["gTk%d" % b],
                   chan="gTk%d" % b)
            sc.dma("sync", bgt[b][0:64, :], BG[t0:t0 + 64, :], reads=[zkeys["bgt%d" % b]], writes=["bgt%d" % b],
                   chan="bgt%d" % b)

        def load_xres(pr):
            b = (pr // 2) % 2
            t0 = pr * 64
            sc.dma("sync", xres[b][:], x_in[t0:t0 + 128, :], writes=["xres%d" % b], chan="xres%d" % b)

        def pre(ch, m):
            st = ch // 4
            sbi = st % 2
            tl = (ch % 4) * 64
            qF = qFs[sbi][:, :, tl:tl + 64]
            kF = kFs[sbi][:, :, tl:tl + 64]
            kq = ["qFs%d" % sbi, "kFs%d" % sbi]
            bg_ = bgt[m]
            beta = bg_[0:64, 0:8]
            BGK = "bgt%d" % m
            sm, dGC, eR, qdec, Dt, dec, t1, attn = smM[m], dGCM[m], eRM[m], qdecM[m], DtM[m], decM[m], t1M[m], attnM[m]
            Pm, PTm, TTm = PmM[m], PTmM[m], TTmM[m]
            TT16, aT16, VBb, RKb, KDb, gg = TT16M[m], aT16M[m], VBbM[m], RKbM[m], KDbM[m], ggM[m]
            X = pP[2 * m][:, 0:512]
            psA = X[:, 0:16]
            Tup_ps = v8(pP[2 * m][0:64, 0:512])
            R_ps = v8(pP[2 * m][:, 512:1024])
            KK_ps = v8(pP[2 * m + 1][0:64, 0:512])
            QK_ps = v8(pP[2 * m + 1][0:64, 512:1024])
            bX, bR, bK, bQ = "b%d" % (4 * m), "b%d" % (4 * m + 1), "b%d" % (4 * m + 2), "b%d" % (4 * m + 3)
            M = str(m)

            def mmKK(e):
                ins = None
                for h in range(8):
                    ins = e.matmul(KK_ps[:, h, :], kF[:, h, :], kF[:, h, :], start=True, stop=True)
                for h in range(8):
                    ins = e.matmul(QK_ps[:, h, :], qF[:, h, :], kF[:, h, :], start=True, stop=True)
                return ins

            def mm1(e):
                e.matmul(psA[0:64, 0:8], U, bg_[:, 8:16], start=True, stop=True)
                return e.matmul(psA[:, 8:16], onesP[:], bg_[:, 8:16], start=True, stop=True)

            A("tensor", mm1, reads=[BGK, "cst", "onesP"], writes=[bX])
            A("tensor", mmKK, reads=kq, writes=[bK, bQ])
            yield
            A("vector", lambda e: e.tensor_copy(sm[0:64, 0:8], psA[0:64, 0:8]), reads=[bX], writes=["sm_g" + M])
            A("vector", lambda e: e.tensor_copy(sm[:, 8:16], psA[:, 8:16]), reads=[bX], writes=["sm_g2" + M])
            yield
            A("scalar", lambda e: e.activation(out=sm[0:64, 16:24], in_=sm[0:64, 0:8], func=AF.Exp), reads=["sm_g" + M],
              writes=["sm_e" + M])
            A("scalar", lambda e: e.activation(out=sm[:, 24:32], in_=sm[:, 8:16], func=AF.Exp), reads=["sm_g2" + M],
              writes=["sm_e2" + M])
            gc = sm[0:64, 0:8]
            A("vector", lambda e: e.tensor_tensor(out=dGC[0:64], in0=bc_h(id64, 64), in1=bc_f(gc, 64, 64), op=ALU.mult),
              reads=["sm_g" + M, "ident", zkeys["dGC" + M]], writes=["dGC" + M])
            yield
            A("tensor", lambda e: e.matmul(R_ps, onesP[:], dGC[:], start=True, stop=True), reads=["dGC" + M, "onesP"],
              writes=[bR])
            A("vector", lambda e: e.tensor_tensor(out=sm[0:64, 32:40], in0=sm[0:64, 8:16], in1=sm[0:64, 0:8],
                                                  op=ALU.subtract), reads=["sm_g" + M, "sm_g2" + M], writes=["sm_d" + M])
            A("vector", lambda e: e.tensor_tensor(out=sm[0:64, 48:56], in0=beta, in1=sm[0:64, 16:24], op=ALU.mult),
              reads=[BGK, "sm_e" + M], writes=["sm_b" + M])
            yield
            A("scalar", lambda e: e.activation(out=sm[0:64, 40:48], in_=sm[0:64, 32:40], func=AF.Exp),
              reads=["sm_d" + M], writes=["sm_k" + M])
            A("scalar", lambda e: e.activation(out=eR[:], in_=R_ps, func=AF.Exp), reads=[bR], writes=["eR" + M])
            yield
            A("vector", lambda e: e.tensor_tensor(out=Dt[:], in0=bc_f(gc, 64, 64), in1=R_ps[0:64], op=ALU.subtract),
              reads=[bR, "sm_g" + M, "eR" + M], writes=["Dt" + M])
            A("vector", lambda e: e.scalar_tensor_tensor(out=Dt[:], in0=Dt[:], scalar=0.0, in1=bc_h(mbi, 64),
                                                         op0=ALU.min, op1=ALU.add), reads=["Dt" + M, "cst"],
              writes=["Dt" + M])
            yield
            A("scalar", lambda e: e.activation(out=dec[:], in_=Dt[:], func=AF.Exp), reads=["Dt" + M], writes=["dec" + M])
            A("vector", lambda e: e.tensor_tensor(out=qdec[:], in0=qF, in1=eR[:], op=ALU.mult),
              reads=["eR" + M, kq[0]], writes=["qdec" + M])
            yield
            A("gpsimd", lambda e: e.tensor_tensor(out=t1[:], in0=dec[:], in1=bc_h(st01, 64), op=ALU.mult),
              reads=["dec" + M, "cst"], writes=["t1" + M])
            A("gpsimd", lambda e: e.tensor_tensor(out=t1[:], in0=t1[:], in1=bc_f(beta, 64, 64), op=ALU.mult),
              reads=["t1" + M, BGK], writes=["t1" + M])
            A("vector", lambda e: e.tensor_tensor(out=attn[0:64], in0=QK_ps, in1=dec[:], op=ALU.mult),
              reads=[bQ, "dec" + M, zkeys["attn" + M]], writes=["attn" + M])
            yield
            L = Pm[0]
            A("vector", lambda e: e.tensor_tensor(out=L[0:64], in0=KK_ps, in1=t1[:], op=ALU.mult),
              reads=[bK, "t1" + M, zkeys["Pm%d_0" % m]], writes=["Pm%d_0" % m])
            yield
            LT_ps, aT_ps = KK_ps, QK_ps

            def trL(e):
                ins = None
                for h in range(8):
                    ins = e.matmul(LT_ps[:, h, :], L[:, h, :], cx.ident[:, 0:64], start=True, stop=True)
                for h in range(8):
                    ins = e.matmul(aT_ps[:, h, :], attn[:, h, :], cx.ident[:, 0:64], start=True, stop=True)
                return ins

            A("tensor", trL, reads=["Pm%d_0" % m, "attn" + M, "ident"], writes=[bK, bQ])
            yield
            A("scalar", lambda e: e.copy(PTm[0][0:64], LT_ps), reads=[bK, zkeys["PTm%d_0" % m]], writes=["PTm%d_0" % m])
            yield
            A("vector", lambda e: e.tensor_tensor(out=TTm[0][0:64], in0=bc_h(id64, 64), in1=PTm[0][0:64],
                                                  op=ALU.subtract),
              reads=["PTm%d_0" % m, "ident", zkeys["TTm%d_0" % m]], writes=["TTm%d_0" % m])
            A("scalar", lambda e: e.copy(aT16[0:64], aT_ps), reads=[bQ, zkeys["aT16" + M]], writes=["aT16" + M])
            yield
            P2_ps, PT2_ps = KK_ps, QK_ps
            ci = 0
            for lvl in range(5):
                P, PT, Tc = Pm[ci], PTm[ci], TTm[ci]
                Pn, PTn, Tn = Pm[1 - ci], PTm[1 - ci], TTm[1 - ci]
                kP, kPT, kT_ = "Pm%d_%d" % (m, ci), "PTm%d_%d" % (m, ci), "TTm%d_%d" % (m, ci)
                kPn, kPTn, kTn = "Pm%d_%d" % (m, 1 - ci), "PTm%d_%d" % (m, 1 - ci), "TTm%d_%d" % (m, 1 - ci)
                last = lvl == 4

                def mmsq(e, P=P, PT=PT, last=last):
                    ins = None
                    for h in range(8):
                        ins = e.matmul(P2_ps[:, h, :], PT[:, h, :], P[:, h, :], start=True, stop=True)
                    if not last:
                        for h in range(8):
                            ins = e.matmul(PT2_ps[:, h, :], P[:, h, :], PT[:, h, :], start=True, stop=True)
                    return ins

                A("tensor", mmsq, reads=[kP, kPT], writes=[bK, bQ])
                yield
                A("scalar", lambda e, Pn=Pn: e.copy(Pn[0:64], P2_ps), reads=[bK, zkeys[kPn]], writes=[kPn])
                if not last:
                    A("vector", lambda e, PTn=PTn: e.tensor_copy(PTn[0:64], PT2_ps), reads=[bQ, zkeys[kPTn]],
                      writes=[kPTn])
                yield

                def mmT(e, Pn=Pn, Tc=Tc):
                    ins = None
                    for h in range(8):
                        ins = e.matmul(Tup_ps[:, h, :], Pn[:, h, :], Tc[:, h, :], start=True, stop=True)
                    return ins

                A("tensor", mmT, reads=[kPn, kT_], writes=[bX])
                yield
                A("vector", lambda e, Tn=Tn, Tc=Tc: e.tensor_tensor(out=Tn[0:64], in0=Tc[0:64], in1=Tup_ps, op=ALU.add),
                  reads=[bX, kT_, zkeys[kTn]], writes=[kTn])
                yield
                ci = 1 - ci
            Tfin = TTm[ci]
            kTf = "TTm%d_%d" % (m, ci)
            A("scalar", lambda e: e.copy(TT16[0:64], Tfin[0:64]), reads=[kTf, zkeys["TT16" + M]], writes=["TT16" + M])
            kT_b, vT_b = kTok[m], vTok[m]
            A("gpsimd", lambda e: e.tensor_tensor(out=VBb[0:64], in0=vT_b[:], in1=bc_f(beta, 64, 128), op=ALU.mult),
              reads=["vTok%d" % m, BGK, zkeys["VBb" + M]], writes=["VBb" + M])
            A("gpsimd", lambda e: e.tensor_tensor(out=RKb[0:64], in0=kT_b[:], in1=bc_f(sm[0:64, 48:56], 64, 128),
                                                  op=ALU.mult),
              reads=["kTok%d" % m, "sm_b" + M, zkeys["RKb" + M]], writes=["RKb" + M])
            A("gpsimd", lambda e: e.tensor_tensor(out=KDb[0:64], in0=kT_b[:], in1=bc_f(sm[0:64, 40:48], 64, 128),
                                                  op=ALU.mult),
              reads=["kTok%d" % m, "sm_k" + M, zkeys["KDb" + M]], writes=["KDb" + M])
            A("gpsimd", lambda e: e.tensor_tensor(out=gg[:], in0=gTk[m][:], in1=bc_h(ogb[:], 64), op=ALU.mult),
              reads=["gTk%d" % m, "ogb"], writes=["gg" + M])
            yield

        def chain(ch, m):
            sm = smM[m]
            M = str(m)
            TT16, aT16, VBb, RKb, KDb, gg, qdec = TT16M[m], aT16M[m], VBbM[m], RKbM[m], KDbM[m], ggM[m], qdecM[m]
            A("gpsimd", lambda e: e.tensor_tensor(out=Stmp[:], in0=Sst[:], in1=bc_f(sm[:, 24:32], 128, 128), op=ALU.mult),
              reads=["Sst", "sm_e2" + M], writes=["Stmp"])

            def mmkcd(e):
                ins = None
                for h in range(8):
                    ins = e.matmul(kcd_ps[:, h, :], RKb[:, h, :], TT16[:, h, :], start=True, stop=True)
                return ins

            A("tensor", mmkcd, reads=["RKb" + M, "TT16" + M], writes=["b7"])
            A("scalar", lambda e: e.activation(out=nk16[:], in_=kcd_ps, func=AF.Copy, scale=-1.0), reads=["b7"],
              writes=["nk16"])

            def mmvn(e):
                ins = None
                for h in range(8):
                    e.matmul(vn_ps[:, h, :], TT16[:, h, :], VBb[:, h, :], start=True, stop=False)
                    ins = e.matmul(vn_ps[:, h, :], nk16[:, h, :], Sb[:, h, :], start=False, stop=True)
                return ins

            A("tensor", mmvn, reads=["TT16" + M, "VBb" + M, "nk16", "Sb"], writes=["b0", "b1"])
            A("vector", lambda e: e.tensor_copy(vnb[0:64], vn_ps), reads=["b0", "b1", zkeys["vnb"]], writes=["vnb"])

            def mmo(e):
                ins = None
                for h in range(8):
                    e.matmul(o_ps[:, h, :], qdec[:, h, :], Sb[:, h, :], start=True, stop=False)
                    ins = e.matmul(o_ps[:, h, :], aT16[:, h, :], vnb[:, h, :], start=False, stop=True)
                return ins

            A("tensor", mmo, reads=["qdec" + M, "Sb", "aT16" + M, "vnb"], writes=["b2", "b3"])

            def mmS(e):
                ins = None
                for h in range(8):
                    ins = e.matmul(Sn_ps[:, h, :], KDb[:, h, :], vnb[:, h, :], start=True, stop=True)
                return ins

            A("tensor", mmS, reads=["KDb" + M, "vnb"], writes=["b4", "b5"])
            A("vector", lambda e: e.tensor_tensor(out=Sst[:], in0=Stmp[:], in1=Sn_ps, op=ALU.add),
              reads=["Stmp", "b4", "b5"], writes=["Sst"])
            A("scalar", lambda e: e.copy(Sb[:], Sst[:]), reads=["Sst"], writes=["Sb"])
            A("scalar", lambda e: e.activation(out=osq[:], in_=o_ps, func=AF.Square), reads=["b2", "b3"], writes=["osq"])
            A("vector", lambda e: e.tensor_reduce(out=smc[:], in_=osq[:], axis=AX.X, op=ALU.add),
              reads=["osq"], writes=["sm_o"])
            A("scalar", lambda e: e.activation(out=smc[:], in_=smc[:], func=AF.Sqrt, bias=cx.epsb[0:64, :],
                                               scale=1.0 / 128), reads=["sm_o", "epsb"], writes=["sm_o"])
            A("vector", lambda e: e.reciprocal(smc[:], smc[:]), reads=["sm_o"], writes=["sm_o"])
            A("vector", lambda e: e.tensor_tensor(out=o1[0:64], in0=o_ps, in1=bc_f(smc[:], 64, 128), op=ALU.mult),
              reads=["b2", "b3", "sm_o", zkeys["o1"]], writes=["o1"])
            A("vector", lambda e: e.tensor_tensor(out=o1[0:64], in0=o1[0:64], in1=gg[:], op=ALU.mult),
              reads=["o1", "gg" + M], writes=["o1"])

            def trO(e):
                ins = None
                for c in range(8):
                    ins = e.matmul(oT_ps[:, c, :], o1[:, c, :], cx.ident[:, 0:64], start=True, stop=True)
                return ins

            A("tensor", trO, reads=["o1", "ident"], writes=["b6"])
            A("scalar", lambda e: e.copy(oT[:, :, m * 64:(m + 1) * 64], oT_ps), reads=["b6"], writes=["oT%d" % m])

        def out_proj(pr):
            xb = (pr // 2) % 2
            for nh in range(2):
                def mmout(e, nh=nh):
                    ins = None
                    for c in range(8):
                        ins = e.matmul(out_ps[nh], oT[:, c, :], wout[:, c, nh * 512:(nh + 1) * 512],
                                       start=(c == 0), stop=(c == 7))
                    return ins

                A("tensor", mmout, reads=["oT0", "oT1", "wout"], writes=["b%d" % nh])
                A("vector", lambda e, nh=nh: e.tensor_tensor(
                    out=ho[:, nh * 512:(nh + 1) * 512], in0=out_ps[nh], in1=xres[xb][:, nh * 512:(nh + 1) * 512],
                    op=ALU.add), reads=["b%d" % nh, "xres%d" % xb], writes=["ho_%d" % nh])
            sc.dma("sync", h_out[pr * 64:pr * 64 + 128, :], ho[:], reads=["ho_0", "ho_1"], chan="ho")

        assert NCHK % 2 == 0
        load_super(0)
        for c0 in range(2):
            load_chunk(c0)
        load_xres(0)
        for pr in range(0, NCHK, 2):
            chs = [pr, pr + 1]
            st = pr // 4
            if pr % 4 == 0 and (st + 1) * 4 < NCHK:
                load_super(st + 1)
            gens = [pre(c, c % 2) for c in chs]
            alive = list(gens)
            while alive:
                nxt = []
                for g in alive:
                    try:
                        next(g)
                        nxt.append(g)
                    except StopIteration:
                        pass
                alive = nxt
            for c in chs:
                if c + 2 < NCHK:
                    load_chunk(c + 2)
            if pr + 2 < NCHK:
                load_xres(pr + 2)
            for c in chs:
                chain(c, c % 2)
            out_proj(pr)
        sc.flush()


_W_SHAPES = {
    "a_norm": [1, D], "a_w_in": [1, D, A_IN], "a_conv_w": [1, 4, 3072], "a_A_log": [1, 8], "a_dt_bias": [1, 8],
    "a_onorm": [1, 128], "a_w_out": [1, D, D], "kv_norm": [D], "kv_w": [D, 512], "kv_b": [512],
    "b_norm": [1, D], "b_w_q": [1, D, D], "b_b_q": [1, D], "b_sinks": [1, 16], "b_w_o": [1, D, D], "b_b_o": [1, D],
    "f_norm": [2, D], "f_w_up": [2, D, 2 * FF], "f_conv_w": [2, 3, 2 * FF], "f_conv_b": [2, 2 * FF],
    "f_w_down": [2, FF, D], "final_norm": [D],
}


def build_program():
    nc = bass.Bass("TRN2", target_bir_lowering=False)

    def din(name, shape):
        return nc.dram_tensor(name, shape, F32, kind="ExternalInput").ap()

    def dint(name, shape):
        return nc.dram_tensor(name, shape, F32, kind="Internal").ap()

    x = din("x", [S, D])
    w = {k: din(k, shp) for k, shp in _W_SHAPES.items()}
    ident = din("c_ident", [128, 128])
    bm = din("c_bm", [128, 16, 256])
    cst = din("c_delta", [64, 3, 64])
    out = nc.dram_tensor("out", [S, D], F32, kind="ExternalOutput").ap()
    QF = dint("s_QF", [8, 128, S])
    KF = dint("s_KF", [8, 128, S])
    KT = dint("s_KT", [S, D])
    VT = dint("s_VT", [S, D])
    GT = dint("s_GT", [S, D])
    BG = dint("s_BG", [S, 16])
    h1 = dint("s_h1", [S, D])
    h2 = dint("s_h2", [S, D])
    h3 = dint("s_h3", [S, D])
    with ExitStack() as es:
        block = es.enter_context(nc.Block())
        sc = Sched(nc, block, es)
        cx = Ctx()
        load_consts(sc, es, nc, cx, ident)
        delta_proj_phase(sc, nc, cx, x, w["a_norm"][0], w["a_w_in"][0], w["a_conv_w"][0], w["a_A_log"][0],
                         w["a_dt_bias"][0], QF, KF, KT, VT, GT, BG)
        delta_phase(sc, nc, cx, x, h1, QF, KF, KT, VT, GT, BG, w["a_onorm"][0], w["a_w_out"][0], cst)
        ffn_phase(sc, nc, cx, h1, h2, w["f_norm"][0], w["f_w_up"][0], w["f_conv_w"][0], w["f_conv_b"][0],
                  w["f_w_down"][0])
        attn_phase(sc, nc, cx, h2, h3, w["kv_norm"], w["kv_w"], w["kv_b"], w["b_norm"][0], w["b_w_q"][0],
                   w["b_b_q"][0], w["b_sinks"][0], w["b_w_o"][0], w["b_b_o"][0], bm)
        ffn_phase(sc, nc, cx, h3, out, w["f_norm"][1], w["f_w_up"][1], w["f_conv_w"][1], w["f_conv_b"][1],
                  w["f_w_down"][1], fin_ap=w["final_norm"])
    return nc


def kernel(**inputs):
    x = np.ascontiguousarray(np.asarray(inputs["x"], dtype=np.float32))
    consts = {
        "c_ident": np.eye(128, dtype=np.float32),
        "c_bm": make_biasmask(),
        "c_delta": make_delta_consts(),
    }
    shared = {k: np.ascontiguousarray(np.asarray(inputs[k], dtype=np.float32)) for k in _W_SHAPES}
    shared.update(consts)
    nc = build_program()
    in_maps = []
    for b in range(NB):
        m = dict(shared)
        m["x"] = np.ascontiguousarray(x[b])
        in_maps.append(m)
    res = run_bass_kernel_spmd(nc, in_maps, core_ids=list(range(NB)))
    return np.stack([np.asarray(r["out"], dtype=np.float32) for r in res.results], axis=0)
```
